# Optimizing a Trainium2 kernel written in Bass

```python
import math
import jax, jax.numpy as jnp
from jax import lax
import numpy as np

D_MODEL = 1024
BATCH = 4
SEQ = 4096
DEPTH = 1
DEC_BATCH = 8
DEC_SEQ = 32
PAST_LEN = 4096

CHUNK = 64
N_META = 16
N_DIFF_HEADS = 8
DIFF_HEAD_DIM = 64
DIFF_V_DIM = 2 * DIFF_HEAD_DIM
DIFF_WIDTH = N_DIFF_HEADS * DIFF_V_DIM
N_SB_HEADS = 8
SB_HEAD_DIM = 64
SB_WIDTH = N_SB_HEADS * SB_HEAD_DIM
N_BRANCH = 2
D_FF = 4 * D_MODEL
ROT_DIM = DIFF_HEAD_DIM // 4
ROPE_THETA = 500000.0
Q_BLOCK = 128
EPS = 1e-6
NEG_INF = -1e30
IN_COLS = 3 * DIFF_WIDTH + 3 * SB_WIDTH + N_BRANCH * D_MODEL
IN_SPLITS = [DIFF_WIDTH, 2 * DIFF_WIDTH, 3 * DIFF_WIDTH,
             3 * DIFF_WIDTH + SB_WIDTH, 3 * DIFF_WIDTH + 2 * SB_WIDTH, 3 * DIFF_WIDTH + 3 * SB_WIDTH]

kernel_name = "hybrid_diff_stickbreak_stream_step"


def _rmsnorm(x, g):
    x32 = x.astype(jnp.float32)
    y = x32 * lax.rsqrt(jnp.mean(x32 * x32, axis=-1, keepdims=True) + EPS)
    return (y * g.astype(jnp.float32)).astype(x.dtype)


def _rope(x, pos):
    half = ROT_DIM // 2
    inv = ROPE_THETA ** (-jnp.arange(0, ROT_DIM, 2, dtype=jnp.float32) / ROT_DIM)
    ang = pos.astype(jnp.float32)[:, None] * inv[None, :]
    shp = (pos.shape[0],) + (1,) * (x.ndim - 3) + (half,)
    cos = jnp.cos(ang).reshape(shp)
    sin = jnp.sin(ang).reshape(shp)
    xr = x[..., :ROT_DIM].astype(jnp.float32)
    x1, x2 = xr[..., :half], xr[..., half:]
    rot = jnp.concatenate([x1 * cos - x2 * sin, x2 * cos + x1 * sin], axis=-1)
    return jnp.concatenate([rot.astype(x.dtype), x[..., ROT_DIM:]], axis=-1)


def _query_blocks(tq):
    blk = min(Q_BLOCK, tq)
    return blk, -(-tq // blk)


def _split_blocks(a, blk, nb, axis):
    pad = nb * blk - a.shape[axis]
    widths = [(0, 0)] * a.ndim
    widths[axis] = (0, pad)
    a = jnp.pad(a, widths, mode="edge")
    a = a.reshape(a.shape[:axis] + (nb, blk) + a.shape[axis + 1:])
    return jnp.moveaxis(a, axis, 0)


def _merge_blocks(o, tq):
    o = jnp.moveaxis(o, 0, 1)
    o = o.reshape((o.shape[0], -1) + o.shape[3:])
    return o[:, :tq]


def _diff_attention(q, k, v, q_chunk, k_chunk, lam):
    tq = q.shape[1]
    blk, nb = _query_blocks(tq)
    scale = DIFF_HEAD_DIM ** -0.5

    def block(args):
        qb, cb = args
        s = jnp.einsum("bqhcd,bkhcd->bchqk", qb, k).astype(jnp.float32) * scale
        mask = k_chunk[None, :] <= cb[:, None]
        p = jax.nn.softmax(jnp.where(mask, s, NEG_INF), axis=-1)
        a = p[:, 0] - lam * p[:, 1]
        return jnp.einsum("bhqk,bkhe->bqhe", a.astype(v.dtype), v)

    o = lax.map(block, (_split_blocks(q, blk, nb, 1), _split_blocks(q_chunk, blk, nb, 0)))
    return _merge_blocks(o, tq)


def _stick_breaking(q, k, v, q_idx, k_idx):
    tq = q.shape[1]
    blk, nb = _query_blocks(tq)
    scale = SB_HEAD_DIM ** -0.5

    def block(args):
        qb, ib = args
        z = jnp.einsum("bqhd,bkhd->bhqk", qb, k).astype(jnp.float32) * scale
        mask = k_idx[None, :] < ib[:, None]
        log_1m = jnp.where(mask, jax.nn.log_sigmoid(-z), 0.0)
        suffix = lax.cumsum(log_1m, axis=3, reverse=True) - log_1m
        w = jnp.where(mask, jnp.exp(jax.nn.log_sigmoid(z) + suffix), 0.0)
        return jnp.einsum("bhqk,bkhd->bqhd", w.astype(v.dtype), v)

    o = lax.map(block, (_split_blocks(q, blk, nb, 1), _split_blocks(q_idx, blk, nb, 0)))
    return _merge_blocks(o, tq)


def _layer(x, q_pos, q_chunk, k_pos, k_chunk, past_dk, past_dv, past_sk, past_sv, lam_init,
           g_mix, w_in, q_norm_g, k_norm_g, lam_q1, lam_k1, lam_q2, lam_k2, sub_g,
           w_diff_out, w_sb_out, w_out, g_ffn, w_ff1, w_ff2):
    b, t, _ = x.shape
    h = _rmsnorm(x, g_mix)
    dq, dk, dv, sq, sk, sv, gate = jnp.split(h @ w_in, IN_SPLITS, axis=-1)
    dq = _rope(_rmsnorm(dq.reshape(b, t, N_DIFF_HEADS, 2, DIFF_HEAD_DIM), q_norm_g), q_pos)
    dk = _rope(_rmsnorm(dk.reshape(b, t, N_DIFF_HEADS, 2, DIFF_HEAD_DIM), k_norm_g), q_pos)
    dk = dk.reshape(b, t, N_DIFF_HEADS, 2 * DIFF_HEAD_DIM)
    dv = dv.reshape(b, t, N_DIFF_HEADS, DIFF_V_DIM)
    sq = sq.reshape(b, t, N_SB_HEADS, SB_HEAD_DIM)
    sk = sk.reshape(b, t, N_SB_HEADS, SB_HEAD_DIM)
    sv = sv.reshape(b, t, N_SB_HEADS, SB_HEAD_DIM)
    if past_dk is None:
        dk_all, dv_all, sk_all, sv_all = dk, dv, sk, sv
    else:
        dk_all = jnp.concatenate([past_dk, dk], axis=1)
        dv_all = jnp.concatenate([past_dv, dv], axis=1)
        sk_all = jnp.concatenate([past_sk, sk], axis=1)
        sv_all = jnp.concatenate([past_sv, sv], axis=1)
    f32 = jnp.float32
    lam = (jnp.exp(jnp.sum(lam_q1.astype(f32) * lam_k1.astype(f32)))
           - jnp.exp(jnp.sum(lam_q2.astype(f32) * lam_k2.astype(f32))) + lam_init)
    o_d = _diff_attention(dq, dk_all.reshape(b, -1, N_DIFF_HEADS, 2, DIFF_HEAD_DIM), dv_all,
                          q_chunk, k_chunk, lam)
    o_d = _rmsnorm(o_d, sub_g) * (1.0 - lam_init)
    o_s = _stick_breaking(sq, sk_all, sv_all, q_pos, k_pos)
    g_d, g_s = jnp.split(jax.nn.sigmoid(gate), N_BRANCH, axis=-1)
    merged = (g_d * (o_d.reshape(b, t, DIFF_WIDTH) @ w_diff_out)
              + g_s * (o_s.reshape(b, t, SB_WIDTH) @ w_sb_out))
    x = x + merged @ w_out
    x = x + jnp.square(jax.nn.relu(_rmsnorm(x, g_ffn) @ w_ff1)) @ w_ff2
    return x, dk, dv, sk, sv


def setup_inputs(seed: int = 0) -> dict:
    key = jax.random.key(seed)
    ks = jax.random.split(key, 24)

    def n(k, shape, s):
        return jax.random.normal(k, shape, jnp.float32) * s

    return {
        "x_prompt": n(ks[0], (BATCH, SEQ, D_MODEL), 1.0),
        "x_sample": n(ks[1], (DEC_BATCH, DEC_SEQ, D_MODEL), 1.0),
        "cache_diff_k": n(ks[2], (DEPTH, DEC_BATCH, PAST_LEN, N_DIFF_HEADS, 2 * DIFF_HEAD_DIM), 1.0),
        "cache_diff_v": n(ks[3], (DEPTH, DEC_BATCH, PAST_LEN, N_DIFF_HEADS, DIFF_V_DIM), 1.0),
        "cache_sb_k": n(ks[4], (DEPTH, DEC_BATCH, PAST_LEN, N_SB_HEADS, SB_HEAD_DIM), 1.0),
        "cache_sb_v": n(ks[5], (DEPTH, DEC_BATCH, PAST_LEN, N_SB_HEADS, SB_HEAD_DIM), 1.0),
        "meta_tokens": n(ks[6], (N_META, D_MODEL), 1.0),
        "g_mix": 1.0 + n(ks[7], (DEPTH, D_MODEL), 0.02),
        "w_in": n(ks[8], (DEPTH, D_MODEL, IN_COLS), D_MODEL ** -0.5),
        "q_norm_g": 1.0 + n(ks[9], (DEPTH, DIFF_HEAD_DIM), 0.02),
        "k_norm_g": 1.0 + n(ks[10], (DEPTH, DIFF_HEAD_DIM), 0.02),
        "lam_q1": n(ks[11], (DEPTH, DIFF_HEAD_DIM), 0.1),
        "lam_k1": n(ks[12], (DEPTH, DIFF_HEAD_DIM), 0.1),
        "lam_q2": n(ks[13], (DEPTH, DIFF_HEAD_DIM), 0.1),
        "lam_k2": n(ks[14], (DEPTH, DIFF_HEAD_DIM), 0.1),
        "sub_g": 1.0 + n(ks[15], (DEPTH, DIFF_V_DIM), 0.02),
        "w_diff_out": n(ks[16], (DEPTH, DIFF_WIDTH, D_MODEL), DIFF_WIDTH ** -0.5),
        "w_sb_out": n(ks[17], (DEPTH, SB_WIDTH, D_MODEL), SB_WIDTH ** -0.5),
        "w_out": n(ks[18], (DEPTH, D_MODEL, D_MODEL), D_MODEL ** -0.5),
        "g_ffn": 1.0 + n(ks[19], (DEPTH, D_MODEL), 0.02),
        "w_ff1": n(ks[20], (DEPTH, D_MODEL, D_FF), D_MODEL ** -0.5),
        "w_ff2": n(ks[21], (DEPTH, D_FF, D_MODEL), D_FF ** -0.5),
    }


def reference(x_prompt, x_sample, cache_diff_k, cache_diff_v, cache_sb_k, cache_sb_v, meta_tokens,
              g_mix, w_in, q_norm_g, k_norm_g, lam_q1, lam_k1, lam_q2, lam_k2, sub_g,
              w_diff_out, w_sb_out, w_out, g_ffn, w_ff1, w_ff2):
    b, s, _ = x_prompt.shape
    t = s + N_META
    meta = jnp.broadcast_to(meta_tokens[None].astype(x_prompt.dtype), (b, N_META, D_MODEL))
    xp = jnp.concatenate([meta, x_prompt], axis=1)
    p_pos = jnp.arange(t, dtype=jnp.int32)
    p_chunk = jnp.where(p_pos < N_META, 0, (p_pos - N_META) // CHUNK + 1)
    past_len = cache_diff_k.shape[2]
    ds = x_sample.shape[1]
    k_pos_s = jnp.arange(past_len + ds, dtype=jnp.int32)
    k_chunk_s = k_pos_s // CHUNK + 1
    q_pos_s = k_pos_s[past_len:]
    q_chunk_s = k_chunk_s[past_len:]
    xs = x_sample
    pdk, pdv, psk, psv, sdk, sdv, ssk, ssv = [], [], [], [], [], [], [], []
    for l in range(DEPTH):
        lam_init = 0.8 - 0.6 * math.exp(-0.3 * l)
        xp, a1, a2, a3, a4 = _layer(
            xp, p_pos, p_chunk, p_pos, p_chunk, None, None, None, None, lam_init,
            g_mix[l], w_in[l], q_norm_g[l], k_norm_g[l], lam_q1[l], lam_k1[l], lam_q2[l], lam_k2[l],
            sub_g[l], w_diff_out[l], w_sb_out[l], w_out[l], g_ffn[l], w_ff1[l], w_ff2[l])
        xs, c1, c2, c3, c4 = _layer(
            xs, q_pos_s, q_chunk_s, k_pos_s, k_chunk_s,
            cache_diff_k[l], cache_diff_v[l], cache_sb_k[l], cache_sb_v[l], lam_init,
            g_mix[l], w_in[l], q_norm_g[l], k_norm_g[l], lam_q1[l], lam_k1[l], lam_q2[l], lam_k2[l],
            sub_g[l], w_diff_out[l], w_sb_out[l], w_out[l], g_ffn[l], w_ff1[l], w_ff2[l])
        pdk.append(a1); pdv.append(a2); psk.append(a3); psv.append(a4)
        sdk.append(c1); sdv.append(c2); ssk.append(c3); ssv.append(c4)
    y_prompt = xp[:, N_META:]
    y_sample = xs
    return (y_prompt, y_sample,
            jnp.stack(pdk), jnp.stack(pdv), jnp.stack(psk), jnp.stack(psv),
            jnp.stack(sdk), jnp.stack(sdv), jnp.stack(ssk), jnp.stack(ssv))
```

```python
from contextlib import ExitStack

import numpy as np
import concourse.bass as bass
import concourse.mybir as mybir
from concourse.bass_utils import run_bass_kernel_spmd

F32 = mybir.dt.float32
BF16 = mybir.dt.bfloat16
ALU = mybir.AluOpType
AF = mybir.ActivationFunctionType
AX = mybir.AxisListType


class Res:
    __slots__ = ("name", "w", "r", "excl", "wl")

    def __init__(self, name, excl=False):
        self.name = name
        self.wl = []
        self.excl = excl
        self.w = None
        self.r = []


class Sched:
    ENG = ("pe", "act", "dve", "pool", "sp")
    NDMA = {"sp": 12, "pool": 8, "act": 6}

    def __init__(self, nc, es):
        self.nc = nc
        self.es = es
        self.ops = {e: [] for e in self.ENG}
        self.cnt = {e: 0 for e in self.ENG}
        self.waited = {e: {} for e in self.ENG}
        self.sem = {}
        for e in self.ENG:
            self.sem[e] = es.enter_context(nc.semaphore("c_" + e))
        self.dsem = {}
        self.dcnt = {}
        self.drr = {}
        for q, n in self.NDMA.items():
            self.dsem[q] = []
            for i in range(n):
                nm = f"d_{q}{i}"
                self.sem[nm] = es.enter_context(nc.semaphore(nm))
                self.dsem[q].append(nm)
                self.dcnt[nm] = 0
            self.drr[q] = 0
        self.nwaits = 0
        self.pending = {}

    def sbuf(self, name, shape, dtype):
        return self.es.enter_context(self.nc.sbuf_tensor(name, shape, dtype))

    def psum(self, name, shape, dtype):
        return self.es.enter_context(self.nc.psum_tensor(name, shape, dtype))

    def _collect(self, eng, reads, writes, is_dma=False):
        deps = {}

        def add(tk, raw):
            if tk is None:
                return
            s, v = tk
            if s == eng and eng == "pe":
                return
            if deps.get(s, 0) < v:
                deps[s] = v

        for r in reads:
            add(r.w, True)
            for t in r.wl:
                add(t, True)
            if r.excl:
                for t in r.r:
                    add(t, False)
        for w in writes:
            if not (is_dma and w.w is not None and w.w[0].startswith("d_")):
                add(w.w, False)
                for t in w.wl:
                    add(t, False)
            for t in w.r:
                add(t, False)
        waits = []
        wd = self.waited[eng]
        for s, v in deps.items():
            if wd.get(s, 0) < v:
                wd[s] = v
                waits.append((s, v))
        self.nwaits += len(waits)
        return waits

    @staticmethod
    def _update(tk, reads, writes, is_dma=False):
        for r in reads:
            r.r.append(tk)
        for w in writes:
            if is_dma and w.w is not None and w.w[0].startswith("d_"):
                w.wl = w.wl + [w.w]
            else:
                w.wl = []
            w.w = tk
            w.r = []

    def op(self, eng, fn, reads=(), writes=()):
        waits = self.pending.pop(eng, []) + self._collect(eng, reads, writes)
        self.cnt[eng] += 1
        tk = (eng, self.cnt[eng])
        self.ops[eng].append((waits, fn, (eng, 1)))
        self._update(tk, reads, writes)
        return tk

    def dma(self, q, fn, reads=(), writes=()):
        names = self.dsem[q]
        nm = names[self.drr[q] % len(names)]
        self.drr[q] += 1
        waits = self.pending.pop(q, []) + self._collect(q, reads, writes, is_dma=True)
        prev = 16 * self.dcnt[nm]
        if prev and self.waited[q].get(nm, 0) < prev:
            self.waited[q][nm] = prev
            waits.append((nm, prev))
        self.dcnt[nm] += 1
        tk = (nm, 16 * self.dcnt[nm])
        self.ops[q].append((waits, fn, (nm, 16)))
        self._update(tk, reads, writes, is_dma=True)
        return tk

    def barrier(self):
        allt = [(e, c) for e, c in self.cnt.items() if c] + [(nm, 16 * c) for nm, c in self.dcnt.items() if c]
        for e in self.ENG:
            waits = []
            wd = self.waited[e]
            for sname, v in allt:
                if sname == e and e == "pe":
                    continue
                if wd.get(sname, 0) < v:
                    wd[sname] = v
                    waits.append((sname, v))
            self.pending[e] = self.pending.get(e, []) + waits

    def finish(self):
        final = [(nm, 16 * c) for nm, c in self.dcnt.items() if c]
        sem = self.sem

        def replay(name, e, tail=()):
            for waits, fn, inc in self.ops[name]:
                for s, v in waits:
                    e.wait_ge(sem[s], v)
                ins = fn(e)
                ins.then_inc(sem[inc[0]], inc[1])
            for s, v in tail:
                e.wait_ge(sem[s], v)

        with self.nc.Block() as block:
            @block.tensor
            def _(e):
                replay("pe", e)

            @block.scalar
            def _(e):
                replay("act", e)

            @block.vector
            def _(e):
                replay("dve", e)

            @block.gpsimd
            def _(e):
                replay("pool", e)

            @block.sync
            def _(e):
                replay("sp", e, tail=final)


DM = 1024
NH = 8
N_META = 16
DS = 32
EPS = 1e-6
LAM_INIT = 0.2
IN_COLS = 6656
GQ = 4


class Arena:
    def __init__(self, S, nbytes):
        self.n = nbytes
        self.t = S.sbuf("arena", [128, nbytes // 2], BF16)
        self.off = 0

    def seek(self, off):
        self.off = off

    def alloc(self, shape, dtype):
        esz = 4 if dtype == F32 else 2
        n = 1
        for d in shape[1:]:
            n *= d
        nb = (n * esz + 31) // 32 * 32
        assert self.off + nb <= self.n, f"arena overflow {self.off}+{nb}>{self.n}"
        v = self.t[:, self.off // 2:(self.off + nb) // 2]
        self.off += nb
        if dtype == F32:
            v = v.bitcast(F32)
        v = v[:, 0:n]
        if len(shape) == 3:
            v = v.rearrange("p (a b) -> p a b", b=shape[2])
        elif len(shape) == 4:
            v = v.rearrange("p (a b c) -> p a b c", b=shape[2], c=shape[3])
        return v


def build(NB, PB, stop_after=99):
    assert NB % 2 == 0 and PB == NB
    NOWN = NB // 2
    gq = min(GQ, NOWN)
    assert NOWN % gq == 0
    NG = NOWN // gq
    NT = NB * 128 + N_META
    NSLOT = NB + 2
    HALF = NOWN * 128
    SKW = HALF + 128
    KW = NB * 128 + 32

    nc = bass.Bass("TRN2", target_bir_lowering=False)

    def din(name, shape):
        return nc.dram_tensor(name, shape, F32, kind="ExternalInput").ap()

    def dout(name, shape):
        return nc.dram_tensor(name, shape, F32, kind="ExternalOutput").ap()

    x_slots = din("x_slots", [NT, DM])
    xs = din("xs", [DS, DM])
    c_dk = din("c_dk", [PB * 128, NH, 128])
    c_dv = din("c_dv", [PB * 128, NH, 128])
    c_sk = din("c_sk", [PB * 128, NH, 64])
    c_sv = din("c_sv", [PB * 128, NH, 64])
    w_in = din("w_in", [DM, IN_COLS])
    w_do = din("w_do", [DM, DM])
    w_so = din("w_so", [512, DM])
    w_o = din("w_o", [DM, DM])
    w_f1 = din("w_f1", [DM, 4 * DM])
    w_f2 = din("w_f2", [4 * DM, DM])
    g_mix = din("g_mix", [DM])
    g_ffn = din("g_ffn", [DM])
    qng = din("qng", [64])
    kng = din("kng", [64])
    lamv = din("lamv", [4, 64])
    subg = din("subg", [128])
    cs_t = din("cs_t", [128, NSLOT, 8])
    sn_t = din("sn_t", [128, NSLOT, 8])
    cmat = din("cmat", [8, 128, 128])

    y_own = dout("y_own", [NOWN * 128, DM])
    y_s = dout("y_s", [DS, DM])
    kd_o = dout("kd_o", [NOWN * 128, NH, 128])
    vd_o = dout("vd_o", [NOWN * 128, NH, 128])
    sk_o = dout("sk_o", [NOWN * 128, NH, 64])
    sv_o = dout("sv_o", [NOWN * 128, NH, 64])
    kd_m = dout("kd_m", [N_META, NH, 128])
    vd_m = dout("vd_m", [N_META, NH, 128])
    sk_m = dout("sk_m", [N_META, NH, 64])
    sv_m = dout("sv_m", [N_META, NH, 64])
    kd_s = dout("kd_s", [DS, NH, 128])
    vd_s = dout("vd_s", [DS, NH, 128])
    sk_s = dout("sk_s", [DS, NH, 64])
    sv_s = dout("sv_s", [DS, NH, 64])

    es = ExitStack()
    with es:
        S = Sched(nc, es)
        AR = Arena(S, 204800)
        PS = S.psum("PS", [128, 4096], F32)
        RB = [Res(f"bank{i}", excl=True) for i in range(8)]

        def bank(i):
            return PS[:, i * 512:(i + 1) * 512]

        def bankb(i):
            return PS[:, i * 512:(i + 1) * 512].bitcast(BF16)

        def mm(out, lhsT, rhs, start, stop, reads, writes, skip=False):
            S.op("pe", lambda e, o=out, l=lhsT, r=rhs, a=start, b=stop, k=skip:
                 e.matmul(o, lhsT=l, rhs=r, start=a, stop=b, skip_group_check=k), reads, writes)

        def tr(out, in_, reads, writes):
            n = in_.shape[0]
            S.op("pe", lambda e, o=out, i=in_, n=n: e.transpose(out=o, in_=i, identity=ident[0:n, 0:n]),
                 list(reads) + [R_const], writes)

        def act(out, in_, func, reads, writes, scale=1.0, bias=0.0, accum=None):
            S.op("act", lambda e, o=out, i=in_, f=func, s=scale, b=bias, a=accum:
                 e.activation(out=o, in_=i, func=f, scale=s, bias=b, accum_out=a), reads, writes)

        def tt(eng, out, in0, in1, op, reads, writes):
            S.op(eng, lambda e, o=out, a=in0, b=in1, p=op: e.tensor_tensor(out=o, in0=a, in1=b, op=p), reads, writes)

        def ts(eng, out, in0, s1, s2, op0, op1, reads, writes):
            if s2 is None:
                S.op(eng, lambda e, o=out, a=in0, x=s1, p=op0: e.tensor_scalar(out=o, in0=a, scalar1=x, scalar2=None, op0=p), reads, writes)
            else:
                S.op(eng, lambda e, o=out, a=in0, x=s1, y=s2, p=op0, q=op1:
                     e.tensor_scalar(out=o, in0=a, scalar1=x, scalar2=y, op0=p, op1=q), reads, writes)

        def stt(eng, out, in0, scalar, in1, op0, op1, reads, writes):
            S.op(eng, lambda e, o=out, a=in0, s=scalar, b=in1, p=op0, q=op1:
                 e.scalar_tensor_tensor(out=o, in0=a, scalar=s, in1=b, op0=p, op1=q), reads, writes)

        def cp(eng, out, in_, reads, writes):
            if eng == "act":
                act(out, in_, AF.Copy, reads, writes)
            else:
                S.op(eng, lambda e, o=out, i=in_: e.tensor_copy(out=o, in_=i), reads, writes)

        def dma(q, out, in_, reads, writes):
            S.dma(q, lambda e, o=out, i=in_: e.dma_start(out=o, in_=i), reads, writes)

        def barrier():
            S.barrier()

        R_const = Res("const")
        cm = AR.alloc([128, 8, 128], BF16)
        trineg, Dd, Ds, MO, MA, MB, onesneg, zero = [cm[:, i, :] for i in range(8)]
        ident = AR.alloc([128, 128], BF16)
        CS = AR.alloc([128, NSLOT, 8], F32)
        SN = AR.alloc([128, NSLOT, 8], F32)
        qg = AR.alloc([128, 64], F32)
        kg = AR.alloc([128, 64], F32)
        subg8 = AR.alloc([128, 128], F32)
        lv = AR.alloc([128, 4, 64], F32)
        lamt = AR.alloc([128, 16], F32)
        OFF_O = AR.off

        dma("pool", cm, cmat.rearrange("m p c -> p m c"), [], [R_const])
        dma("sp", CS, cs_t, [], [R_const])
        dma("sp", SN, sn_t, [], [R_const])
        dma("sp", qg, qng.partition_broadcast(128), [], [R_const])
        dma("sp", kg, kng.partition_broadcast(128), [], [R_const])
        dma("sp", subg8, subg.partition_broadcast(128), [], [R_const])
        for i in range(4):
            dma("sp", lv[:, i, :], lamv[i].partition_broadcast(128), [], [R_const])
        S.op("pool", lambda e: e.memset(ident, 1.0), [], [R_const])
        S.op("pool", lambda e: e.affine_select(out=ident, in_=ident, pattern=[[-1, 128]], compare_op=ALU.is_equal,
                                               fill=0.0, base=0, channel_multiplier=1), [R_const], [R_const])
        ts("dve", subg8, subg8, 1.0 - LAM_INIT, None, ALU.mult, None, [R_const], [R_const])
        tt("dve", lv[:, 0, :], lv[:, 0, :], lv[:, 1, :], ALU.mult, [R_const], [R_const])
        tt("dve", lv[:, 2, :], lv[:, 2, :], lv[:, 3, :], ALU.mult, [R_const], [R_const])
        S.op("dve", lambda e: e.tensor_reduce(out=lamt[:, 0:1], in_=lv[:, 0, :], axis=AX.X, op=ALU.add), [R_const], [R_const])
        S.op("dve", lambda e: e.tensor_reduce(out=lamt[:, 1:2], in_=lv[:, 2, :], axis=AX.X, op=ALU.add), [R_const], [R_const])
        act(lamt[:, 2:4], lamt[:, 0:2], AF.Exp, [R_const], [R_const])
        tt("dve", lamt[:, 5:6], lamt[:, 3:4], lamt[:, 2:3], ALU.subtract, [R_const], [R_const])
        ts("dve", lamt[:, 4:5], lamt[:, 5:6], -LAM_INIT, None, ALU.add, None, [R_const], [R_const])
        neg_lam = lamt[:, 4:5]

        Od = AR.alloc([128, NOWN + 1, 1024], BF16)
        Os = AR.alloc([128, NOWN + 1, 512], BF16)
        R_O = [Res(f"O{j}") for j in range(NOWN + 1)]
        OFF_P12 = AR.off
        hT = AR.alloc([128, 8, NT], BF16)
        hTs = AR.alloc([128, 8, DS], BF16)
        R_hT = [Res(f"hT{s}") for s in range(NB + 2)]
        OFF_HEAD = AR.off
        NALT = 2 * gq + 1
        KT = AR.alloc([128, KW + NALT * 128], BF16)
        Vaug = AR.alloc([128, NB + 1 + NALT, 130], BF16)
        SKT2 = AR.alloc([128, SKW + gq * 128 + 128], BF16)
        SVaug = AR.alloc([128, NB + 1 + NALT, 66], BF16)
        QT = AR.alloc([128, (NOWN + gq) * 128], BF16)
        SQT2 = AR.alloc([128, (NOWN + gq) * 128], BF16)
        Wh = AR.alloc([128, 8, 576], BF16)
        R_K = [Res(f"K{s}") for s in range(NB + 1)]
        R_Q = [Res(f"Q{s}") for s in range(NOWN)]
        R_Kalt = [Res(f"Ka{s}") for s in range(NALT)]
        R_Qalt = [Res(f"Qa{s}") for s in range(gq)]

        class Map:
            def __init__(self, par):
                self.par = par

            def alt(self, s):
                if not self.par:
                    return None
                if s < gq:
                    return s
                if NOWN <= s < NOWN + gq:
                    return gq + s - NOWN
                if s == NB:
                    return 2 * gq
                return None

            def kt(self, s):
                a = self.alt(s)
                return s * 128 if a is None else KW + a * 128

            def vs(self, s):
                a = self.alt(s)
                return s if a is None else NB + 1 + a

            def skl(self, s):
                a = self.alt(s)
                if a is None:
                    return (0, s * 128) if s < NOWN else (64, (s - NOWN) * 128)
                if a < gq:
                    return 0, SKW + a * 128
                return 64, SKW + (a - gq) * 128

            def qc(self, j):
                return (NOWN + j) * 128 if (self.par and j < gq) else j * 128

            def rk(self, s):
                a = self.alt(s)
                return R_K[s] if a is None else R_Kalt[a]

            def rq(self, j):
                return R_Qalt[j] if (self.par and j < gq) else R_Q[j]

        MAPS = [Map(0), Map(1)]
        R_Wh = Res("Wh")
        OFF_WORK = AR.off

        def slot_tok0(s):
            return s * 128

        def sk_loc(s):
            if s < NOWN:
                return 0, s * 128
            return 64, (s - NOWN) * 128

        S.op("pool", lambda e, t=Vaug[:, :, 128:130]: e.memset(t, 1.0), [], R_K)
        S.op("pool", lambda e, t=SVaug[:, :, 64:66]: e.memset(t, 1.0), [], R_K)

        AR.seek(OFF_WORK)
        x_st = [AR.alloc([128, DM], F32) for _ in range(2)]
        h_sb = [AR.alloc([128, DM], BF16) for _ in range(2)]
        junk = AR.alloc([128, DM], BF16)
        gb = AR.alloc([128, DM], F32)
        st1 = AR.alloc([128, 8], F32)
        R_x = [Res("x0"), Res("x1")]
        R_h = [Res("h0"), Res("h1")]
        R_junk = Res("junk")
        R_gb = Res("gb")
        R_st = [Res("st0"), Res("st1")]
        dma("sp", gb, g_mix.partition_broadcast(128), [], [R_gb])

        p1_cnt = [0]

        def norm_block(src_ap, nt, dstT, R_dst, gtile, R_g, x_keep=None, tbank=None):
            k = p1_cnt[0] % 2
            p1_cnt[0] += 1
            xb, hb = x_st[k], h_sb[k]
            if src_ap is not None:
                dma("sp", xb[0:nt, :], src_ap, [], [R_x[k]])
            else:
                xb = x_keep
            stc = st1[:, 4 * k:4 * k + 4]
            act(junk[0:nt, :], xb[0:nt, :], AF.Square, [R_x[k]], [R_junk, R_st[k]], accum=stc[0:nt, 0:1])
            act(stc[0:nt, 1:2], stc[0:nt, 0:1], AF.Ln, [R_st[k]], [R_st[k]], scale=1.0 / DM, bias=EPS)
            act(stc[0:nt, 2:3], stc[0:nt, 1:2], AF.Exp, [R_st[k]], [R_st[k]], scale=-0.5)
            stt("dve", hb[0:nt, :], xb[0:nt, :], stc[0:nt, 2:3], gtile[0:nt, :], ALU.mult, ALU.mult,
                [R_x[k], R_st[k], R_g], [R_h[k]])
            bi = 6 + k if tbank is None else tbank
            pT = bankb(bi)
            for ch in range(8):
                tr(pT[:, ch * 128:ch * 128 + nt], hb[0:nt, ch * 128:(ch + 1) * 128], [R_h[k]], [RB[bi]])
            src = pT.rearrange("p (c t) -> p c t", t=128)[:, :, 0:nt]
            return lambda: cp("act" if k == 0 else "dve", dstT, src, [RB[bi]], [R_dst])

        pend = None
        for s in range(NB + 1):
            nt = 128 if s < NB else N_META
            t0 = slot_tok0(s)
            nxt = norm_block(x_slots[t0:t0 + nt, :], nt, hT[:, :, t0:t0 + nt], R_hT[s], gb, R_gb)
            if pend is not None:
                pend()
            pend = nxt
        nxt = norm_block(xs[:, :], DS, hTs[:, :, :], R_hT[NB + 1], gb, R_gb)
        pend()
        nxt()
        barrier()

        AR.seek(OFF_WORK)
        sq_t = AR.alloc([128, 4, 2, 64], F32)
        kn = AR.alloc([128, 4, 2, 64], F32)
        qn = AR.alloc([128, 4, 2, 64], F32)
        knb = AR.alloc([128, 4, 128], BF16)
        qnb = knb
        rt = [AR.alloc([128, 4, 2, 8], F32) for _ in range(4)]
        vout = AR.alloc([128, 4, 128], F32)
        skout = AR.alloc([128, 4, 64], F32)
        svout = AR.alloc([128, 4, 64], F32)
        skb2 = AR.alloc([128, 4, 128], BF16)
        sqb2 = AR.alloc([128, 4, 128], BF16)
        stk = AR.alloc([128, 4, 4, 2], F32)
        stq = AR.alloc([128, 4, 4, 2], F32)
        Ebuf = [AR.alloc([128, 2, 512], BF16) for _ in range(2)]
        Et = [[Ebuf[b][:, c, :] for b in range(2)] for c in range(2)]
        Lt = [[AR.alloc([128, 512], BF16) for _ in range(2)] for _ in range(2)]
        Wt = Et
        Oacc = AR.alloc([128, 4, 64], F32)
        tmpS = AR.alloc([128, 4, 64], F32)
        stick = AR.alloc([128, 8], F32)
        o4 = [AR.alloc([128, 128], F32) for _ in range(4)]
        sqj = AR.alloc([128, 128], BF16)
        fin = AR.alloc([128, 16], F32)
        fin2 = AR.alloc([128, 16], F32)
        R_o4 = [Res(f"o4_{i}") for i in range(4)]
        R_sqj, R_fin2 = Res("sqj"), Res("fin2")
        OFF_P3 = AR.off
        R_sq, R_kn, R_qn, R_knb, R_rt, R_rt2 = Res("sq"), Res("kn"), Res("qn"), Res("knb"), Res("rt"), Res("rt2")
        R_qnb = R_knb
        R_vo, R_sko, R_svo, R_skb, R_sqb, R_stk, R_stq = (Res("vo"), Res("sko"), Res("svo"), Res("skb"),
                                                         Res("sqb"), Res("stk"), Res("stq"))
        R_E = [[Res(f"E{c}{b}") for b in range(2)] for c in range(2)]
        R_L = [[Res(f"L{c}{b}") for b in range(2)] for c in range(2)]
        R_W = R_E
        R_Oacc, R_tmpS, R_stick, R_ot, R_tmpt, R_fin = Res("Oacc"), Res("tmpS"), Res("stick"), Res("ot"), Res("tmpt"), Res("fin")
        S.op("pool", lambda e: e.memset(skb2, 0.0), [], [R_skb])

        w_in_v = w_in.rearrange("(c p) n -> p c n", p=128)

        def load_wh(hd):
            segs = [(1024 + hd * 128, 0, 128), (2048 + hd * 128, 128, 128), (3584 + hd * 64, 256, 64),
                    (4096 + hd * 64, 320, 64), (hd * 128, 384, 128), (3072 + hd * 64, 512, 64)]
            for c0, d0, w in segs:
                dma("pool", Wh[:, :, d0:d0 + w], w_in_v[:, :, c0:c0 + w], [], [R_Wh])

        KV4 = PS[:, 0:2048].rearrange("p (s c) -> p s c", c=512)
        Q4 = PS[:, 2048:3072].rearrange("p (s c) -> p s c", c=256)

        def qk_norm_ops(ns, nt, dst, R_dst, stt_, R_stt, gvec, slot0):
            ops = []
            d = dst[0:nt, 0:ns]
            ops.append(lambda: tt("pool", sq_t[0:nt, 0:ns], d, d, ALU.mult, [R_dst], [R_sq]))
            ops.append(lambda: S.op("dve", lambda e, n=nt, m=ns, s=stt_: e.tensor_reduce(out=s[0:n, 0, 0:m, :], in_=sq_t[0:n, 0:m], axis=AX.X, op=ALU.add),
                                    [R_sq], [R_stt]))
            ops.append(lambda: act(stt_[0:nt, 1, 0:ns, :], stt_[0:nt, 0, 0:ns, :], AF.Ln, [R_stt], [R_stt], scale=1.0 / 64, bias=EPS))
            ops.append(lambda: act(stt_[0:nt, 2, 0:ns, :], stt_[0:nt, 1, 0:ns, :], AF.Exp, [R_stt], [R_stt], scale=-0.5))
            ops.append(lambda: tt("dve", d, d, stt_[0:nt, 2, 0:ns, :].unsqueeze(3).to_broadcast([nt, ns, 2, 64]), ALU.mult,
                                  [R_stt, R_dst], [R_dst]))
            ops.append(lambda: tt("dve", d, d, gvec[0:nt, :].unsqueeze(1).unsqueeze(1).to_broadcast([nt, ns, 2, 64]), ALU.mult,
                                  [R_dst, R_const], [R_dst]))
            cosb = CS[0:nt, slot0:slot0 + ns, :].unsqueeze(2).to_broadcast([nt, ns, 2, 8])
            sinb = SN[0:nt, slot0:slot0 + ns, :].unsqueeze(2).to_broadcast([nt, ns, 2, 8])
            x1 = dst[0:nt, 0:ns, :, 0:8]
            x2 = dst[0:nt, 0:ns, :, 8:16]
            r = [t[0:nt, 0:ns] for t in rt]
            ops.append(lambda: tt("pool", r[0], x1, cosb, ALU.mult, [R_dst, R_const], [R_rt]))
            ops.append(lambda: tt("dve", r[1], x2, sinb, ALU.mult, [R_dst, R_const], [R_rt2]))
            ops.append(lambda: tt("pool", r[2], x2, cosb, ALU.mult, [R_dst, R_const], [R_rt]))
            ops.append(lambda: tt("dve", r[3], x1, sinb, ALU.mult, [R_dst, R_const], [R_rt2]))
            ops.append(lambda: tt("pool", x1, r[0], r[1], ALU.subtract, [R_rt, R_rt2], [R_dst]))
            ops.append(lambda: tt("dve", x2, r[2], r[3], ALU.add, [R_rt, R_rt2], [R_dst]))
            return ops

        def projA(slots, nt, with_q, hsrc, R_hsrc, qslots, outs, rope_slot0, par=0, bgfg=False):
            ns = len(slots)
            s0 = slots[0]
            mp = MAPS[par]
            Rk = [mp.rk(s) for s in slots]
            v0 = mp.vs(s0)
            pb_, c0 = mp.skl(s0)
            two_d = outs is not None and len(outs["kd"].shape) == 2

            def o3(ap):
                return ap[:, 0, :] if two_d else ap
            ops = []
            if not bgfg:
                for si, s in enumerate(slots):
                    hs = hsrc(si)
                    for ch in range(8):
                        mm(bank(si)[0:nt, 0:384], hs[:, ch, :], Wh[:, ch, 0:384], ch == 0, ch == 7, [R_hsrc[si], R_Wh], [RB[si]])
                    if with_q:
                        qb = 4 + si // 2
                        qo = (si % 2) * 256
                        for ch in range(8):
                            mm(bank(qb)[0:nt, qo:qo + 192], hs[:, ch, :], Wh[:, ch, 384:576], ch == 0, ch == 7,
                               [R_hsrc[si], R_Wh], [RB[qb]])
                kvv = KV4[0:nt, 0:ns, :]
                allb = [RB[i] for i in range(ns)]
                act(kn[0:nt, 0:ns], kvv[:, :, 0:128].rearrange("p s (c d) -> p s c d", c=2), AF.Copy, allb, [R_kn])
                vv = kvv[:, :, 128:256]
                cp("dve", Vaug[0:nt, v0:v0 + ns, 0:128], vv, allb, Rk)
                skv = kvv[:, :, 256:320]
                cp("dve", skb2[0:nt, 0:ns, pb_:pb_ + 64], skv, allb, [R_skb])
                svv = kvv[:, :, 320:384]
                cp("dve", SVaug[0:nt, v0:v0 + ns, 0:64], svv, allb, Rk)
                if outs is not None:
                    cp("act", vout[0:nt, 0:ns, :], vv, allb, [R_vo])
                    dma("sp", outs["vd"], o3(vout[0:nt, 0:ns, :]), [R_vo], [])
                    cp("act", skout[0:nt, 0:ns, :], skv, allb, [R_sko])
                    dma("sp", outs["sk"], o3(skout[0:nt, 0:ns, :]), [R_sko], [])
                    cp("act", svout[0:nt, 0:ns, :], svv, allb, [R_svo])
                    dma("sp", outs["sv"], o3(svout[0:nt, 0:ns, :]), [R_svo], [])
                if with_q:
                    qv = Q4[0:nt, 0:ns, :]
                    qbanks = [RB[4], RB[5]]
                    cp("dve", qn[0:nt, 0:ns], qv[:, :, 0:128].rearrange("p s (c d) -> p s c d", c=2), qbanks, [R_qn])
                    act(sqb2[0:nt, 0:ns, 0:64], qv[:, :, 128:192], AF.Copy, qbanks, [R_sqb], scale=0.125)
                    act(sqb2[0:nt, 0:ns, 64:128], qv[:, :, 128:192], AF.Copy, qbanks, [R_sqb], scale=0.125)
            else:
                b7 = bank(7)
                r7 = [RB[7]]
                for si, s in enumerate(slots):
                    hs = hsrc(si)

                    def mm_kv(hs=hs, si=si):
                        for ch in range(8):
                            mm(b7[0:nt, 0:384], hs[:, ch, :], Wh[:, ch, 0:384], ch == 0, ch == 7, [R_hsrc[si], R_Wh], r7)
                    ops.append(mm_kv)
                    ops.append(lambda si=si: act(kn[0:nt, si], b7[0:nt, 0:128].rearrange("p (c d) -> p c d", c=2), AF.Copy, r7, [R_kn]))
                    ops.append(lambda si=si: cp("dve", Vaug[0:nt, v0 + si, 0:128], b7[0:nt, 128:256], r7, [Rk[si]]))
                    ops.append(lambda si=si: cp("dve", skb2[0:nt, si, pb_:pb_ + 64], b7[0:nt, 256:320], r7, [R_skb]))
                    ops.append(lambda si=si: cp("dve", SVaug[0:nt, v0 + si, 0:64], b7[0:nt, 320:384], r7, [Rk[si]]))
                    if outs is not None:
                        ops.append(lambda si=si: cp("act", vout[0:nt, si, :], b7[0:nt, 128:256], r7, [R_vo]))
                        ops.append(lambda si=si: cp("act", skout[0:nt, si, :], b7[0:nt, 256:320], r7, [R_sko]))
                        ops.append(lambda si=si: cp("act", svout[0:nt, si, :], b7[0:nt, 320:384], r7, [R_svo]))
                    if with_q:
                        def mm_q(hs=hs, si=si):
                            for ch in range(8):
                                mm(b7[0:nt, 0:192], hs[:, ch, :], Wh[:, ch, 384:576], ch == 0, ch == 7, [R_hsrc[si], R_Wh], r7)
                        ops.append(mm_q)
                        ops.append(lambda si=si: cp("dve", qn[0:nt, si], b7[0:nt, 0:128].rearrange("p (c d) -> p c d", c=2), r7, [R_qn]))
                        ops.append(lambda si=si: act(sqb2[0:nt, si, 0:64], b7[0:nt, 128:192], AF.Copy, r7, [R_sqb], scale=0.125))
                        ops.append(lambda si=si: act(sqb2[0:nt, si, 64:128], b7[0:nt, 128:192], AF.Copy, r7, [R_sqb], scale=0.125))
                if outs is not None:
                    def out_dmas():
                        dma("sp", outs["vd"], o3(vout[0:nt, 0:ns, :]), [R_vo], [])
                        dma("sp", outs["sk"], o3(skout[0:nt, 0:ns, :]), [R_sko], [])
                        dma("sp", outs["sv"], o3(svout[0:nt, 0:ns, :]), [R_svo], [])
                    ops.append(out_dmas)
            wdt = ns * 128 if nt == 128 else nt
            pT2 = bankb(7)

            def sk_tr():
                for si in range(ns):
                    tr(pT2[:, si * 128:si * 128 + nt], skb2[0:nt, si, :], [R_skb], [RB[7]])
            ops.append(sk_tr)
            ops.append(lambda: cp("dve", SKT2[pb_:pb_ + 64, c0:c0 + wdt], pT2[pb_:pb_ + 64, 0:wdt], [RB[7]], Rk))
            if with_q:
                Rq = [mp.rq(s) for s in qslots]
                q0 = mp.qc(qslots[0])

                def sq_tr():
                    for si in range(ns):
                        tr(pT2[:, si * 128:si * 128 + nt], sqb2[0:nt, si, :], [R_sqb], [RB[7]])
                ops.append(sq_tr)
                ops.append(lambda: cp("dve", SQT2[:, q0:q0 + wdt], pT2[:, 0:wdt], [RB[7]], Rq))
            ops += qk_norm_ops(ns, nt, kn, R_kn, stk, R_stk, kg, rope_slot0)
            knv = kn[0:nt, 0:ns].rearrange("p s c d -> p s (c d)")
            if outs is not None:
                ops.append(lambda: dma("sp", outs["kd"], o3(knv), [R_kn], []))
            ops.append(lambda: cp("pool", knb[0:nt, 0:ns, :], knv, [R_kn], [R_knb]))

            def k_tr():
                for si in range(ns):
                    tr(pT2[:, si * 128:si * 128 + nt], knb[0:nt, si, :], [R_knb], [RB[7]])
            ops.append(k_tr)
            t0 = mp.kt(s0)
            ops.append(lambda: cp("dve", KT[:, t0:t0 + wdt], pT2[:, 0:wdt], [RB[7]], Rk))
            if with_q:
                ops += qk_norm_ops(ns, nt, qn, R_qn, stq, R_stq, qg, rope_slot0)
                ops.append(lambda: ts("dve", qnb[0:nt, 0:ns, :], qn[0:nt, 0:ns].rearrange("p s c d -> p s (c d)"), 0.125, None,
                                      ALU.mult, None, [R_qn], [R_qnb]))

                def q_tr():
                    for si in range(ns):
                        tr(pT2[:, si * 128:si * 128 + nt], qnb[0:nt, si, :], [R_qnb], [RB[7]])
                ops.append(q_tr)
                ops.append(lambda: cp("dve", QT[:, q0:q0 + wdt], pT2[:, 0:wdt], [RB[7]], Rq))
            return ops

        def run_serial(ops):
            for o in ops:
                o()

        def weave(steps, ops, frac=0.75):
            n, m = len(steps), len(ops)
            k = 0
            for i, st in enumerate(steps):
                st()
                tgt = m if n == 0 else min(m, int((i + 1) * m / max(1.0, n * frac)) + 1)
                while k < tgt:
                    ops[k]()
                    k += 1
            while k < m:
                ops[k]()
                k += 1

        def attn_diff(q0, nqb, qw, blocks, out_blk0, hd, par=0):
            mp = MAPS[par]
            def acc(i, c):
                a = i * 2 + c
                return 4 + a // 3, (a % 3) * 129

            def finalize():
                diff_finalize(nqb, qw, acc, out_blk0, hd)
            started = set()
            nblk = len(blocks)
            lastblk = {}
            for bi_, kb in enumerate(blocks):
                for i in range(kb["fq"], nqb):
                    lastblk[i] = bi_

            def qk(t):
                kb = blocks[t]
                buf = t % 2
                nk, fq, s = kb["nk"], kb["fq"], kb["slot"]
                N = (nqb - fq) * qw
                c0 = q0 + fq * qw
                k0 = mp.kt(s)
                for c in range(2):
                    bk = buf * 2 + c
                    mm(bank(bk)[0:nk, 0:N], KT[c * 64:(c + 1) * 64, k0:k0 + nk], QT[c * 64:(c + 1) * 64, c0:c0 + N], True, True,
                       [mp.rk(s)] + kb["rq"], [RB[bk]])
                b2 = buf * 2
                act(Ebuf[buf][0:nk, :, 0:N], PS[0:nk, b2 * 512:(b2 + 2) * 512].rearrange("p (c n) -> p c n", c=2)[:, :, 0:N], AF.Exp,
                    [RB[b2], RB[b2 + 1]], [R_E[0][buf], R_E[1][buf]])
                for c in range(2):
                    for qi, m in kb["masks"].items():
                        sl = Et[c][buf][0:nk, (qi - fq) * qw:(qi - fq + 1) * qw]
                        tt("dve", sl, sl, m[0:nk, 0:qw], ALU.mult, [R_E[c][buf], R_const], [R_E[c][buf]])

            def pv(t):
                kb = blocks[t]
                buf = t % 2
                nk, fq, s = kb["nk"], kb["fq"], kb["slot"]
                for i in range(fq, nqb):
                    for c in range(2):
                        bk, col = acc(i, c)
                        st = bk not in started
                        started.add(bk)
                        mm(bank(bk)[0:qw, col:col + 129], Et[c][buf][0:nk, (i - fq) * qw:(i - fq + 1) * qw],
                           Vaug[0:nk, mp.vs(s), 0:129], st, lastblk[i] == t, [R_E[c][buf], mp.rk(s)], [RB[bk]], skip=True)

            steps = []
            for t in range(nblk + 1):
                def st(t=t):
                    if t < nblk:
                        qk(t)
                    if t >= 1:
                        pv(t - 1)
                steps.append(st)
            steps.append(lambda: finalize())
            return steps

        def _unused():
            pass

        def diff_finalize(nqb, qw, acc, out_blk0, hd):
            f = fin[0:qw]
            f2 = fin2[0:qw]
            A = []
            for i in range(nqb):
                b0, c0_ = acc(i, 0)
                b1, c1_ = acc(i, 1)
                A.append((b0, bank(b0)[0:qw, c0_:c0_ + 129], b1, bank(b1)[0:qw, c1_:c1_ + 129]))
            for i, (b0, a0, b1, a1) in enumerate(A):
                S.op("dve", lambda e, o=f[:, i:i + 1], a=a0[:, 128:129]: e.reciprocal(out=o, in_=a), [RB[b0]], [R_fin])
                S.op("dve", lambda e, o=f[:, 4 + i:5 + i], a=a1[:, 128:129]: e.reciprocal(out=o, in_=a), [RB[b1]], [R_fin])
            ts("dve", f[:, 8:8 + nqb], f[:, 4:4 + nqb], neg_lam[0:qw, 0:1], None, ALU.mult, None, [R_fin, R_const], [R_fin])
            for i, (b0, a0, b1, a1) in enumerate(A):
                act(o4[i][0:qw, :], a1[:, 0:128], AF.Copy, [RB[b1], R_fin], [R_o4[i]], scale=f[:, 8 + i:9 + i])
                stt("dve", o4[i][0:qw, :], a0[:, 0:128], f[:, i:i + 1], o4[i][0:qw, :], ALU.mult, ALU.add,
                    [RB[b0], R_fin, R_o4[i]], [R_o4[i]])
                act(sqj[0:qw, :], o4[i][0:qw, :], AF.Square, [R_o4[i]], [R_sqj, R_fin2], accum=f2[:, i:i + 1])
            act(f2[:, 4:4 + nqb], f2[:, 0:nqb], AF.Ln, [R_fin2], [R_fin2], scale=1.0 / 128, bias=EPS)
            act(f2[:, 8:8 + nqb], f2[:, 4:4 + nqb], AF.Exp, [R_fin2], [R_fin2], scale=-0.5)
            for i in range(nqb):
                stt("dve", Od[0:qw, out_blk0 + i, hd * 128:(hd + 1) * 128], o4[i][0:qw, :], f2[:, 8 + i:9 + i], subg8[0:qw, :],
                    ALU.mult, ALU.mult, [R_o4[i], R_fin2, R_const], [R_O[out_blk0 + i]])

        def attn_sb(q0, nqb, qw, units, out_blk0, hd, ma, mb, par=0):
            mp = MAPS[par]
            nu = len(units)
            tt_ = None
            S.op("dve", lambda e: e.memset(stick[0:qw, 0:nqb], 1.0), [], [R_stick])
            S.op("dve", lambda e: e.memset(Oacc[0:qw, 0:nqb, :], 0.0), [], [R_Oacc])

            def geo(u):
                kb = units[u][0]
                fq = kb["fq"]
                return fq, (nqb - fq) * qw, q0 + fq * qw

            def s1(u):
                fq, N, c0 = geo(u)
                buf = u % 2
                for x, kb in enumerate(units[u]):
                    bk = buf * 2 + x
                    nk, s = kb["nk"], kb["slot"]
                    pb_, k0 = mp.skl(s)
                    mm(bank(bk)[0:nk, 0:N], SKT2[pb_:pb_ + 64, k0:k0 + nk], SQT2[pb_:pb_ + 64, c0:c0 + N], True, True,
                       [mp.rk(s)] + kb["rq"], [RB[bk]])
                for x, kb in enumerate(units[u]):
                    bk = buf * 2 + x
                    nk = kb["nk"]
                    eb = 6
                    act(bank(eb)[0:nk, 0:N], bank(bk)[0:nk, 0:N], AF.Exp, [RB[bk]], [RB[eb]])
                    act(Lt[x][buf][0:nk, 0:N], bank(eb)[0:nk, 0:N], AF.Ln, [RB[eb]], [R_L[x][buf]], bias=1.0)
                    for qi, m in kb["masks"].items():
                        sl = Lt[x][buf][0:nk, (qi - fq) * qw:(qi - fq + 1) * qw]
                        tt("dve", sl, sl, m[0:nk, 0:qw], ALU.mult, [R_L[x][buf], R_const], [R_L[x][buf]])

            def s2(u):
                fq, N, c0 = geo(u)
                buf = u % 2
                un = units[u]
                for x, kb in enumerate(un):
                    bk = buf * 2 + x
                    nk = kb["nk"]
                    pair = len(un) == 2
                    mm(bank(bk)[0:nk, 0:N], trineg[0:nk, 0:nk], Lt[x][buf][0:nk, 0:N], False, not pair,
                       [R_L[x][buf], R_const], [RB[bk]], skip=True)
                    if pair:
                        ok = un[1 - x]["nk"]
                        cm_ = ma if x == 0 else mb
                        mm(bank(bk)[0:nk, 0:N], cm_[0:ok, 0:nk], Lt[1 - x][buf][0:ok, 0:N], False, True,
                           [R_L[1 - x][buf], R_const], [RB[bk]], skip=True)
                if len(un) == 2 and un[0]["nk"] == un[1]["nk"]:
                    b2 = buf * 2
                    nk2 = un[0]["nk"]
                    act(Ebuf[buf][0:nk2, :, 0:N], PS[0:nk2, b2 * 512:(b2 + 2) * 512].rearrange("p (c n) -> p c n", c=2)[:, :, 0:N], AF.Exp,
                        [RB[b2], RB[b2 + 1]], [R_W[0][buf], R_W[1][buf]])
                for x, kb in enumerate(un):
                    bk = buf * 2 + x
                    nk = kb["nk"]
                    if not (len(un) == 2 and un[0]["nk"] == un[1]["nk"]):
                        act(Wt[x][buf][0:nk, 0:N], bank(bk)[0:nk, 0:N], AF.Exp, [RB[bk]], [R_W[x][buf]])
                    for qi, m in kb["masks"].items():
                        sl = Wt[x][buf][0:nk, (qi - fq) * qw:(qi - fq + 1) * qw]
                        tt("dve", sl, sl, m[0:nk, 0:qw], ALU.mult, [R_W[x][buf], R_const], [R_W[x][buf]])

            def s3(u):
                fq, N, c0 = geo(u)
                buf = u % 2
                un = units[u]
                pbk = 4 + buf
                first = True
                for i in range(fq, nqb):
                    for x, kb in enumerate(un):
                        nk, s = kb["nk"], kb["slot"]
                        mm(bank(pbk)[0:qw, i * 65:i * 65 + 65], Wt[x][buf][0:nk, (i - fq) * qw:(i - fq + 1) * qw],
                           SVaug[0:nk, mp.vs(s), 0:65], first, x == len(un) - 1, [R_W[x][buf], mp.rk(s)], [RB[pbk]], skip=True)
                        first = False
                n = nqb - fq
                Pv = bank(pbk)[0:qw, 0:nqb * 65].rearrange("p (i c) -> p i c", c=65)
                tt("dve", tmpS[0:qw, fq:nqb, :], Pv[:, fq:nqb, 0:64],
                   stick[0:qw, fq:nqb].unsqueeze(2).to_broadcast([qw, n, 64]), ALU.mult, [RB[pbk], R_stick], [R_tmpS])
                tt("dve", Oacc[0:qw, fq:nqb, :], Oacc[0:qw, fq:nqb, :], tmpS[0:qw, fq:nqb, :], ALU.add, [R_tmpS, R_Oacc], [R_Oacc])
                if u < nu - 1:
                    ts("dve", fin[0:qw, 8:8 + n], Pv[:, fq:nqb, 64], -1.0, 1.0, ALU.mult, ALU.add, [RB[pbk]], [R_fin])
                    tt("dve", stick[0:qw, fq:nqb], stick[0:qw, fq:nqb], fin[0:qw, 8:8 + n], ALU.mult, [R_fin, R_stick], [R_stick])

            steps = []
            for t in range(nu + 2):
                def st(t=t):
                    if t < nu:
                        s1(t)
                    if 1 <= t <= nu:
                        s2(t - 1)
                    if t >= 2:
                        s3(t - 2)
                steps.append(st)
            steps.append(lambda: cp("pool", Os[0:qw, out_blk0:out_blk0 + nqb, hd * 64:(hd + 1) * 64], Oacc[0:qw, 0:nqb, :], [R_Oacc],
                                    [R_O[out_blk0 + i] for i in range(nqb)]))
            return steps

        kd_v = kd_o.rearrange("(s t) h d -> t s h d", t=128)
        vd_v = vd_o.rearrange("(s t) h d -> t s h d", t=128)
        sk_v = sk_o.rearrange("(s t) h d -> t s h d", t=128)
        sv_v = sv_o.rearrange("(s t) h d -> t s h d", t=128)

        def own_batch(g, hd):
            own = list(range(g * gq, (g + 1) * gq))
            return projA(own, 128, True, lambda si, o=own: hT[:, :, o[si] * 128:(o[si] + 1) * 128], [R_hT[s] for s in own], own,
                         dict(kd=kd_v[:, own[0]:own[-1] + 1, hd, :], vd=vd_v[:, own[0]:own[-1] + 1, hd, :],
                              sk=sk_v[:, own[0]:own[-1] + 1, hd, :], sv=sv_v[:, own[0]:own[-1] + 1, hd, :]), own[0], par=hd % 2)

        def oth_batch(g, hd):
            oth = [NOWN + j for j in range(g * gq, (g + 1) * gq)]
            return projA(oth, 128, False, lambda si, o=oth: hT[:, :, o[si] * 128:(o[si] + 1) * 128], [R_hT[s] for s in oth], None,
                         None, oth[0], par=hd % 2)

        def meta_batch(hd, bgfg=False):
            return projA([NB], N_META, False, lambda si: hT[:, :, NB * 128:NB * 128 + N_META], [R_hT[NB]], None,
                         dict(kd=kd_m[:, hd, :], vd=vd_m[:, hd, :], sk=sk_m[:, hd, :], sv=sv_m[:, hd, :]), NB, par=hd % 2,
                         bgfg=bgfg)

        def prompt_head(hd):
            par = hd % 2
            mp = MAPS[par]
            if hd == 0:
                run_serial(meta_batch(hd))
                run_serial(own_batch(0, hd))
                run_serial(oth_batch(0, hd))
            if NG == 1 and hd + 1 < NH:
                load_wh(hd + 1)
            for g in range(NG):
                j0 = g * gq
                rq = [mp.rq(j) for j in range(j0, j0 + gq)]
                q0 = mp.qc(j0)
                if g + 1 < NG:
                    bg = own_batch(g + 1, hd)
                elif hd + 1 < NH:
                    bg = own_batch(0, hd + 1)
                else:
                    bg = []
                if stop_after >= 2:
                    blocks = [dict(slot=NB, nk=N_META, fq=0, masks={}, rq=rq)]
                    for i in range(j0 + gq):
                        fq = max(i, j0) - j0
                        mo = {fq: Dd} if i >= j0 else {}
                        mo2 = {fq: MO} if i >= j0 else {}
                        blocks.append(dict(slot=i, nk=128, fq=fq, masks=mo, rq=rq))
                        blocks.append(dict(slot=NOWN + i, nk=128, fq=fq, masks=mo2, rq=rq))
                    weave(attn_diff(q0, gq, 128, blocks, j0, hd, par), bg)
                else:
                    run_serial(bg)
                if g + 1 < NG:
                    bg = oth_batch(g + 1, hd)
                elif hd + 1 < NH:
                    bg = oth_batch(0, hd + 1)
                    bg = bg + meta_batch(hd + 1, bgfg=True)
                else:
                    bg = []
                if g + 1 == NG - 1 and hd + 1 < NH:
                    load_wh(hd + 1)
                if stop_after >= 3:
                    units = []
                    for i in range(j0 + gq - 1, -1, -1):
                        fq = max(i, j0) - j0
                        mo = {fq: Ds} if i >= j0 else {}
                        mo2 = {fq: MO} if i >= j0 else {}
                        units.append([dict(slot=i, nk=128, fq=fq, masks=mo, rq=rq),
                                      dict(slot=NOWN + i, nk=128, fq=fq, masks=mo2, rq=rq)])
                    units.append([dict(slot=NB, nk=N_META, fq=0, masks={}, rq=rq)])
                    weave(attn_sb(q0, gq, 128, units, j0, hd, MA, MB, par), bg)
                else:
                    run_serial(bg)

        load_wh(0)
        for hd in range(NH if stop_after >= 1 else 0):
            prompt_head(hd)
        barrier()

        def sample_diff(hd):
            qw = DS
            batches = [list(range(b0, min(b0 + 16, PB))) for b0 in range(0, PB, 16)] + [[NB]]
            nbt = len(batches)
            started = set()

            def nkof(b):
                return DS if b == NB else 128

            def qk(t):
                bt = batches[t]
                buf = t % 2
                nk = nkof(bt[0])
                for c in range(2):
                    bk = buf * 2 + c
                    for k, b in enumerate(bt):
                        k0 = slot_tok0(b)
                        mm(bank(bk)[0:nk, k * qw:(k + 1) * qw], KT[c * 64:(c + 1) * 64, k0:k0 + nk], QT[c * 64:(c + 1) * 64, 0:qw],
                           k == 0, True, [R_K[b], R_Q[0]], [RB[bk]], skip=True)
                for c in range(2):
                    bk = buf * 2 + c
                    act(Et[c][buf][0:nk, 0:len(bt) * qw], bank(bk)[0:nk, 0:len(bt) * qw], AF.Exp, [RB[bk]], [R_E[c][buf]])

            def pv(t):
                bt = batches[t]
                buf = t % 2
                nk = nkof(bt[0])
                for k, b in enumerate(bt):
                    for c in range(2):
                        bk, col = 4, c * 129
                        st = bk not in started
                        started.add(bk)
                        mm(bank(bk)[0:qw, col:col + 129], Et[c][buf][0:nk, k * qw:(k + 1) * qw], Vaug[0:nk, b, 0:129],
                           st, (t == nbt - 1 and k == len(bt) - 1), [R_E[c][buf], R_K[b]], [RB[bk]], skip=True)

            for t in range(nbt + 1):
                if t < nbt:
                    qk(t)
                if t >= 1:
                    pv(t - 1)
            diff_finalize(1, qw, lambda i, c: (4, c * 129), NOWN, hd)

        def sample_sb(hd):
            qw = DS
            pairs = [(2 * i, 2 * i + 1) for i in range(PB // 2 - 1, -1, -1)]
            batches = [[(NB,)]] + [pairs[i:i + 7] for i in range(0, len(pairs), 7)]
            nbt = len(batches)
            S.op("dve", lambda e: e.memset(stick[0:qw, 0:1], 1.0), [], [R_stick])
            S.op("dve", lambda e: e.memset(Oacc[0:qw, 0:1, :], 0.0), [], [R_Oacc])

            def nkof(b):
                return DS if b == NB else 128

            def s1(t):
                bt = batches[t]
                buf = t % 2
                npair = len(bt)
                N = npair * qw
                nx = len(bt[0])
                nk = nkof(bt[0][0])
                for x in range(nx):
                    bk = buf * 2 + x
                    for k, un in enumerate(bt):
                        b = un[x]
                        pb_, k0 = sk_loc(b)
                        mm(bank(bk)[0:nk, k * qw:(k + 1) * qw], SKT2[pb_:pb_ + 64, k0:k0 + nk], SQT2[pb_:pb_ + 64, 0:qw], k == 0, True,
                           [R_K[b], R_Q[0]], [RB[bk]], skip=True)
                for x in range(nx):
                    bk = buf * 2 + x
                    act(bank(6)[0:nk, 0:N], bank(bk)[0:nk, 0:N], AF.Exp, [RB[bk]], [RB[6]])
                    act(Lt[x][buf][0:nk, 0:N], bank(6)[0:nk, 0:N], AF.Ln, [RB[6]], [R_L[x][buf]], bias=1.0)
                    if nx == 1:
                        sl = Lt[x][buf][0:nk, 0:qw]
                        tt("dve", sl, sl, Ds[0:nk, 0:qw], ALU.mult, [R_L[x][buf], R_const], [R_L[x][buf]])

            def s2(t):
                bt = batches[t]
                buf = t % 2
                N = len(bt) * qw
                nx = len(bt[0])
                nk = nkof(bt[0][0])
                for x in range(nx):
                    bk = buf * 2 + x
                    mm(bank(bk)[0:nk, 0:N], trineg[0:nk, 0:nk], Lt[x][buf][0:nk, 0:N], False, nx == 1, [R_L[x][buf], R_const], [RB[bk]], skip=True)
                    if nx == 2:
                        cm_ = onesneg if x == 0 else zero
                        mm(bank(bk)[0:nk, 0:N], cm_[0:nk, 0:nk], Lt[1 - x][buf][0:nk, 0:N], False, True,
                           [R_L[1 - x][buf], R_const], [RB[bk]], skip=True)
                for x in range(nx):
                    bk = buf * 2 + x
                    act(Wt[x][buf][0:nk, 0:N], bank(bk)[0:nk, 0:N], AF.Exp, [RB[bk]], [R_W[x][buf]])
                    if nx == 1:
                        sl = Wt[x][buf][0:nk, 0:qw]
                        tt("dve", sl, sl, Ds[0:nk, 0:qw], ALU.mult, [R_W[x][buf], R_const], [R_W[x][buf]])

            def s3(t):
                bt = batches[t]
                buf = t % 2
                nx = len(bt[0])
                nk = nkof(bt[0][0])
                pbk = 4 + buf
                first = True
                for k, un in enumerate(bt):
                    for x in range(nx):
                        b = un[x]
                        mm(bank(pbk)[0:qw, k * 65:k * 65 + 65], Wt[x][buf][0:nk, k * qw:(k + 1) * qw], SVaug[0:nk, b, 0:65],
                           first, x == nx - 1, [R_W[x][buf], R_K[b]], [RB[pbk]], skip=True)
                        first = False
                for k, un in enumerate(bt):
                    Pk = bank(pbk)[0:qw, k * 65:k * 65 + 65]
                    stt("dve", Oacc[0:qw, 0, :], Pk[:, 0:64], stick[0:qw, 0:1], Oacc[0:qw, 0, :], ALU.mult, ALU.add,
                        [RB[pbk], R_stick, R_Oacc], [R_Oacc])
                    if not (t == nbt - 1 and k == len(bt) - 1):
                        ts("dve", fin[0:qw, 8:9], Pk[:, 64:65], -1.0, 1.0, ALU.mult, ALU.add, [RB[pbk]], [R_fin])
                        tt("dve", stick[0:qw, 0:1], stick[0:qw, 0:1], fin[0:qw, 8:9], ALU.mult, [R_fin, R_stick], [R_stick])

            for t in range(nbt + 2):
                if t < nbt:
                    s1(t)
                if 1 <= t <= nbt:
                    s2(t - 1)
                if t >= 2:
                    s3(t - 2)
            cp("pool", Os[0:qw, NOWN:NOWN + 1, hd * 64:(hd + 1) * 64], Oacc[0:qw, 0:1, :], [R_Oacc], [R_O[NOWN]])

        if stop_after >= 4:
            AR.seek(OFF_P12)
            Kc2 = [AR.alloc([128, PB, 128], BF16) for _ in range(2)]
            SKc2 = [AR.alloc([128, PB, 128], BF16) for _ in range(2)]
            KT_B = AR.alloc([128, KW], BF16)
            Vaug_B = AR.alloc([128, NB + 1, 130], BF16)
            SKT2_B = AR.alloc([128, SKW], BF16)
            SVaug_B = AR.alloc([128, NB + 1, 66], BF16)
            assert AR.off <= OFF_P12 + 8 * NT * 2
            R_K_B = [Res(f"KB{s}") for s in range(NB + 1)]
            ksets = [(KT, Vaug, SKT2, SVaug, R_K), (KT_B, Vaug_B, SKT2_B, SVaug_B, R_K_B)]
            R_Kc2, R_SKc2 = [Res("Kc0"), Res("Kc1")], [Res("SKc0"), Res("SKc1")]
            for k in range(2):
                S.op("pool", lambda e, t=SKc2[k]: e.memset(t, 0.0), [], [R_SKc2[k]])
            S.op("pool", lambda e, t=Vaug_B[:, :, 128:130]: e.memset(t, 1.0), [], R_K_B)
            S.op("pool", lambda e, t=SVaug_B[:, :, 64:66]: e.memset(t, 1.0), [], R_K_B)
            ck_v = c_dk.rearrange("(b t) h d -> t b h d", t=128)
            cv_v = c_dv.rearrange("(b t) h d -> t b h d", t=128)
            csk_v = c_sk.rearrange("(b t) h d -> t b h d", t=128)
            csv_v = c_sv.rearrange("(b t) h d -> t b h d", t=128)
            HB = PB // 2

            def issue_loads(hd):
                k = hd % 2
                kt_, va_, skt_, sva_, rk_ = ksets[k]
                dma("pool", Kc2[k], ck_v[:, :, hd, :], [], [R_Kc2[k]])
                dma("pool", va_[:, 0:PB, 0:128], cv_v[:, :, hd, :], [], rk_[0:PB])
                dma("pool", SKc2[k][:, 0:HB, 0:64], csk_v[:, 0:HB, hd, :], [], [R_SKc2[k]])
                dma("pool", SKc2[k][:, HB:PB, 64:128], csk_v[:, HB:PB, hd, :], [], [R_SKc2[k]])
                dma("pool", sva_[:, 0:PB, 0:64], csv_v[:, :, hd, :], [], rk_[0:PB])

            load_wh(0)
            issue_loads(0)
            for hd in range(NH):
                if hd + 1 < NH:
                    issue_loads(hd + 1)
                k2 = hd % 2
                KT, Vaug, SKT2, SVaug, R_K = ksets[k2]
                Kc, SKc, R_Kc, R_SKc = Kc2[k2], SKc2[k2], R_Kc2[k2], R_SKc2[k2]
                bgs = projA([NB], DS, True, lambda si: hTs[:, :, :], [R_hT[NB + 1]], [0],
                            dict(kd=kd_s[:, hd, :], vd=vd_s[:, hd, :], sk=sk_s[:, hd, :], sv=sv_s[:, hd, :]), NB + 1)
                if hd + 1 < NH:
                    load_wh(hd + 1)
                tsteps = []
                for b0 in range(0, PB, 4):
                    def st_k(b0=b0, KT=KT, Kc=Kc, R_Kc=R_Kc, R_K=R_K):
                        bi = 2 + (b0 // 4) % 2
                        pT = bankb(bi)
                        for k in range(4):
                            tr(pT[:, k * 128:(k + 1) * 128], Kc[:, b0 + k, :], [R_Kc], [RB[bi]])
                        cp("act", KT[:, b0 * 128:(b0 + 4) * 128], pT[:, 0:512], [RB[bi]], R_K[b0:b0 + 4])
                    tsteps.append(st_k)
                for b0 in range(0, PB, 4):
                    def st_s(b0=b0, SKT2=SKT2, SKc=SKc, R_SKc=R_SKc, R_K=R_K):
                        bi = 2 + (b0 // 4) % 2
                        pT = bankb(bi)
                        for k in range(4):
                            tr(pT[:, k * 128:(k + 1) * 128], SKc[:, b0 + k, :], [R_SKc], [RB[bi]])
                        pb_, c0 = sk_loc(b0)
                        cp("dve", SKT2[pb_:pb_ + 64, c0:c0 + 512], pT[pb_:pb_ + 64, 0:512], [RB[bi]], R_K[b0:b0 + 4])
                    tsteps.append(st_s)
                weave(tsteps, bgs)
                sample_diff(hd)
                sample_sb(hd)
            barrier()

        if stop_after >= 5:
            OFF_MG = OFF_P12
            AR.seek(OFF_MG)
            mg = AR.alloc([128, NOWN + 1, 1024], BF16)
            R_mg = [Res(f"mg{j}") for j in range(NOWN + 1)]
            OFF_X1 = AR.off
            x_st = [AR.alloc([128, DM], F32) for _ in range(2)]
            h_sb = [AR.alloc([128, DM], BF16) for _ in range(2)]
            junk = AR.alloc([128, DM], BF16)
            gb = AR.alloc([128, DM], F32)
            st1 = AR.alloc([128, 8], F32)
            hTb = [AR.alloc([128, 8, 128], BF16) for _ in range(2)]
            odT = [AR.alloc([128, 8, 128], BF16) for _ in range(2)]
            osT = [AR.alloc([128, 4, 128], BF16) for _ in range(2)]
            egd = [AR.alloc([128, 1024], F32) for _ in range(2)]
            egs = [AR.alloc([128, 1024], F32) for _ in range(2)]
            Wd = AR.alloc([128, 8, 1024], BF16)
            Ws = AR.alloc([128, 4, 1024], BF16)
            Wg = AR.alloc([128, 8, 2048], BF16)
            R_x = [Res("x0"), Res("x1")]
            R_h = [Res("h0"), Res("h1")]
            R_junk, R_gb = Res("junk"), Res("gb")
            R_st = [Res("st0"), Res("st1")]
            R_hTb, R_odT, R_osT, R_egd, R_egs = [[Res(f"{n}{k}") for k in range(2)] for n in ("hTb", "odT", "osT", "egd", "egs")]
            R_Wd, R_Ws, R_Wg = Res("Wd"), Res("Ws"), Res("Wg")
            p1_cnt[0] = 0
            dma("sp", gb, g_mix.partition_broadcast(128), [], [R_gb])
            dma("pool", Wg, w_in_v[:, :, 4608:6656], [], [R_Wg])
            dma("pool", Wd, w_do.rearrange("(c p) n -> p c n", p=128), [], [R_Wd])
            dma("pool", Ws, w_so.rearrange("(c p) n -> p c n", p=128), [], [R_Ws])

            def blk_src(j):
                if j < NOWN:
                    return x_slots[j * 128:(j + 1) * 128, :], 128
                return xs[:, :], DS

            def c1a_A1(j):
                src, nt = blk_src(j)
                k = j % 2
                xb, hb = x_st[k], h_sb[k]
                dma("sp", xb[0:nt, :], src, [], [R_x[k]])
                stc = st1[:, 4 * k:4 * k + 4]
                act(junk[0:nt, :], xb[0:nt, :], AF.Square, [R_x[k]], [R_junk, R_st[k]], accum=stc[0:nt, 0:1])
                act(stc[0:nt, 1:2], stc[0:nt, 0:1], AF.Ln, [R_st[k]], [R_st[k]], scale=1.0 / DM, bias=EPS)
                act(stc[0:nt, 2:3], stc[0:nt, 1:2], AF.Exp, [R_st[k]], [R_st[k]], scale=-0.5)
                stt("dve", hb[0:nt, :], xb[0:nt, :], stc[0:nt, 2:3], gb[0:nt, :], ALU.mult, ALU.mult,
                    [R_x[k], R_st[k], R_gb], [R_h[k]])

            def c1a_Th(j):
                src, nt = blk_src(j)
                k = j % 2
                pT = bankb(7)
                for ch in range(8):
                    tr(pT[:, ch * 128:ch * 128 + nt], h_sb[k][0:nt, ch * 128:(ch + 1) * 128], [R_h[k]], [RB[7]])
                cp("dve", hTb[k][:, :, 0:nt], pT.rearrange("p (c t) -> p c t", t=128)[:, :, 0:nt], [RB[7]], [R_hTb[k]])

            def c1a_G(j):
                src, nt = blk_src(j)
                k = j % 2
                for gi, (b0, dst, R_dst) in enumerate([(0, egd[k], R_egd[k]), (2, egs[k], R_egs[k])]):
                    for half in range(2):
                        for ch in range(8):
                            mm(bank(b0 + half)[0:nt, :], hTb[k][:, ch, 0:nt], Wg[:, ch, gi * 1024 + half * 512:gi * 1024 + (half + 1) * 512],
                               ch == 0, ch == 7, [R_hTb[k], R_Wg], [RB[b0 + half]])
                    gv = PS[0:nt, b0 * 512:(b0 + 2) * 512]
                    act(dst[0:nt, :], gv, AF.Exp, [RB[b0], RB[b0 + 1]], [R_dst], scale=-1.0)
                    act(dst[0:nt, :], dst[0:nt, :], AF.Ln, [R_dst], [R_dst], bias=1.0)
                    act(dst[0:nt, :], dst[0:nt, :], AF.Exp, [R_dst], [R_dst], scale=-1.0)

            def c1a_To(j):
                src, nt = blk_src(j)
                k = j % 2
                pT = bankb(7)
                for ch in range(8):
                    tr(pT[:, ch * 128:ch * 128 + nt], Od[0:nt, j, ch * 128:(ch + 1) * 128], [R_O[j]], [RB[7]])
                cp("act", odT[k][:, :, 0:nt], pT.rearrange("p (c t) -> p c t", t=128)[:, :, 0:nt], [RB[7]], [R_odT[k]])
                for ch in range(4):
                    tr(pT[:, ch * 128:ch * 128 + nt], Os[0:nt, j, ch * 128:(ch + 1) * 128], [R_O[j]], [RB[7]])
                cp("dve", osT[k][:, :, 0:nt], pT[:, 0:512].rearrange("p (c t) -> p c t", t=128)[:, :, 0:nt], [RB[7]], [R_osT[k]])

            def c1a_M(j):
                src, nt = blk_src(j)
                k = j % 2
                for half in range(2):
                    for ch in range(8):
                        mm(bank(4 + half)[0:nt, :], odT[k][:, ch, 0:nt], Wd[:, ch, half * 512:(half + 1) * 512], ch == 0, ch == 7,
                           [R_odT[k], R_Wd], [RB[4 + half]])
                tt("dve", egd[k][0:nt, :], PS[0:nt, 2048:3072], egd[k][0:nt, :], ALU.mult, [RB[4], RB[5], R_egd[k]], [R_egd[k]])
                for half in range(2):
                    for ch in range(4):
                        mm(bank(6)[0:nt, :], osT[k][:, ch, 0:nt], Ws[:, ch, half * 512:(half + 1) * 512], ch == 0, ch == 3,
                           [R_osT[k], R_Ws], [RB[6]])
                    hs_ = slice(half * 512, (half + 1) * 512)
                    tt("dve", egs[k][0:nt, hs_], bank(6)[0:nt, :], egs[k][0:nt, hs_], ALU.mult, [RB[6], R_egs[k]], [R_egs[k]])
                tt("pool", mg[0:nt, j, :], egd[k][0:nt, :], egs[k][0:nt, :], ALU.add, [R_egd[k], R_egs[k]], [R_mg[j]])

            c1a_A1(0)
            c1a_A1(1)
            c1a_Th(0)
            c1a_To(0)
            for j in range(NOWN + 1):
                c1a_G(j)
                if j + 1 <= NOWN:
                    c1a_Th(j + 1)
                c1a_M(j)
                if j + 1 <= NOWN:
                    c1a_To(j + 1)
                if j + 2 <= NOWN:
                    c1a_A1(j + 2)
            barrier()

            AR.seek(OFF_X1)
            x1 = AR.alloc([128, NOWN + 1, 1024], F32)
            h2T = AR.alloc([128, 8, NOWN * 128 + DS], BF16)
            OFF_END = AR.off
            AR.seek(OFF_O if OFF_O + 38000 <= OFF_MG else OFF_END)
            x_st = [AR.alloc([128, DM], F32) for _ in range(2)]
            h_sb = [AR.alloc([128, DM], BF16) for _ in range(2)]
            junk = AR.alloc([128, DM], BF16)
            gb = AR.alloc([128, DM], F32)
            st1 = AR.alloc([128, 8], F32)
            mT = [AR.alloc([128, 8, 128], BF16) for _ in range(2)]
            Wo = AR.alloc([128, 8, 1024], BF16)
            assert AR.off <= OFF_MG or AR.off > OFF_END
            R_x1 = [Res(f"x1_{j}") for j in range(NOWN + 1)]
            R_h2T = [Res(f"h2T{j}") for j in range(NOWN + 1)]
            R_x = [Res("x0"), Res("x1")]
            R_h = [Res("h0"), Res("h1")]
            R_junk, R_gb = Res("junk"), Res("gb")
            R_st = [Res("st0"), Res("st1")]
            R_mT, R_Wo = [Res("mT0"), Res("mT1")], Res("Wo")
            p1_cnt[0] = 0
            dma("sp", gb, g_ffn.partition_broadcast(128), [], [R_gb])
            dma("pool", Wo, w_o.rearrange("(c p) n -> p c n", p=128), [], [R_Wo])
            def c1b_Tm(j):
                src, nt = blk_src(j)
                k = j % 2
                pT = bankb(6 + k)
                for ch in range(8):
                    tr(pT[:, ch * 128:ch * 128 + nt], mg[0:nt, j, ch * 128:(ch + 1) * 128], [R_mg[j]], [RB[6 + k]])
                cp("act", mT[k][:, :, 0:nt], pT.rearrange("p (c t) -> p c t", t=128)[:, :, 0:nt], [RB[6 + k]], [R_mT[k]])

            def c1b_X(j):
                src, nt = blk_src(j)
                dma("sp", x_st[j % 2][0:nt, :], src, [], [R_x[j % 2]])

            def c1b_Mo(j):
                src, nt = blk_src(j)
                k = j % 2
                b0 = 2 * k
                for half in range(2):
                    for ch in range(8):
                        mm(bank(b0 + half)[0:nt, :], mT[k][:, ch, 0:nt], Wo[:, ch, half * 512:(half + 1) * 512], ch == 0, ch == 7,
                           [R_mT[k], R_Wo], [RB[b0 + half]])
                tt("dve", x1[0:nt, j, :], PS[0:nt, b0 * 512:(b0 + 2) * 512], x_st[k][0:nt, :], ALU.add,
                   [RB[b0], RB[b0 + 1], R_x[k]], [R_x1[j]])
                if j + 2 <= NOWN:
                    c1b_X(j + 2)
                stc = st1[:, 4 * k:4 * k + 4]
                act(junk[0:nt, :], x1[0:nt, j, :], AF.Square, [R_x1[j]], [R_junk, R_st[k]], accum=stc[0:nt, 0:1])
                act(stc[0:nt, 1:2], stc[0:nt, 0:1], AF.Ln, [R_st[k]], [R_st[k]], scale=1.0 / DM, bias=EPS)
                act(stc[0:nt, 2:3], stc[0:nt, 1:2], AF.Exp, [R_st[k]], [R_st[k]], scale=-0.5)
                stt("dve", h_sb[k][0:nt, :], x1[0:nt, j, :], stc[0:nt, 2:3], gb[0:nt, :], ALU.mult, ALU.mult,
                    [R_x1[j], R_st[k], R_gb], [R_h[k]])

            def c1b_Tn(j):
                src, nt = blk_src(j)
                kk = j % 2
                pT = bankb(4 + kk)
                for ch in range(8):
                    tr(pT[:, ch * 128:ch * 128 + nt], h_sb[kk][0:nt, ch * 128:(ch + 1) * 128], [R_h[kk]], [RB[4 + kk]])
                cp("act", h2T[:, :, j * 128:j * 128 + nt], pT.rearrange("p (c t) -> p c t", t=128)[:, :, 0:nt], [RB[4 + kk]], [R_h2T[j]])

            c1b_X(0)
            c1b_X(1)
            c1b_Tm(0)
            c1b_Mo(0)
            c1b_Tm(1)
            for j in range(NOWN + 1):
                if j + 1 <= NOWN:
                    c1b_Mo(j + 1)
                c1b_Tn(j)
                if j + 2 <= NOWN:
                    c1b_Tm(j + 2)
            barrier()

            AR.seek(OFF_O if OFF_O + 46000 <= OFF_X1 else OFF_END)
            W1c = [AR.alloc([128, 8, 512], BF16) for _ in range(2)]
            W2c = [AR.alloc([128, 4, 1024], BF16) for _ in range(2)]
            hid = [AR.alloc([128, 4, 512], BF16) for _ in range(2)]
            r_t = [AR.alloc([128, 512], F32) for _ in range(2)]
            assert AR.off <= OFF_X1 or AR.off > OFF_END
            R_W1, R_W2 = [Res("W1a"), Res("W1b")], [Res("W2a"), Res("W2b")]
            R_hid, R_r = [Res("hid0"), Res("hid1")], [Res("r0"), Res("r1")]
            w1v = w_f1.rearrange("(c p) n -> p c n", p=128)
            w2v = w_f2.rearrange("(c p) n -> p c n", p=128)
            tgs = []
            for b0 in range(0, NOWN, 4):
                nb_ = min(4, NOWN - b0)
                tgs.append((b0 * 128, nb_ * 128, [(b0 + i, 128) for i in range(nb_)]))
            tgs.append((NOWN * 128, DS, [(NOWN, DS)]))
            NCH = 8
            rr = [0]

            def load_ffn(c):
                k = c % 2
                dma("pool", W1c[k], w1v[:, :, c * 512:(c + 1) * 512], [], [R_W1[k]])
                dma("pool", W2c[k], w2v[:, c * 4:(c + 1) * 4, :], [], [R_W2[k]])

            items = [(c, ti) for c in range(NCH) for ti in range(len(tgs))]

            def ffn1(idx):
                c, ti = items[idx]
                k = c % 2
                t0, ntok, blks = tgs[ti]
                hb_ = idx % 2
                rj = [R_h2T[b] for b, _ in blks]
                for sub in range(4):
                    bk = sub % 2
                    for ch in range(8):
                        mm(bank(bk)[:, 0:ntok], W1c[k][:, ch, sub * 128:(sub + 1) * 128], h2T[:, ch, t0:t0 + ntok], ch == 0, ch == 7,
                           [R_W1[k]] + rj, [RB[bk]])
                    rb_ = sub % 2
                    act(r_t[rb_][:, 0:ntok], bank(bk)[:, 0:ntok], AF.Relu, [RB[bk]], [R_r[rb_]])
                    tt("pool" if sub % 2 else "dve", hid[hb_][:, sub, 0:ntok], r_t[rb_][:, 0:ntok], r_t[rb_][:, 0:ntok], ALU.mult,
                       [R_r[rb_]], [R_hid[hb_]])

            def ffn2(idx):
                c, ti = items[idx]
                k = c % 2
                t0, ntok, blks = tgs[ti]
                hb_ = idx % 2
                for bi_, (b, nt) in enumerate(blks):
                    yb = 2 + 2 * (bi_ % 2)
                    for half in range(2):
                        for sub in range(4):
                            mm(bank(yb + half)[0:nt, :], hid[hb_][:, sub, bi_ * 128:bi_ * 128 + nt],
                               W2c[k][:, sub, half * 512:(half + 1) * 512], sub == 0, sub == 3, [R_hid[hb_], R_W2[k]], [RB[yb + half]])
                    tt("dve", x1[0:nt, b, :], x1[0:nt, b, :], PS[0:nt, yb * 512:(yb + 2) * 512], ALU.add,
                       [RB[yb], RB[yb + 1], R_x1[b]], [R_x1[b]])

            load_ffn(0)
            if NCH > 1:
                load_ffn(1)
            ffn1(0)
            for idx in range(len(items)):
                if idx + 1 < len(items):
                    ffn1(idx + 1)
                ffn2(idx)
                c, ti = items[idx]
                if ti == len(tgs) - 1 and c + 2 < NCH:
                    load_ffn(c + 2)
            for j in range(NOWN):
                dma("sp", y_own[j * 128:(j + 1) * 128, :], x1[:, j, :], [R_x1[j]], [])
            dma("sp", y_s[:, :], x1[0:DS, NOWN, :], [R_x1[NOWN]], [])

        S.finish()
    return nc


ROT_DIM = 16
ROPE_THETA = 500000.0


def _rope_tables(pos):
    inv = ROPE_THETA ** (-np.arange(0, ROT_DIM, 2, dtype=np.float32) / ROT_DIM)
    ang = pos.astype(np.float32)[:, None] * inv[None, :].astype(np.float32)
    return np.cos(ang).astype(np.float32), np.sin(ang).astype(np.float32)


_NC_CACHE = {}
STOP_AFTER = 99


def kernel(x_prompt, x_sample, cache_diff_k, cache_diff_v, cache_sb_k, cache_sb_v, meta_tokens,
           g_mix, w_in, q_norm_g, k_norm_g, lam_q1, lam_k1, lam_q2, lam_k2, sub_g,
           w_diff_out, w_sb_out, w_out, g_ffn, w_ff1, w_ff2):
    f = lambda a: np.ascontiguousarray(np.asarray(a, dtype=np.float32))
    x_prompt, x_sample, meta_tokens = f(x_prompt), f(x_sample), f(meta_tokens)
    B, SEQ, _ = x_prompt.shape
    NB = SEQ // 128
    PAST = cache_diff_k.shape[2]
    PB = PAST // 128
    NOWN = NB // 2
    NSLOT = NB + 2
    key = (NB, PB, STOP_AFTER)
    if key not in _NC_CACHE:
        _NC_CACHE[key] = build(NB, PB, STOP_AFTER)
    nc = _NC_CACHE[key]

    ii = np.arange(128)
    Dd = ((ii[:, None] // 64) <= (ii[None, :] // 64)).astype(np.float32)
    Dsm = (ii[:, None] < ii[None, :]).astype(np.float32)
    trineg = -(ii[:, None] >= ii[None, :]).astype(np.float32)
    ones = np.ones((128, 128), np.float32)
    shared = dict(
        w_in=f(w_in)[0], w_do=f(w_diff_out)[0], w_so=f(w_sb_out)[0], w_o=f(w_out)[0], w_f1=f(w_ff1)[0], w_f2=f(w_ff2)[0],
        g_mix=f(g_mix)[0], g_ffn=f(g_ffn)[0], qng=f(q_norm_g)[0], kng=f(k_norm_g)[0], subg=f(sub_g)[0],
        lamv=np.stack([f(lam_q1)[0], f(lam_k1)[0], f(lam_q2)[0], f(lam_k2)[0]]),
    )
    in_maps = []
    for c in range(8):
        b, p = c // 2, c % 2
        own = [2 * j + p for j in range(NOWN)]
        oth = [2 * j + 1 - p for j in range(NOWN)]
        order = own + oth
        xb = x_prompt[b].reshape(NB, 128, DM)
        x_slots = np.concatenate([xb[order].reshape(NB * 128, DM), meta_tokens], axis=0)
        cs = np.zeros((128, NSLOT, 8), np.float32)
        sn = np.zeros((128, NSLOT, 8), np.float32)
        for s, g in enumerate(order):
            cc, ss = _rope_tables(N_META + 128 * g + np.arange(128))
            cs[:, s], sn[:, s] = cc, ss
        cc, ss = _rope_tables(np.arange(N_META))
        cs[:N_META, NB], sn[:N_META, NB] = cc, ss
        cc, ss = _rope_tables(PAST + np.arange(DS))
        cs[:DS, NB + 1], sn[:DS, NB + 1] = cc, ss
        a_, b_ = (1.0, 0.0) if p == 0 else (0.0, 1.0)
        cmat = np.stack([trineg, Dd, Dsm, ones * float(p), -a_ * ones, -b_ * ones, -ones, 0 * ones]).astype(np.float32)
        m = dict(shared)
        m.update(x_slots=np.ascontiguousarray(x_slots), xs=x_sample[c],
                 c_dk=f(cache_diff_k)[0, c], c_dv=f(cache_diff_v)[0, c], c_sk=f(cache_sb_k)[0, c], c_sv=f(cache_sb_v)[0, c],
                 cs_t=cs, sn_t=sn, cmat=cmat)
        in_maps.append(m)
    res = run_bass_kernel_spmd(nc, in_maps, core_ids=list(range(8)))
    R = res.results
    T = SEQ + N_META
    y_prompt = np.zeros((B, SEQ, DM), np.float32)
    y_sample = np.zeros((8, DS, DM), np.float32)
    pk = np.zeros((1, B, T, NH, 128), np.float32)
    pv = np.zeros((1, B, T, NH, 128), np.float32)
    psk = np.zeros((1, B, T, NH, 64), np.float32)
    psv = np.zeros((1, B, T, NH, 64), np.float32)
    sk_ = np.zeros((1, 8, DS, NH, 128), np.float32)
    sv_ = np.zeros((1, 8, DS, NH, 128), np.float32)
    ssk = np.zeros((1, 8, DS, NH, 64), np.float32)
    ssv = np.zeros((1, 8, DS, NH, 64), np.float32)
    for c in range(8):
        b, p = c // 2, c % 2
        r = R[c]
        for j in range(NOWN):
            g = 2 * j + p
            y_prompt[b, 128 * g:128 * (g + 1)] = r["y_own"][128 * j:128 * (j + 1)]
            sl = slice(N_META + 128 * g, N_META + 128 * (g + 1))
            pk[0, b, sl] = r["kd_o"][128 * j:128 * (j + 1)]
            pv[0, b, sl] = r["vd_o"][128 * j:128 * (j + 1)]
            psk[0, b, sl] = r["sk_o"][128 * j:128 * (j + 1)]
            psv[0, b, sl] = r["sv_o"][128 * j:128 * (j + 1)]
        if p == 0:
            pk[0, b, :N_META] = r["kd_m"]
            pv[0, b, :N_META] = r["vd_m"]
            psk[0, b, :N_META] = r["sk_m"]
            psv[0, b, :N_META] = r["sv_m"]
        y_sample[c] = r["y_s"]
        sk_[0, c], sv_[0, c], ssk[0, c], ssv[0, c] = r["kd_s"], r["vd_s"], r["sk_s"], r["sv_s"]
    return (y_prompt, y_sample, pk, pv, psk, psv, sk_, sv_, ssk, ssv)
```

```python
from contextlib import ExitStack

import numpy as np
import concourse.bass as bass
import concourse.mybir as mybir
from concourse.bass_utils import run_bass_kernel_spmd

F32 = mybir.dt.float32
BF16 = mybir.dt.bfloat16
ALU = mybir.AluOpType
AF = mybir.ActivationFunctionType
AX = mybir.AxisListType


class Res:
    __slots__ = ("name", "w", "r", "excl", "wl")

    def __init__(self, name, excl=False):
        self.name = name
        self.wl = []
        self.excl = excl
        self.w = None
        self.r = []


class Sched:
    ENG = ("pe", "act", "dve", "pool", "sp")
    NDMA = {"sp": 12, "pool": 8, "act": 6}

    def __init__(self, nc, es):
        self.nc = nc
        self.es = es
        self.ops = {e: [] for e in self.ENG}
        self.cnt = {e: 0 for e in self.ENG}
        self.waited = {e: {} for e in self.ENG}
        self.sem = {}
        for e in self.ENG:
            self.sem[e] = es.enter_context(nc.semaphore("c_" + e))
        self.dsem = {}
        self.dcnt = {}
        self.drr = {}
        for q, n in self.NDMA.items():
            self.dsem[q] = []
            for i in range(n):
                nm = f"d_{q}{i}"
                self.sem[nm] = es.enter_context(nc.semaphore(nm))
                self.dsem[q].append(nm)
                self.dcnt[nm] = 0
            self.drr[q] = 0
        self.nwaits = 0
        self.pending = {}

    def sbuf(self, name, shape, dtype):
        return self.es.enter_context(self.nc.sbuf_tensor(name, shape, dtype))

    def psum(self, name, shape, dtype):
        return self.es.enter_context(self.nc.psum_tensor(name, shape, dtype))

    def _collect(self, eng, reads, writes, is_dma=False):
        deps = {}

        def add(tk, raw):
            if tk is None:
                return
            s, v = tk
            if s == eng and eng == "pe":
                return
            if deps.get(s, 0) < v:
                deps[s] = v

        for r in reads:
            add(r.w, True)
            for t in r.wl:
                add(t, True)
            if r.excl:
                for t in r.r:
                    add(t, False)
        for w in writes:
            if not (is_dma and w.w is not None and w.w[0].startswith("d_")):
                add(w.w, False)
                for t in w.wl:
                    add(t, False)
            for t in w.r:
                add(t, False)
        waits = []
        wd = self.waited[eng]
        for s, v in deps.items():
            if wd.get(s, 0) < v:
                wd[s] = v
                waits.append((s, v))
        self.nwaits += len(waits)
        return waits

    @staticmethod
    def _update(tk, reads, writes, is_dma=False):
        for r in reads:
            r.r.append(tk)
        for w in writes:
            if is_dma and w.w is not None and w.w[0].startswith("d_"):
                w.wl = w.wl + [w.w]
            else:
                w.wl = []
            w.w = tk
            w.r = []

    def op(self, eng, fn, reads=(), writes=()):
        waits = self.pending.pop(eng, []) + self._collect(eng, reads, writes)
        self.cnt[eng] += 1
        tk = (eng, self.cnt[eng])
        self.ops[eng].append((waits, fn, (eng, 1)))
        self._update(tk, reads, writes)
        return tk

    def dma(self, q, fn, reads=(), writes=()):
        names = self.dsem[q]
        nm = names[self.drr[q] % len(names)]
        self.drr[q] += 1
        waits = self.pending.pop(q, []) + self._collect(q, reads, writes, is_dma=True)
        prev = 16 * self.dcnt[nm]
        if prev and self.waited[q].get(nm, 0) < prev:
            self.waited[q][nm] = prev
            waits.append((nm, prev))
        self.dcnt[nm] += 1
        tk = (nm, 16 * self.dcnt[nm])
        self.ops[q].append((waits, fn, (nm, 16)))
        self._update(tk, reads, writes, is_dma=True)
        return tk

    def barrier(self):
        allt = [(e, c) for e, c in self.cnt.items() if c] + [(nm, 16 * c) for nm, c in self.dcnt.items() if c]
        for e in self.ENG:
            waits = []
            wd = self.waited[e]
            for sname, v in allt:
                if sname == e and e == "pe":
                    continue
                if wd.get(sname, 0) < v:
                    wd[sname] = v
                    waits.append((sname, v))
            self.pending[e] = self.pending.get(e, []) + waits

    def finish(self):
        final = [(nm, 16 * c) for nm, c in self.dcnt.items() if c]
        sem = self.sem

        def replay(name, e, tail=()):
            for waits, fn, inc in self.ops[name]:
                for s, v in waits:
                    e.wait_ge(sem[s], v)
                ins = fn(e)
                ins.then_inc(sem[inc[0]], inc[1])
            for s, v in tail:
                e.wait_ge(sem[s], v)

        with self.nc.Block() as block:
            @block.tensor
            def _(e):
                replay("pe", e)

            @block.scalar
            def _(e):
                replay("act", e)

            @block.vector
            def _(e):
                replay("dve", e)

            @block.gpsimd
            def _(e):
                replay("pool", e)

            @block.sync
            def _(e):
                replay("sp", e, tail=final)


DM = 1024
NH = 8
N_META = 16
DS = 32
EPS = 1e-6
LAM_INIT = 0.2
IN_COLS = 6656
GQ = 4


class Arena:
    def __init__(self, S, nbytes):
        self.n = nbytes
        self.t = S.sbuf("arena", [128, nbytes // 2], BF16)
        self.off = 0

    def seek(self, off):
        self.off = off

    def alloc(self, shape, dtype):
        esz = 4 if dtype == F32 else 2
        n = 1
        for d in shape[1:]:
            n *= d
        nb = (n * esz + 31) // 32 * 32
        assert self.off + nb <= self.n, f"arena overflow {self.off}+{nb}>{self.n}"
        v = self.t[:, self.off // 2:(self.off + nb) // 2]
        self.off += nb
        if dtype == F32:
            v = v.bitcast(F32)
        v = v[:, 0:n]
        if len(shape) == 3:
            v = v.rearrange("p (a b) -> p a b", b=shape[2])
        elif len(shape) == 4:
            v = v.rearrange("p (a b c) -> p a b c", b=shape[2], c=shape[3])
        return v


def build(NB, PB, stop_after=99):
    assert NB % 2 == 0 and PB == NB
    NOWN = NB // 2
    gq = min(GQ, NOWN)
    assert NOWN % gq == 0
    NG = NOWN // gq
    NT = NB * 128 + N_META
    NSLOT = NB + 2
    HALF = NOWN * 128
    SKW = HALF + 128
    KW = NB * 128 + 32

    nc = bass.Bass("TRN2", target_bir_lowering=False)

    def din(name, shape):
        return nc.dram_tensor(name, shape, F32, kind="ExternalInput").ap()

    def dout(name, shape):
        return nc.dram_tensor(name, shape, F32, kind="ExternalOutput").ap()

    x_slots = din("x_slots", [NT, DM])
    xs = din("xs", [DS, DM])
    c_dk = din("c_dk", [PB * 128, NH, 128])
    c_dv = din("c_dv", [PB * 128, NH, 128])
    c_sk = din("c_sk", [PB * 128, NH, 64])
    c_sv = din("c_sv", [PB * 128, NH, 64])
    w_in = din("w_in", [DM, IN_COLS])
    w_do = din("w_do", [DM, DM])
    w_so = din("w_so", [512, DM])
    w_o = din("w_o", [DM, DM])
    w_f1 = din("w_f1", [DM, 4 * DM])
    w_f2 = din("w_f2", [4 * DM, DM])
    g_mix = din("g_mix", [DM])
    g_ffn = din("g_ffn", [DM])
    qng = din("qng", [64])
    kng = din("kng", [64])
    lamv = din("lamv", [4, 64])
    subg = din("subg", [128])
    cs_t = din("cs_t", [128, NSLOT, 8])
    sn_t = din("sn_t", [128, NSLOT, 8])
    cmat = din("cmat", [8, 128, 128])

    y_own = dout("y_own", [NOWN * 128, DM])
    y_s = dout("y_s", [DS, DM])
    kd_o = dout("kd_o", [NOWN * 128, NH, 128])
    vd_o = dout("vd_o", [NOWN * 128, NH, 128])
    sk_o = dout("sk_o", [NOWN * 128, NH, 64])
    sv_o = dout("sv_o", [NOWN * 128, NH, 64])
    kd_m = dout("kd_m", [N_META, NH, 128])
    vd_m = dout("vd_m", [N_META, NH, 128])
    sk_m = dout("sk_m", [N_META, NH, 64])
    sv_m = dout("sv_m", [N_META, NH, 64])
    kd_s = dout("kd_s", [DS, NH, 128])
    vd_s = dout("vd_s", [DS, NH, 128])
    sk_s = dout("sk_s", [DS, NH, 64])
    sv_s = dout("sv_s", [DS, NH, 64])

    es = ExitStack()
    with es:
        S = Sched(nc, es)
        AR = Arena(S, 204800)
        PS = S.psum("PS", [128, 4096], F32)
        RB = [Res(f"bank{i}", excl=True) for i in range(8)]

        def bank(i):
            return PS[:, i * 512:(i + 1) * 512]

        def bankb(i):
            return PS[:, i * 512:(i + 1) * 512].bitcast(BF16)

        def mm(out, lhsT, rhs, start, stop, reads, writes, skip=False):
            S.op("pe", lambda e, o=out, l=lhsT, r=rhs, a=start, b=stop, k=skip:
                 e.matmul(o, lhsT=l, rhs=r, start=a, stop=b, skip_group_check=k), reads, writes)

        def tr(out, in_, reads, writes):
            n = in_.shape[0]
            S.op("pe", lambda e, o=out, i=in_, n=n: e.transpose(out=o, in_=i, identity=ident[0:n, 0:n]),
                 list(reads) + [R_const], writes)

        def act(out, in_, func, reads, writes, scale=1.0, bias=0.0, accum=None):
            S.op("act", lambda e, o=out, i=in_, f=func, s=scale, b=bias, a=accum:
                 e.activation(out=o, in_=i, func=f, scale=s, bias=b, accum_out=a), reads, writes)

        def tt(eng, out, in0, in1, op, reads, writes):
            S.op(eng, lambda e, o=out, a=in0, b=in1, p=op: e.tensor_tensor(out=o, in0=a, in1=b, op=p), reads, writes)

        def ts(eng, out, in0, s1, s2, op0, op1, reads, writes):
            if s2 is None:
                S.op(eng, lambda e, o=out, a=in0, x=s1, p=op0: e.tensor_scalar(out=o, in0=a, scalar1=x, scalar2=None, op0=p), reads, writes)
            else:
                S.op(eng, lambda e, o=out, a=in0, x=s1, y=s2, p=op0, q=op1:
                     e.tensor_scalar(out=o, in0=a, scalar1=x, scalar2=y, op0=p, op1=q), reads, writes)

        def stt(eng, out, in0, scalar, in1, op0, op1, reads, writes):
            S.op(eng, lambda e, o=out, a=in0, s=scalar, b=in1, p=op0, q=op1:
                 e.scalar_tensor_tensor(out=o, in0=a, scalar=s, in1=b, op0=p, op1=q), reads, writes)

        def cp(eng, out, in_, reads, writes):
            if eng == "act":
                act(out, in_, AF.Copy, reads, writes)
            else:
                S.op(eng, lambda e, o=out, i=in_: e.tensor_copy(out=o, in_=i), reads, writes)

        def dma(q, out, in_, reads, writes):
            S.dma(q, lambda e, o=out, i=in_: e.dma_start(out=o, in_=i), reads, writes)

        def barrier():
            S.barrier()

        R_const = Res("const")
        cm = AR.alloc([128, 8, 128], BF16)
        trineg, Dd, Ds, MO, MA, MB, onesneg, zero = [cm[:, i, :] for i in range(8)]
        ident = AR.alloc([128, 128], BF16)
        CS = AR.alloc([128, NSLOT, 8], F32)
        SN = AR.alloc([128, NSLOT, 8], F32)
        qg = AR.alloc([128, 64], F32)
        kg = AR.alloc([128, 64], F32)
        subg8 = AR.alloc([128, 128], F32)
        lv = AR.alloc([128, 4, 64], F32)
        lamt = AR.alloc([128, 16], F32)
        OFF_O = AR.off

        dma("pool", cm, cmat.rearrange("m p c -> p m c"), [], [R_const])
        dma("sp", CS, cs_t, [], [R_const])
        dma("sp", SN, sn_t, [], [R_const])
        dma("sp", qg, qng.partition_broadcast(128), [], [R_const])
        dma("sp", kg, kng.partition_broadcast(128), [], [R_const])
        dma("sp", subg8, subg.partition_broadcast(128), [], [R_const])
        for i in range(4):
            dma("sp", lv[:, i, :], lamv[i].partition_broadcast(128), [], [R_const])
        S.op("pool", lambda e: e.memset(ident, 1.0), [], [R_const])
        S.op("pool", lambda e: e.affine_select(out=ident, in_=ident, pattern=[[-1, 128]], compare_op=ALU.is_equal,
                                               fill=0.0, base=0, channel_multiplier=1), [R_const], [R_const])
        ts("dve", subg8, subg8, 1.0 - LAM_INIT, None, ALU.mult, None, [R_const], [R_const])
        tt("dve", lv[:, 0, :], lv[:, 0, :], lv[:, 1, :], ALU.mult, [R_const], [R_const])
        tt("dve", lv[:, 2, :], lv[:, 2, :], lv[:, 3, :], ALU.mult, [R_const], [R_const])
        S.op("dve", lambda e: e.tensor_reduce(out=lamt[:, 0:1], in_=lv[:, 0, :], axis=AX.X, op=ALU.add), [R_const], [R_const])
        S.op("dve", lambda e: e.tensor_reduce(out=lamt[:, 1:2], in_=lv[:, 2, :], axis=AX.X, op=ALU.add), [R_const], [R_const])
        act(lamt[:, 2:4], lamt[:, 0:2], AF.Exp, [R_const], [R_const])
        tt("dve", lamt[:, 5:6], lamt[:, 3:4], lamt[:, 2:3], ALU.subtract, [R_const], [R_const])
        ts("dve", lamt[:, 4:5], lamt[:, 5:6], -LAM_INIT, None, ALU.add, None, [R_const], [R_const])
        neg_lam = lamt[:, 4:5]

        Od = AR.alloc([128, NOWN + 1, 1024], BF16)
        Os = AR.alloc([128, NOWN + 1, 512], BF16)
        R_O = [Res(f"O{j}") for j in range(NOWN + 1)]
        OFF_P12 = AR.off
        hT = AR.alloc([128, 8, NT], BF16)
        hTs = AR.alloc([128, 8, DS], BF16)
        R_hT = [Res(f"hT{s}") for s in range(NB + 2)]
        OFF_HEAD = AR.off
        NALT = 2 * gq + 1
        KT = AR.alloc([128, KW + NALT * 128], BF16)
        Vaug = AR.alloc([128, NB + 1 + NALT, 130], BF16)
        SKT2 = AR.alloc([128, SKW + gq * 128 + 128], BF16)
        SVaug = AR.alloc([128, NB + 1 + NALT, 66], BF16)
        QT = AR.alloc([128, (NOWN + gq) * 128], BF16)
        SQT2 = AR.alloc([128, (NOWN + gq) * 128], BF16)
        Wh = AR.alloc([128, 8, 576], BF16)
        R_K = [Res(f"K{s}") for s in range(NB + 1)]
        R_Q = [Res(f"Q{s}") for s in range(NOWN)]
        R_Kalt = [Res(f"Ka{s}") for s in range(NALT)]
        R_Qalt = [Res(f"Qa{s}") for s in range(gq)]

        class Map:
            def __init__(self, par):
                self.par = par

            def alt(self, s):
                if not self.par:
                    return None
                if s < gq:
                    return s
                if NOWN <= s < NOWN + gq:
                    return gq + s - NOWN
                if s == NB:
                    return 2 * gq
                return None

            def kt(self, s):
                a = self.alt(s)
                return s * 128 if a is None else KW + a * 128

            def vs(self, s):
                a = self.alt(s)
                return s if a is None else NB + 1 + a

            def skl(self, s):
                a = self.alt(s)
                if a is None:
                    return (0, s * 128) if s < NOWN else (64, (s - NOWN) * 128)
                if a < gq:
                    return 0, SKW + a * 128
                return 64, SKW + (a - gq) * 128

            def qc(self, j):
                return (NOWN + j) * 128 if (self.par and j < gq) else j * 128

            def rk(self, s):
                a = self.alt(s)
                return R_K[s] if a is None else R_Kalt[a]

            def rq(self, j):
                return R_Qalt[j] if (self.par and j < gq) else R_Q[j]

        MAPS = [Map(0), Map(1)]
        R_Wh = Res("Wh")
        OFF_WORK = AR.off

        def slot_tok0(s):
            return s * 128

        def sk_loc(s):
            if s < NOWN:
                return 0, s * 128
            return 64, (s - NOWN) * 128

        S.op("pool", lambda e, t=Vaug[:, :, 128:130]: e.memset(t, 1.0), [], R_K)
        S.op("pool", lambda e, t=SVaug[:, :, 64:66]: e.memset(t, 1.0), [], R_K)

        AR.seek(OFF_WORK)
        x_st = [AR.alloc([128, DM], F32) for _ in range(2)]
        h_sb = [AR.alloc([128, DM], BF16) for _ in range(2)]
        junk = AR.alloc([128, DM], BF16)
        gb = AR.alloc([128, DM], F32)
        st1 = AR.alloc([128, 8], F32)
        R_x = [Res("x0"), Res("x1")]
        R_h = [Res("h0"), Res("h1")]
        R_junk = Res("junk")
        R_gb = Res("gb")
        R_st = [Res("st0"), Res("st1")]
        dma("sp", gb, g_mix.partition_broadcast(128), [], [R_gb])

        p1_cnt = [0]

        def norm_block(src_ap, nt, dstT, R_dst, gtile, R_g, x_keep=None, tbank=None):
            k = p1_cnt[0] % 2
            p1_cnt[0] += 1
            xb, hb = x_st[k], h_sb[k]
            if src_ap is not None:
                dma("sp", xb[0:nt, :], src_ap, [], [R_x[k]])
            else:
                xb = x_keep
            stc = st1[:, 4 * k:4 * k + 4]
            act(junk[0:nt, :], xb[0:nt, :], AF.Square, [R_x[k]], [R_junk, R_st[k]], accum=stc[0:nt, 0:1])
            act(stc[0:nt, 1:2], stc[0:nt, 0:1], AF.Ln, [R_st[k]], [R_st[k]], scale=1.0 / DM, bias=EPS)
            act(stc[0:nt, 2:3], stc[0:nt, 1:2], AF.Exp, [R_st[k]], [R_st[k]], scale=-0.5)
            stt("dve", hb[0:nt, :], xb[0:nt, :], stc[0:nt, 2:3], gtile[0:nt, :], ALU.mult, ALU.mult,
                [R_x[k], R_st[k], R_g], [R_h[k]])
            bi = 6 + k if tbank is None else tbank
            pT = bankb(bi)
            for ch in range(8):
                tr(pT[:, ch * 128:ch * 128 + nt], hb[0:nt, ch * 128:(ch + 1) * 128], [R_h[k]], [RB[bi]])
            src = pT.rearrange("p (c t) -> p c t", t=128)[:, :, 0:nt]
            return lambda: cp("act" if k == 0 else "dve", dstT, src, [RB[bi]], [R_dst])

        pend = None
        for s in range(NB + 1):
            nt = 128 if s < NB else N_META
            t0 = slot_tok0(s)
            nxt = norm_block(x_slots[t0:t0 + nt, :], nt, hT[:, :, t0:t0 + nt], R_hT[s], gb, R_gb)
            if pend is not None:
                pend()
            pend = nxt
        nxt = norm_block(xs[:, :], DS, hTs[:, :, :], R_hT[NB + 1], gb, R_gb)
        pend()
        nxt()
        barrier()

        AR.seek(OFF_WORK)
        sq_t = AR.alloc([128, 4, 2, 64], F32)
        kn = AR.alloc([128, 4, 2, 64], F32)
        qn = AR.alloc([128, 4, 2, 64], F32)
        knb = AR.alloc([128, 4, 128], BF16)
        qnb = knb
        rt = [AR.alloc([128, 4, 2, 8], F32) for _ in range(4)]
        vout = AR.alloc([128, 4, 128], F32)
        skout = AR.alloc([128, 4, 64], F32)
        svout = AR.alloc([128, 4, 64], F32)
        skb2 = AR.alloc([128, 4, 128], BF16)
        sqb2 = AR.alloc([128, 4, 128], BF16)
        stk = AR.alloc([128, 4, 4, 2], F32)
        stq = AR.alloc([128, 4, 4, 2], F32)
        Et = [[AR.alloc([128, 512], BF16) for _ in range(2)] for _ in range(2)]
        Lt = [[AR.alloc([128, 512], BF16) for _ in range(2)] for _ in range(2)]
        Wt = Et
        Oacc = AR.alloc([128, 4, 64], F32)
        tmpS = AR.alloc([128, 4, 64], F32)
        stick = AR.alloc([128, 8], F32)
        o4 = [AR.alloc([128, 128], F32) for _ in range(4)]
        sqj = AR.alloc([128, 128], BF16)
        fin = AR.alloc([128, 16], F32)
        fin2 = AR.alloc([128, 16], F32)
        R_o4 = [Res(f"o4_{i}") for i in range(4)]
        R_sqj, R_fin2 = Res("sqj"), Res("fin2")
        OFF_P3 = AR.off
        R_sq, R_kn, R_qn, R_knb, R_rt, R_rt2 = Res("sq"), Res("kn"), Res("qn"), Res("knb"), Res("rt"), Res("rt2")
        R_qnb = R_knb
        R_vo, R_sko, R_svo, R_skb, R_sqb, R_stk, R_stq = (Res("vo"), Res("sko"), Res("svo"), Res("skb"),
                                                         Res("sqb"), Res("stk"), Res("stq"))
        R_E = [[Res(f"E{c}{b}") for b in range(2)] for c in range(2)]
        R_L = [[Res(f"L{c}{b}") for b in range(2)] for c in range(2)]
        R_W = R_E
        R_Oacc, R_tmpS, R_stick, R_ot, R_tmpt, R_fin = Res("Oacc"), Res("tmpS"), Res("stick"), Res("ot"), Res("tmpt"), Res("fin")
        S.op("pool", lambda e: e.memset(skb2, 0.0), [], [R_skb])

        w_in_v = w_in.rearrange("(c p) n -> p c n", p=128)

        def load_wh(hd):
            segs = [(1024 + hd * 128, 0, 128), (2048 + hd * 128, 128, 128), (3584 + hd * 64, 256, 64),
                    (4096 + hd * 64, 320, 64), (hd * 128, 384, 128), (3072 + hd * 64, 512, 64)]
            for c0, d0, w in segs:
                dma("pool", Wh[:, :, d0:d0 + w], w_in_v[:, :, c0:c0 + w], [], [R_Wh])

        KV4 = PS[:, 0:2048].rearrange("p (s c) -> p s c", c=512)
        Q4 = PS[:, 2048:3072].rearrange("p (s c) -> p s c", c=256)

        def qk_norm_ops(ns, nt, dst, R_dst, stt_, R_stt, gvec, slot0):
            ops = []
            d = dst[0:nt, 0:ns]
            ops.append(lambda: tt("pool", sq_t[0:nt, 0:ns], d, d, ALU.mult, [R_dst], [R_sq]))
            ops.append(lambda: S.op("dve", lambda e, n=nt, m=ns, s=stt_: e.tensor_reduce(out=s[0:n, 0, 0:m, :], in_=sq_t[0:n, 0:m], axis=AX.X, op=ALU.add),
                                    [R_sq], [R_stt]))
            ops.append(lambda: act(stt_[0:nt, 1, 0:ns, :], stt_[0:nt, 0, 0:ns, :], AF.Ln, [R_stt], [R_stt], scale=1.0 / 64, bias=EPS))
            ops.append(lambda: act(stt_[0:nt, 2, 0:ns, :], stt_[0:nt, 1, 0:ns, :], AF.Exp, [R_stt], [R_stt], scale=-0.5))
            ops.append(lambda: tt("dve", d, d, stt_[0:nt, 2, 0:ns, :].unsqueeze(3).to_broadcast([nt, ns, 2, 64]), ALU.mult,
                                  [R_stt, R_dst], [R_dst]))
            ops.append(lambda: tt("dve", d, d, gvec[0:nt, :].unsqueeze(1).unsqueeze(1).to_broadcast([nt, ns, 2, 64]), ALU.mult,
                                  [R_dst, R_const], [R_dst]))
            cosb = CS[0:nt, slot0:slot0 + ns, :].unsqueeze(2).to_broadcast([nt, ns, 2, 8])
            sinb = SN[0:nt, slot0:slot0 + ns, :].unsqueeze(2).to_broadcast([nt, ns, 2, 8])
            x1 = dst[0:nt, 0:ns, :, 0:8]
            x2 = dst[0:nt, 0:ns, :, 8:16]
            r = [t[0:nt, 0:ns] for t in rt]
            ops.append(lambda: tt("pool", r[0], x1, cosb, ALU.mult, [R_dst, R_const], [R_rt]))
            ops.append(lambda: tt("dve", r[1], x2, sinb, ALU.mult, [R_dst, R_const], [R_rt2]))
            ops.append(lambda: tt("pool", r[2], x2, cosb, ALU.mult, [R_dst, R_const], [R_rt]))
            ops.append(lambda: tt("dve", r[3], x1, sinb, ALU.mult, [R_dst, R_const], [R_rt2]))
            ops.append(lambda: tt("pool", x1, r[0], r[1], ALU.subtract, [R_rt, R_rt2], [R_dst]))
            ops.append(lambda: tt("dve", x2, r[2], r[3], ALU.add, [R_rt, R_rt2], [R_dst]))
            return ops

        def projA(slots, nt, with_q, hsrc, R_hsrc, qslots, outs, rope_slot0, par=0, bgfg=False):
            ns = len(slots)
            s0 = slots[0]
            mp = MAPS[par]
            Rk = [mp.rk(s) for s in slots]
            v0 = mp.vs(s0)
            pb_, c0 = mp.skl(s0)
            two_d = outs is not None and len(outs["kd"].shape) == 2

            def o3(ap):
                return ap[:, 0, :] if two_d else ap
            ops = []
            if not bgfg:
                for si, s in enumerate(slots):
                    hs = hsrc(si)
                    for ch in range(8):
                        mm(bank(si)[0:nt, 0:384], hs[:, ch, :], Wh[:, ch, 0:384], ch == 0, ch == 7, [R_hsrc[si], R_Wh], [RB[si]])
                    if with_q:
                        qb = 4 + si // 2
                        qo = (si % 2) * 256
                        for ch in range(8):
                            mm(bank(qb)[0:nt, qo:qo + 192], hs[:, ch, :], Wh[:, ch, 384:576], ch == 0, ch == 7,
                               [R_hsrc[si], R_Wh], [RB[qb]])
                kvv = KV4[0:nt, 0:ns, :]
                allb = [RB[i] for i in range(ns)]
                act(kn[0:nt, 0:ns], kvv[:, :, 0:128].rearrange("p s (c d) -> p s c d", c=2), AF.Copy, allb, [R_kn])
                vv = kvv[:, :, 128:256]
                cp("dve", Vaug[0:nt, v0:v0 + ns, 0:128], vv, allb, Rk)
                skv = kvv[:, :, 256:320]
                cp("dve", skb2[0:nt, 0:ns, pb_:pb_ + 64], skv, allb, [R_skb])
                svv = kvv[:, :, 320:384]
                cp("dve", SVaug[0:nt, v0:v0 + ns, 0:64], svv, allb, Rk)
                if outs is not None:
                    cp("act", vout[0:nt, 0:ns, :], vv, allb, [R_vo])
                    dma("sp", outs["vd"], o3(vout[0:nt, 0:ns, :]), [R_vo], [])
                    cp("act", skout[0:nt, 0:ns, :], skv, allb, [R_sko])
                    dma("sp", outs["sk"], o3(skout[0:nt, 0:ns, :]), [R_sko], [])
                    cp("act", svout[0:nt, 0:ns, :], svv, allb, [R_svo])
                    dma("sp", outs["sv"], o3(svout[0:nt, 0:ns, :]), [R_svo], [])
                if with_q:
                    qv = Q4[0:nt, 0:ns, :]
                    qbanks = [RB[4], RB[5]]
                    cp("dve", qn[0:nt, 0:ns], qv[:, :, 0:128].rearrange("p s (c d) -> p s c d", c=2), qbanks, [R_qn])
                    act(sqb2[0:nt, 0:ns, 0:64], qv[:, :, 128:192], AF.Copy, qbanks, [R_sqb], scale=0.125)
                    act(sqb2[0:nt, 0:ns, 64:128], qv[:, :, 128:192], AF.Copy, qbanks, [R_sqb], scale=0.125)
            else:
                b7 = bank(7)
                r7 = [RB[7]]
                for si, s in enumerate(slots):
                    hs = hsrc(si)

                    def mm_kv(hs=hs, si=si):
                        for ch in range(8):
                            mm(b7[0:nt, 0:384], hs[:, ch, :], Wh[:, ch, 0:384], ch == 0, ch == 7, [R_hsrc[si], R_Wh], r7)
                    ops.append(mm_kv)
                    ops.append(lambda si=si: act(kn[0:nt, si], b7[0:nt, 0:128].rearrange("p (c d) -> p c d", c=2), AF.Copy, r7, [R_kn]))
                    ops.append(lambda si=si: cp("dve", Vaug[0:nt, v0 + si, 0:128], b7[0:nt, 128:256], r7, [Rk[si]]))
                    ops.append(lambda si=si: cp("dve", skb2[0:nt, si, pb_:pb_ + 64], b7[0:nt, 256:320], r7, [R_skb]))
                    ops.append(lambda si=si: cp("dve", SVaug[0:nt, v0 + si, 0:64], b7[0:nt, 320:384], r7, [Rk[si]]))
                    if outs is not None:
                        ops.append(lambda si=si: cp("act", vout[0:nt, si, :], b7[0:nt, 128:256], r7, [R_vo]))
                        ops.append(lambda si=si: cp("act", skout[0:nt, si, :], b7[0:nt, 256:320], r7, [R_sko]))
                        ops.append(lambda si=si: cp("act", svout[0:nt, si, :], b7[0:nt, 320:384], r7, [R_svo]))
                    if with_q:
                        def mm_q(hs=hs, si=si):
                            for ch in range(8):
                                mm(b7[0:nt, 0:192], hs[:, ch, :], Wh[:, ch, 384:576], ch == 0, ch == 7, [R_hsrc[si], R_Wh], r7)
                        ops.append(mm_q)
                        ops.append(lambda si=si: cp("dve", qn[0:nt, si], b7[0:nt, 0:128].rearrange("p (c d) -> p c d", c=2), r7, [R_qn]))
                        ops.append(lambda si=si: act(sqb2[0:nt, si, 0:64], b7[0:nt, 128:192], AF.Copy, r7, [R_sqb], scale=0.125))
                        ops.append(lambda si=si: act(sqb2[0:nt, si, 64:128], b7[0:nt, 128:192], AF.Copy, r7, [R_sqb], scale=0.125))
                if outs is not None:
                    def out_dmas():
                        dma("sp", outs["vd"], o3(vout[0:nt, 0:ns, :]), [R_vo], [])
                        dma("sp", outs["sk"], o3(skout[0:nt, 0:ns, :]), [R_sko], [])
                        dma("sp", outs["sv"], o3(svout[0:nt, 0:ns, :]), [R_svo], [])
                    ops.append(out_dmas)
            wdt = ns * 128 if nt == 128 else nt
            pT2 = bankb(7)

            def sk_tr():
                for si in range(ns):
                    tr(pT2[:, si * 128:si * 128 + nt], skb2[0:nt, si, :], [R_skb], [RB[7]])
            ops.append(sk_tr)
            ops.append(lambda: cp("dve", SKT2[pb_:pb_ + 64, c0:c0 + wdt], pT2[pb_:pb_ + 64, 0:wdt], [RB[7]], Rk))
            if with_q:
                Rq = [mp.rq(s) for s in qslots]
                q0 = mp.qc(qslots[0])

                def sq_tr():
                    for si in range(ns):
                        tr(pT2[:, si * 128:si * 128 + nt], sqb2[0:nt, si, :], [R_sqb], [RB[7]])
                ops.append(sq_tr)
                ops.append(lambda: cp("dve", SQT2[:, q0:q0 + wdt], pT2[:, 0:wdt], [RB[7]], Rq))
            ops += qk_norm_ops(ns, nt, kn, R_kn, stk, R_stk, kg, rope_slot0)
            knv = kn[0:nt, 0:ns].rearrange("p s c d -> p s (c d)")
            if outs is not None:
                ops.append(lambda: dma("sp", outs["kd"], o3(knv), [R_kn], []))
            ops.append(lambda: cp("pool", knb[0:nt, 0:ns, :], knv, [R_kn], [R_knb]))

            def k_tr():
                for si in range(ns):
                    tr(pT2[:, si * 128:si * 128 + nt], knb[0:nt, si, :], [R_knb], [RB[7]])
            ops.append(k_tr)
            t0 = mp.kt(s0)
            ops.append(lambda: cp("dve", KT[:, t0:t0 + wdt], pT2[:, 0:wdt], [RB[7]], Rk))
            if with_q:
                ops += qk_norm_ops(ns, nt, qn, R_qn, stq, R_stq, qg, rope_slot0)
                ops.append(lambda: ts("dve", qnb[0:nt, 0:ns, :], qn[0:nt, 0:ns].rearrange("p s c d -> p s (c d)"), 0.125, None,
                                      ALU.mult, None, [R_qn], [R_qnb]))

                def q_tr():
                    for si in range(ns):
                        tr(pT2[:, si * 128:si * 128 + nt], qnb[0:nt, si, :], [R_qnb], [RB[7]])
                ops.append(q_tr)
                ops.append(lambda: cp("dve", QT[:, q0:q0 + wdt], pT2[:, 0:wdt], [RB[7]], Rq))
            return ops

        def run_serial(ops):
            for o in ops:
                o()

        def weave(steps, ops, frac=0.75):
            n, m = len(steps), len(ops)
            k = 0
            for i, st in enumerate(steps):
                st()
                tgt = m if n == 0 else min(m, int((i + 1) * m / max(1.0, n * frac)) + 1)
                while k < tgt:
                    ops[k]()
                    k += 1
            while k < m:
                ops[k]()
                k += 1

        def attn_diff(q0, nqb, qw, blocks, out_blk0, hd, par=0):
            mp = MAPS[par]
            def acc(i, c):
                a = i * 2 + c
                return 4 + a // 3, (a % 3) * 129

            def finalize():
                diff_finalize(nqb, qw, acc, out_blk0, hd)
            started = set()
            nblk = len(blocks)
            lastblk = {}
            for bi_, kb in enumerate(blocks):
                for i in range(kb["fq"], nqb):
                    lastblk[i] = bi_

            def qk(t):
                kb = blocks[t]
                buf = t % 2
                nk, fq, s = kb["nk"], kb["fq"], kb["slot"]
                N = (nqb - fq) * qw
                c0 = q0 + fq * qw
                k0 = mp.kt(s)
                for c in range(2):
                    bk = buf * 2 + c
                    mm(bank(bk)[0:nk, 0:N], KT[c * 64:(c + 1) * 64, k0:k0 + nk], QT[c * 64:(c + 1) * 64, c0:c0 + N], True, True,
                       [mp.rk(s)] + kb["rq"], [RB[bk]])
                for c in range(2):
                    bk = buf * 2 + c
                    act(Et[c][buf][0:nk, 0:N], bank(bk)[0:nk, 0:N], AF.Exp, [RB[bk]], [R_E[c][buf]])
                    for qi, m in kb["masks"].items():
                        sl = Et[c][buf][0:nk, (qi - fq) * qw:(qi - fq + 1) * qw]
                        tt("dve", sl, sl, m[0:nk, 0:qw], ALU.mult, [R_E[c][buf], R_const], [R_E[c][buf]])

            def pv(t):
                kb = blocks[t]
                buf = t % 2
                nk, fq, s = kb["nk"], kb["fq"], kb["slot"]
                for i in range(fq, nqb):
                    for c in range(2):
                        bk, col = acc(i, c)
                        st = bk not in started
                        started.add(bk)
                        mm(bank(bk)[0:qw, col:col + 129], Et[c][buf][0:nk, (i - fq) * qw:(i - fq + 1) * qw],
                           Vaug[0:nk, mp.vs(s), 0:129], st, lastblk[i] == t, [R_E[c][buf], mp.rk(s)], [RB[bk]], skip=True)

            steps = []
            for t in range(nblk + 1):
                def st(t=t):
                    if t < nblk:
                        qk(t)
                    if t >= 1:
                        pv(t - 1)
                steps.append(st)
            steps.append(lambda: finalize())
            return steps

        def _unused():
            pass

        def diff_finalize(nqb, qw, acc, out_blk0, hd):
            f = fin[0:qw]
            f2 = fin2[0:qw]
            A = []
            for i in range(nqb):
                b0, c0_ = acc(i, 0)
                b1, c1_ = acc(i, 1)
                A.append((b0, bank(b0)[0:qw, c0_:c0_ + 129], b1, bank(b1)[0:qw, c1_:c1_ + 129]))
            for i, (b0, a0, b1, a1) in enumerate(A):
                S.op("dve", lambda e, o=f[:, i:i + 1], a=a0[:, 128:129]: e.reciprocal(out=o, in_=a), [RB[b0]], [R_fin])
                S.op("dve", lambda e, o=f[:, 4 + i:5 + i], a=a1[:, 128:129]: e.reciprocal(out=o, in_=a), [RB[b1]], [R_fin])
            ts("dve", f[:, 8:8 + nqb], f[:, 4:4 + nqb], neg_lam[0:qw, 0:1], None, ALU.mult, None, [R_fin, R_const], [R_fin])
            for i, (b0, a0, b1, a1) in enumerate(A):
                act(o4[i][0:qw, :], a1[:, 0:128], AF.Copy, [RB[b1], R_fin], [R_o4[i]], scale=f[:, 8 + i:9 + i])
                stt("dve", o4[i][0:qw, :], a0[:, 0:128], f[:, i:i + 1], o4[i][0:qw, :], ALU.mult, ALU.add,
                    [RB[b0], R_fin, R_o4[i]], [R_o4[i]])
                act(sqj[0:qw, :], o4[i][0:qw, :], AF.Square, [R_o4[i]], [R_sqj, R_fin2], accum=f2[:, i:i + 1])
            act(f2[:, 4:4 + nqb], f2[:, 0:nqb], AF.Ln, [R_fin2], [R_fin2], scale=1.0 / 128, bias=EPS)
            act(f2[:, 8:8 + nqb], f2[:, 4:4 + nqb], AF.Exp, [R_fin2], [R_fin2], scale=-0.5)
            for i in range(nqb):
                stt("dve", Od[0:qw, out_blk0 + i, hd * 128:(hd + 1) * 128], o4[i][0:qw, :], f2[:, 8 + i:9 + i], subg8[0:qw, :],
                    ALU.mult, ALU.mult, [R_o4[i], R_fin2, R_const], [R_O[out_blk0 + i]])

        def attn_sb(q0, nqb, qw, units, out_blk0, hd, ma, mb, par=0):
            mp = MAPS[par]
            nu = len(units)
            tt_ = None
            S.op("dve", lambda e: e.memset(stick[0:qw, 0:nqb], 1.0), [], [R_stick])
            S.op("dve", lambda e: e.memset(Oacc[0:qw, 0:nqb, :], 0.0), [], [R_Oacc])

            def geo(u):
                kb = units[u][0]
                fq = kb["fq"]
                return fq, (nqb - fq) * qw, q0 + fq * qw

            def s1(u):
                fq, N, c0 = geo(u)
                buf = u % 2
                for x, kb in enumerate(units[u]):
                    bk = buf * 2 + x
                    nk, s = kb["nk"], kb["slot"]
                    pb_, k0 = mp.skl(s)
                    mm(bank(bk)[0:nk, 0:N], SKT2[pb_:pb_ + 64, k0:k0 + nk], SQT2[pb_:pb_ + 64, c0:c0 + N], True, True,
                       [mp.rk(s)] + kb["rq"], [RB[bk]])
                for x, kb in enumerate(units[u]):
                    bk = buf * 2 + x
                    nk = kb["nk"]
                    eb = 6
                    act(bank(eb)[0:nk, 0:N], bank(bk)[0:nk, 0:N], AF.Exp, [RB[bk]], [RB[eb]])
                    act(Lt[x][buf][0:nk, 0:N], bank(eb)[0:nk, 0:N], AF.Ln, [RB[eb]], [R_L[x][buf]], bias=1.0)
                    for qi, m in kb["masks"].items():
                        sl = Lt[x][buf][0:nk, (qi - fq) * qw:(qi - fq + 1) * qw]
                        tt("dve", sl, sl, m[0:nk, 0:qw], ALU.mult, [R_L[x][buf], R_const], [R_L[x][buf]])

            def s2(u):
                fq, N, c0 = geo(u)
                buf = u % 2
                un = units[u]
                for x, kb in enumerate(un):
                    bk = buf * 2 + x
                    nk = kb["nk"]
                    pair = len(un) == 2
                    mm(bank(bk)[0:nk, 0:N], trineg[0:nk, 0:nk], Lt[x][buf][0:nk, 0:N], False, not pair,
                       [R_L[x][buf], R_const], [RB[bk]], skip=True)
                    if pair:
                        ok = un[1 - x]["nk"]
                        cm_ = ma if x == 0 else mb
                        mm(bank(bk)[0:nk, 0:N], cm_[0:ok, 0:nk], Lt[1 - x][buf][0:ok, 0:N], False, True,
                           [R_L[1 - x][buf], R_const], [RB[bk]], skip=True)
                for x, kb in enumerate(un):
                    bk = buf * 2 + x
                    nk = kb["nk"]
                    act(Wt[x][buf][0:nk, 0:N], bank(bk)[0:nk, 0:N], AF.Exp, [RB[bk]], [R_W[x][buf]])
                    for qi, m in kb["masks"].items():
                        sl = Wt[x][buf][0:nk, (qi - fq) * qw:(qi - fq + 1) * qw]
                        tt("dve", sl, sl, m[0:nk, 0:qw], ALU.mult, [R_W[x][buf], R_const], [R_W[x][buf]])

            def s3(u):
                fq, N, c0 = geo(u)
                buf = u % 2
                un = units[u]
                pbk = 4 + buf
                first = True
                for i in range(fq, nqb):
                    for x, kb in enumerate(un):
                        nk, s = kb["nk"], kb["slot"]
                        mm(bank(pbk)[0:qw, i * 65:i * 65 + 65], Wt[x][buf][0:nk, (i - fq) * qw:(i - fq + 1) * qw],
                           SVaug[0:nk, mp.vs(s), 0:65], first, x == len(un) - 1, [R_W[x][buf], mp.rk(s)], [RB[pbk]], skip=True)
                        first = False
                n = nqb - fq
                Pv = bank(pbk)[0:qw, 0:nqb * 65].rearrange("p (i c) -> p i c", c=65)
                tt("dve", tmpS[0:qw, fq:nqb, :], Pv[:, fq:nqb, 0:64],
                   stick[0:qw, fq:nqb].unsqueeze(2).to_broadcast([qw, n, 64]), ALU.mult, [RB[pbk], R_stick], [R_tmpS])
                tt("dve", Oacc[0:qw, fq:nqb, :], Oacc[0:qw, fq:nqb, :], tmpS[0:qw, fq:nqb, :], ALU.add, [R_tmpS, R_Oacc], [R_Oacc])
                if u < nu - 1:
                    ts("dve", fin[0:qw, 8:8 + n], Pv[:, fq:nqb, 64], -1.0, 1.0, ALU.mult, ALU.add, [RB[pbk]], [R_fin])
                    tt("dve", stick[0:qw, fq:nqb], stick[0:qw, fq:nqb], fin[0:qw, 8:8 + n], ALU.mult, [R_fin, R_stick], [R_stick])

            steps = []
            for t in range(nu + 2):
                def st(t=t):
                    if t < nu:
                        s1(t)
                    if 1 <= t <= nu:
                        s2(t - 1)
                    if t >= 2:
                        s3(t - 2)
                steps.append(st)
            steps.append(lambda: cp("pool", Os[0:qw, out_blk0:out_blk0 + nqb, hd * 64:(hd + 1) * 64], Oacc[0:qw, 0:nqb, :], [R_Oacc],
                                    [R_O[out_blk0 + i] for i in range(nqb)]))
            return steps

        kd_v = kd_o.rearrange("(s t) h d -> t s h d", t=128)
        vd_v = vd_o.rearrange("(s t) h d -> t s h d", t=128)
        sk_v = sk_o.rearrange("(s t) h d -> t s h d", t=128)
        sv_v = sv_o.rearrange("(s t) h d -> t s h d", t=128)

        def own_batch(g, hd):
            own = list(range(g * gq, (g + 1) * gq))
            return projA(own, 128, True, lambda si, o=own: hT[:, :, o[si] * 128:(o[si] + 1) * 128], [R_hT[s] for s in own], own,
                         dict(kd=kd_v[:, own[0]:own[-1] + 1, hd, :], vd=vd_v[:, own[0]:own[-1] + 1, hd, :],
                              sk=sk_v[:, own[0]:own[-1] + 1, hd, :], sv=sv_v[:, own[0]:own[-1] + 1, hd, :]), own[0], par=hd % 2)

        def oth_batch(g, hd):
            oth = [NOWN + j for j in range(g * gq, (g + 1) * gq)]
            return projA(oth, 128, False, lambda si, o=oth: hT[:, :, o[si] * 128:(o[si] + 1) * 128], [R_hT[s] for s in oth], None,
                         None, oth[0], par=hd % 2)

        def meta_batch(hd, bgfg=False):
            return projA([NB], N_META, False, lambda si: hT[:, :, NB * 128:NB * 128 + N_META], [R_hT[NB]], None,
                         dict(kd=kd_m[:, hd, :], vd=vd_m[:, hd, :], sk=sk_m[:, hd, :], sv=sv_m[:, hd, :]), NB, par=hd % 2,
                         bgfg=bgfg)

        def prompt_head(hd):
            par = hd % 2
            mp = MAPS[par]
            if hd == 0:
                run_serial(meta_batch(hd))
                run_serial(own_batch(0, hd))
                run_serial(oth_batch(0, hd))
            if NG == 1 and hd + 1 < NH:
                load_wh(hd + 1)
            for g in range(NG):
                j0 = g * gq
                rq = [mp.rq(j) for j in range(j0, j0 + gq)]
                q0 = mp.qc(j0)
                if g + 1 < NG:
                    bg = own_batch(g + 1, hd)
                elif hd + 1 < NH:
                    bg = own_batch(0, hd + 1)
                else:
                    bg = []
                if stop_after >= 2:
                    blocks = [dict(slot=NB, nk=N_META, fq=0, masks={}, rq=rq)]
                    for i in range(j0 + gq):
                        fq = max(i, j0) - j0
                        mo = {fq: Dd} if i >= j0 else {}
                        mo2 = {fq: MO} if i >= j0 else {}
                        blocks.append(dict(slot=i, nk=128, fq=fq, masks=mo, rq=rq))
                        blocks.append(dict(slot=NOWN + i, nk=128, fq=fq, masks=mo2, rq=rq))
                    weave(attn_diff(q0, gq, 128, blocks, j0, hd, par), bg)
                else:
                    run_serial(bg)
                if g + 1 < NG:
                    bg = oth_batch(g + 1, hd)
                elif hd + 1 < NH:
                    bg = oth_batch(0, hd + 1)
                    bg = bg + meta_batch(hd + 1, bgfg=True)
                else:
                    bg = []
                if g + 1 == NG - 1 and hd + 1 < NH:
                    load_wh(hd + 1)
                if stop_after >= 3:
                    units = []
                    for i in range(j0 + gq - 1, -1, -1):
                        fq = max(i, j0) - j0
                        mo = {fq: Ds} if i >= j0 else {}
                        mo2 = {fq: MO} if i >= j0 else {}
                        units.append([dict(slot=i, nk=128, fq=fq, masks=mo, rq=rq),
                                      dict(slot=NOWN + i, nk=128, fq=fq, masks=mo2, rq=rq)])
                    units.append([dict(slot=NB, nk=N_META, fq=0, masks={}, rq=rq)])
                    weave(attn_sb(q0, gq, 128, units, j0, hd, MA, MB, par), bg)
                else:
                    run_serial(bg)

        load_wh(0)
        for hd in range(NH if stop_after >= 1 else 0):
            prompt_head(hd)
        barrier()

        def sample_diff(hd):
            qw = DS
            batches = [list(range(b0, min(b0 + 16, PB))) for b0 in range(0, PB, 16)] + [[NB]]
            nbt = len(batches)
            started = set()

            def nkof(b):
                return DS if b == NB else 128

            def qk(t):
                bt = batches[t]
                buf = t % 2
                nk = nkof(bt[0])
                for c in range(2):
                    bk = buf * 2 + c
                    for k, b in enumerate(bt):
                        k0 = slot_tok0(b)
                        mm(bank(bk)[0:nk, k * qw:(k + 1) * qw], KT[c * 64:(c + 1) * 64, k0:k0 + nk], QT[c * 64:(c + 1) * 64, 0:qw],
                           k == 0, True, [R_K[b], R_Q[0]], [RB[bk]], skip=True)
                for c in range(2):
                    bk = buf * 2 + c
                    act(Et[c][buf][0:nk, 0:len(bt) * qw], bank(bk)[0:nk, 0:len(bt) * qw], AF.Exp, [RB[bk]], [R_E[c][buf]])

            def pv(t):
                bt = batches[t]
                buf = t % 2
                nk = nkof(bt[0])
                for k, b in enumerate(bt):
                    for c in range(2):
                        bk, col = 4, c * 129
                        st = bk not in started
                        started.add(bk)
                        mm(bank(bk)[0:qw, col:col + 129], Et[c][buf][0:nk, k * qw:(k + 1) * qw], Vaug[0:nk, b, 0:129],
                           st, (t == nbt - 1 and k == len(bt) - 1), [R_E[c][buf], R_K[b]], [RB[bk]], skip=True)

            for t in range(nbt + 1):
                if t < nbt:
                    qk(t)
                if t >= 1:
                    pv(t - 1)
            diff_finalize(1, qw, lambda i, c: (4, c * 129), NOWN, hd)

        def sample_sb(hd):
            qw = DS
            pairs = [(2 * i, 2 * i + 1) for i in range(PB // 2 - 1, -1, -1)]
            batches = [[(NB,)]] + [pairs[i:i + 7] for i in range(0, len(pairs), 7)]
            nbt = len(batches)
            S.op("dve", lambda e: e.memset(stick[0:qw, 0:1], 1.0), [], [R_stick])
            S.op("dve", lambda e: e.memset(Oacc[0:qw, 0:1, :], 0.0), [], [R_Oacc])

            def nkof(b):
                return DS if b == NB else 128

            def s1(t):
                bt = batches[t]
                buf = t % 2
                npair = len(bt)
                N = npair * qw
                nx = len(bt[0])
                nk = nkof(bt[0][0])
                for x in range(nx):
                    bk = buf * 2 + x
                    for k, un in enumerate(bt):
                        b = un[x]
                        pb_, k0 = sk_loc(b)
                        mm(bank(bk)[0:nk, k * qw:(k + 1) * qw], SKT2[pb_:pb_ + 64, k0:k0 + nk], SQT2[pb_:pb_ + 64, 0:qw], k == 0, True,
                           [R_K[b], R_Q[0]], [RB[bk]], skip=True)
                for x in range(nx):
                    bk = buf * 2 + x
                    act(bank(6)[0:nk, 0:N], bank(bk)[0:nk, 0:N], AF.Exp, [RB[bk]], [RB[6]])
                    act(Lt[x][buf][0:nk, 0:N], bank(6)[0:nk, 0:N], AF.Ln, [RB[6]], [R_L[x][buf]], bias=1.0)
                    if nx == 1:
                        sl = Lt[x][buf][0:nk, 0:qw]
                        tt("dve", sl, sl, Ds[0:nk, 0:qw], ALU.mult, [R_L[x][buf], R_const], [R_L[x][buf]])

            def s2(t):
                bt = batches[t]
                buf = t % 2
                N = len(bt) * qw
                nx = len(bt[0])
                nk = nkof(bt[0][0])
                for x in range(nx):
                    bk = buf * 2 + x
                    mm(bank(bk)[0:nk, 0:N], trineg[0:nk, 0:nk], Lt[x][buf][0:nk, 0:N], False, nx == 1, [R_L[x][buf], R_const], [RB[bk]], skip=True)
                    if nx == 2:
                        cm_ = onesneg if x == 0 else zero
                        mm(bank(bk)[0:nk, 0:N], cm_[0:nk, 0:nk], Lt[1 - x][buf][0:nk, 0:N], False, True,
                           [R_L[1 - x][buf], R_const], [RB[bk]], skip=True)
                for x in range(nx):
                    bk = buf * 2 + x
                    act(Wt[x][buf][0:nk, 0:N], bank(bk)[0:nk, 0:N], AF.Exp, [RB[bk]], [R_W[x][buf]])
                    if nx == 1:
                        sl = Wt[x][buf][0:nk, 0:qw]
                        tt("dve", sl, sl, Ds[0:nk, 0:qw], ALU.mult, [R_W[x][buf], R_const], [R_W[x][buf]])

            def s3(t):
                bt = batches[t]
                buf = t % 2
                nx = len(bt[0])
                nk = nkof(bt[0][0])
                pbk = 4 + buf
                first = True
                for k, un in enumerate(bt):
                    for x in range(nx):
                        b = un[x]
                        mm(bank(pbk)[0:qw, k * 65:k * 65 + 65], Wt[x][buf][0:nk, k * qw:(k + 1) * qw], SVaug[0:nk, b, 0:65],
                           first, x == nx - 1, [R_W[x][buf], R_K[b]], [RB[pbk]], skip=True)
                        first = False
                for k, un in enumerate(bt):
                    Pk = bank(pbk)[0:qw, k * 65:k * 65 + 65]
                    stt("dve", Oacc[0:qw, 0, :], Pk[:, 0:64], stick[0:qw, 0:1], Oacc[0:qw, 0, :], ALU.mult, ALU.add,
                        [RB[pbk], R_stick, R_Oacc], [R_Oacc])
                    if not (t == nbt - 1 and k == len(bt) - 1):
                        ts("dve", fin[0:qw, 8:9], Pk[:, 64:65], -1.0, 1.0, ALU.mult, ALU.add, [RB[pbk]], [R_fin])
                        tt("dve", stick[0:qw, 0:1], stick[0:qw, 0:1], fin[0:qw, 8:9], ALU.mult, [R_fin, R_stick], [R_stick])

            for t in range(nbt + 2):
                if t < nbt:
                    s1(t)
                if 1 <= t <= nbt:
                    s2(t - 1)
                if t >= 2:
                    s3(t - 2)
            cp("pool", Os[0:qw, NOWN:NOWN + 1, hd * 64:(hd + 1) * 64], Oacc[0:qw, 0:1, :], [R_Oacc], [R_O[NOWN]])

        if stop_after >= 4:
            AR.seek(OFF_P12)
            Kc2 = [AR.alloc([128, PB, 128], BF16) for _ in range(2)]
            SKc2 = [AR.alloc([128, PB, 128], BF16) for _ in range(2)]
            KT_B = AR.alloc([128, KW], BF16)
            Vaug_B = AR.alloc([128, NB + 1, 130], BF16)
            SKT2_B = AR.alloc([128, SKW], BF16)
            SVaug_B = AR.alloc([128, NB + 1, 66], BF16)
            assert AR.off <= OFF_P12 + 8 * NT * 2
            R_K_B = [Res(f"KB{s}") for s in range(NB + 1)]
            ksets = [(KT, Vaug, SKT2, SVaug, R_K), (KT_B, Vaug_B, SKT2_B, SVaug_B, R_K_B)]
            R_Kc2, R_SKc2 = [Res("Kc0"), Res("Kc1")], [Res("SKc0"), Res("SKc1")]
            for k in range(2):
                S.op("pool", lambda e, t=SKc2[k]: e.memset(t, 0.0), [], [R_SKc2[k]])
            S.op("pool", lambda e, t=Vaug_B[:, :, 128:130]: e.memset(t, 1.0), [], R_K_B)
            S.op("pool", lambda e, t=SVaug_B[:, :, 64:66]: e.memset(t, 1.0), [], R_K_B)
            ck_v = c_dk.rearrange("(b t) h d -> t b h d", t=128)
            cv_v = c_dv.rearrange("(b t) h d -> t b h d", t=128)
            csk_v = c_sk.rearrange("(b t) h d -> t b h d", t=128)
            csv_v = c_sv.rearrange("(b t) h d -> t b h d", t=128)
            HB = PB // 2

            def issue_loads(hd):
                k = hd % 2
                kt_, va_, skt_, sva_, rk_ = ksets[k]
                dma("pool", Kc2[k], ck_v[:, :, hd, :], [], [R_Kc2[k]])
                dma("pool", va_[:, 0:PB, 0:128], cv_v[:, :, hd, :], [], rk_[0:PB])
                dma("pool", SKc2[k][:, 0:HB, 0:64], csk_v[:, 0:HB, hd, :], [], [R_SKc2[k]])
                dma("pool", SKc2[k][:, HB:PB, 64:128], csk_v[:, HB:PB, hd, :], [], [R_SKc2[k]])
                dma("pool", sva_[:, 0:PB, 0:64], csv_v[:, :, hd, :], [], rk_[0:PB])

            load_wh(0)
            issue_loads(0)
            for hd in range(NH):
                if hd + 1 < NH:
                    issue_loads(hd + 1)
                k2 = hd % 2
                KT, Vaug, SKT2, SVaug, R_K = ksets[k2]
                Kc, SKc, R_Kc, R_SKc = Kc2[k2], SKc2[k2], R_Kc2[k2], R_SKc2[k2]
                bgs = projA([NB], DS, True, lambda si: hTs[:, :, :], [R_hT[NB + 1]], [0],
                            dict(kd=kd_s[:, hd, :], vd=vd_s[:, hd, :], sk=sk_s[:, hd, :], sv=sv_s[:, hd, :]), NB + 1)
                if hd + 1 < NH:
                    load_wh(hd + 1)
                tsteps = []
                for b0 in range(0, PB, 4):
                    def st_k(b0=b0, KT=KT, Kc=Kc, R_Kc=R_Kc, R_K=R_K):
                        bi = 2 + (b0 // 4) % 2
                        pT = bankb(bi)
                        for k in range(4):
                            tr(pT[:, k * 128:(k + 1) * 128], Kc[:, b0 + k, :], [R_Kc], [RB[bi]])
                        cp("act", KT[:, b0 * 128:(b0 + 4) * 128], pT[:, 0:512], [RB[bi]], R_K[b0:b0 + 4])
                    tsteps.append(st_k)
                for b0 in range(0, PB, 4):
                    def st_s(b0=b0, SKT2=SKT2, SKc=SKc, R_SKc=R_SKc, R_K=R_K):
                        bi = 2 + (b0 // 4) % 2
                        pT = bankb(bi)
                        for k in range(4):
                            tr(pT[:, k * 128:(k + 1) * 128], SKc[:, b0 + k, :], [R_SKc], [RB[bi]])
                        pb_, c0 = sk_loc(b0)
                        cp("dve", SKT2[pb_:pb_ + 64, c0:c0 + 512], pT[pb_:pb_ + 64, 0:512], [RB[bi]], R_K[b0:b0 + 4])
                    tsteps.append(st_s)
                weave(tsteps, bgs)
                sample_diff(hd)
                sample_sb(hd)
            barrier()

        if stop_after >= 5:
            OFF_MG = OFF_P12
            AR.seek(OFF_MG)
            mg = AR.alloc([128, NOWN + 1, 1024], BF16)
            R_mg = [Res(f"mg{j}") for j in range(NOWN + 1)]
            OFF_X1 = AR.off
            x_st = [AR.alloc([128, DM], F32) for _ in range(2)]
            h_sb = [AR.alloc([128, DM], BF16) for _ in range(2)]
            junk = AR.alloc([128, DM], BF16)
            gb = AR.alloc([128, DM], F32)
            st1 = AR.alloc([128, 8], F32)
            hTb = [AR.alloc([128, 8, 128], BF16) for _ in range(2)]
            odT = [AR.alloc([128, 8, 128], BF16) for _ in range(2)]
            osT = [AR.alloc([128, 4, 128], BF16) for _ in range(2)]
            egd = [AR.alloc([128, 1024], F32) for _ in range(2)]
            egs = [AR.alloc([128, 1024], F32) for _ in range(2)]
            Wd = AR.alloc([128, 8, 1024], BF16)
            Ws = AR.alloc([128, 4, 1024], BF16)
            Wg = AR.alloc([128, 8, 2048], BF16)
            R_x = [Res("x0"), Res("x1")]
            R_h = [Res("h0"), Res("h1")]
            R_junk, R_gb = Res("junk"), Res("gb")
            R_st = [Res("st0"), Res("st1")]
            R_hTb, R_odT, R_osT, R_egd, R_egs = [[Res(f"{n}{k}") for k in range(2)] for n in ("hTb", "odT", "osT", "egd", "egs")]
            R_Wd, R_Ws, R_Wg = Res("Wd"), Res("Ws"), Res("Wg")
            p1_cnt[0] = 0
            dma("sp", gb, g_mix.partition_broadcast(128), [], [R_gb])
            dma("pool", Wg, w_in_v[:, :, 4608:6656], [], [R_Wg])
            dma("pool", Wd, w_do.rearrange("(c p) n -> p c n", p=128), [], [R_Wd])
            dma("pool", Ws, w_so.rearrange("(c p) n -> p c n", p=128), [], [R_Ws])

            def blk_src(j):
                if j < NOWN:
                    return x_slots[j * 128:(j + 1) * 128, :], 128
                return xs[:, :], DS

            def c1a_A1(j):
                src, nt = blk_src(j)
                k = j % 2
                xb, hb = x_st[k], h_sb[k]
                dma("sp", xb[0:nt, :], src, [], [R_x[k]])
                stc = st1[:, 4 * k:4 * k + 4]
                act(junk[0:nt, :], xb[0:nt, :], AF.Square, [R_x[k]], [R_junk, R_st[k]], accum=stc[0:nt, 0:1])
                act(stc[0:nt, 1:2], stc[0:nt, 0:1], AF.Ln, [R_st[k]], [R_st[k]], scale=1.0 / DM, bias=EPS)
                act(stc[0:nt, 2:3], stc[0:nt, 1:2], AF.Exp, [R_st[k]], [R_st[k]], scale=-0.5)
                stt("dve", hb[0:nt, :], xb[0:nt, :], stc[0:nt, 2:3], gb[0:nt, :], ALU.mult, ALU.mult,
                    [R_x[k], R_st[k], R_gb], [R_h[k]])

            def c1a_Th(j):
                src, nt = blk_src(j)
                k = j % 2
                pT = bankb(7)
                for ch in range(8):
                    tr(pT[:, ch * 128:ch * 128 + nt], h_sb[k][0:nt, ch * 128:(ch + 1) * 128], [R_h[k]], [RB[7]])
                cp("dve", hTb[k][:, :, 0:nt], pT.rearrange("p (c t) -> p c t", t=128)[:, :, 0:nt], [RB[7]], [R_hTb[k]])

            def c1a_G(j):
                src, nt = blk_src(j)
                k = j % 2
                for gi, (b0, dst, R_dst) in enumerate([(0, egd[k], R_egd[k]), (2, egs[k], R_egs[k])]):
                    for half in range(2):
                        for ch in range(8):
                            mm(bank(b0 + half)[0:nt, :], hTb[k][:, ch, 0:nt], Wg[:, ch, gi * 1024 + half * 512:gi * 1024 + (half + 1) * 512],
                               ch == 0, ch == 7, [R_hTb[k], R_Wg], [RB[b0 + half]])
                    gv = PS[0:nt, b0 * 512:(b0 + 2) * 512]
                    act(dst[0:nt, :], gv, AF.Exp, [RB[b0], RB[b0 + 1]], [R_dst], scale=-1.0)
                    act(dst[0:nt, :], dst[0:nt, :], AF.Ln, [R_dst], [R_dst], bias=1.0)
                    act(dst[0:nt, :], dst[0:nt, :], AF.Exp, [R_dst], [R_dst], scale=-1.0)

            def c1a_To(j):
                src, nt = blk_src(j)
                k = j % 2
                pT = bankb(7)
                for ch in range(8):
                    tr(pT[:, ch * 128:ch * 128 + nt], Od[0:nt, j, ch * 128:(ch + 1) * 128], [R_O[j]], [RB[7]])
                cp("act", odT[k][:, :, 0:nt], pT.rearrange("p (c t) -> p c t", t=128)[:, :, 0:nt], [RB[7]], [R_odT[k]])
                for ch in range(4):
                    tr(pT[:, ch * 128:ch * 128 + nt], Os[0:nt, j, ch * 128:(ch + 1) * 128], [R_O[j]], [RB[7]])
                cp("dve", osT[k][:, :, 0:nt], pT[:, 0:512].rearrange("p (c t) -> p c t", t=128)[:, :, 0:nt], [RB[7]], [R_osT[k]])

            def c1a_M(j):
                src, nt = blk_src(j)
                k = j % 2
                for half in range(2):
                    for ch in range(8):
                        mm(bank(4 + half)[0:nt, :], odT[k][:, ch, 0:nt], Wd[:, ch, half * 512:(half + 1) * 512], ch == 0, ch == 7,
                           [R_odT[k], R_Wd], [RB[4 + half]])
                tt("dve", egd[k][0:nt, :], PS[0:nt, 2048:3072], egd[k][0:nt, :], ALU.mult, [RB[4], RB[5], R_egd[k]], [R_egd[k]])
                for half in range(2):
                    for ch in range(4):
                        mm(bank(6)[0:nt, :], osT[k][:, ch, 0:nt], Ws[:, ch, half * 512:(half + 1) * 512], ch == 0, ch == 3,
                           [R_osT[k], R_Ws], [RB[6]])
                    hs_ = slice(half * 512, (half + 1) * 512)
                    tt("dve", egs[k][0:nt, hs_], bank(6)[0:nt, :], egs[k][0:nt, hs_], ALU.mult, [RB[6], R_egs[k]], [R_egs[k]])
                tt("pool", mg[0:nt, j, :], egd[k][0:nt, :], egs[k][0:nt, :], ALU.add, [R_egd[k], R_egs[k]], [R_mg[j]])

            c1a_A1(0)
            c1a_A1(1)
            c1a_Th(0)
            c1a_To(0)
            for j in range(NOWN + 1):
                c1a_G(j)
                if j + 1 <= NOWN:
                    c1a_Th(j + 1)
                c1a_M(j)
                if j + 1 <= NOWN:
                    c1a_To(j + 1)
                if j + 2 <= NOWN:
                    c1a_A1(j + 2)
            barrier()

            AR.seek(OFF_X1)
            x1 = AR.alloc([128, NOWN + 1, 1024], F32)
            h2T = AR.alloc([128, 8, NOWN * 128 + DS], BF16)
            OFF_END = AR.off
            AR.seek(OFF_O if OFF_O + 38000 <= OFF_MG else OFF_END)
            x_st = [AR.alloc([128, DM], F32) for _ in range(2)]
            h_sb = [AR.alloc([128, DM], BF16) for _ in range(2)]
            junk = AR.alloc([128, DM], BF16)
            gb = AR.alloc([128, DM], F32)
            st1 = AR.alloc([128, 8], F32)
            mT = [AR.alloc([128, 8, 128], BF16) for _ in range(2)]
            Wo = AR.alloc([128, 8, 1024], BF16)
            assert AR.off <= OFF_MG or AR.off > OFF_END
            R_x1 = [Res(f"x1_{j}") for j in range(NOWN + 1)]
            R_h2T = [Res(f"h2T{j}") for j in range(NOWN + 1)]
            R_x = [Res("x0"), Res("x1")]
            R_h = [Res("h0"), Res("h1")]
            R_junk, R_gb = Res("junk"), Res("gb")
            R_st = [Res("st0"), Res("st1")]
            R_mT, R_Wo = [Res("mT0"), Res("mT1")], Res("Wo")
            p1_cnt[0] = 0
            dma("sp", gb, g_ffn.partition_broadcast(128), [], [R_gb])
            dma("pool", Wo, w_o.rearrange("(c p) n -> p c n", p=128), [], [R_Wo])
            def c1b_Tm(j):
                src, nt = blk_src(j)
                k = j % 2
                pT = bankb(6 + k)
                for ch in range(8):
                    tr(pT[:, ch * 128:ch * 128 + nt], mg[0:nt, j, ch * 128:(ch + 1) * 128], [R_mg[j]], [RB[6 + k]])
                cp("act", mT[k][:, :, 0:nt], pT.rearrange("p (c t) -> p c t", t=128)[:, :, 0:nt], [RB[6 + k]], [R_mT[k]])

            def c1b_X(j):
                src, nt = blk_src(j)
                dma("sp", x_st[j % 2][0:nt, :], src, [], [R_x[j % 2]])

            def c1b_Mo(j):
                src, nt = blk_src(j)
                k = j % 2
                b0 = 2 * k
                for half in range(2):
                    for ch in range(8):
                        mm(bank(b0 + half)[0:nt, :], mT[k][:, ch, 0:nt], Wo[:, ch, half * 512:(half + 1) * 512], ch == 0, ch == 7,
                           [R_mT[k], R_Wo], [RB[b0 + half]])
                tt("dve", x1[0:nt, j, :], PS[0:nt, b0 * 512:(b0 + 2) * 512], x_st[k][0:nt, :], ALU.add,
                   [RB[b0], RB[b0 + 1], R_x[k]], [R_x1[j]])
                if j + 2 <= NOWN:
                    c1b_X(j + 2)
                stc = st1[:, 4 * k:4 * k + 4]
                act(junk[0:nt, :], x1[0:nt, j, :], AF.Square, [R_x1[j]], [R_junk, R_st[k]], accum=stc[0:nt, 0:1])
                act(stc[0:nt, 1:2], stc[0:nt, 0:1], AF.Ln, [R_st[k]], [R_st[k]], scale=1.0 / DM, bias=EPS)
                act(stc[0:nt, 2:3], stc[0:nt, 1:2], AF.Exp, [R_st[k]], [R_st[k]], scale=-0.5)
                stt("dve", h_sb[k][0:nt, :], x1[0:nt, j, :], stc[0:nt, 2:3], gb[0:nt, :], ALU.mult, ALU.mult,
                    [R_x1[j], R_st[k], R_gb], [R_h[k]])

            def c1b_Tn(j):
                src, nt = blk_src(j)
                kk = j % 2
                pT = bankb(4 + kk)
                for ch in range(8):
                    tr(pT[:, ch * 128:ch * 128 + nt], h_sb[kk][0:nt, ch * 128:(ch + 1) * 128], [R_h[kk]], [RB[4 + kk]])
                cp("act", h2T[:, :, j * 128:j * 128 + nt], pT.rearrange("p (c t) -> p c t", t=128)[:, :, 0:nt], [RB[4 + kk]], [R_h2T[j]])

            c1b_X(0)
            c1b_X(1)
            c1b_Tm(0)
            c1b_Mo(0)
            c1b_Tm(1)
            for j in range(NOWN + 1):
                if j + 2 <= NOWN:
                    c1b_Tm(j + 2)
                if j + 1 <= NOWN:
                    c1b_Mo(j + 1)
                c1b_Tn(j)
            barrier()

            AR.seek(OFF_O if OFF_O + 46000 <= OFF_X1 else OFF_END)
            W1c = [AR.alloc([128, 8, 512], BF16) for _ in range(2)]
            W2c = [AR.alloc([128, 4, 1024], BF16) for _ in range(2)]
            hid = [AR.alloc([128, 4, 512], BF16) for _ in range(2)]
            r_t = [AR.alloc([128, 512], F32) for _ in range(2)]
            assert AR.off <= OFF_X1 or AR.off > OFF_END
            R_W1, R_W2 = [Res("W1a"), Res("W1b")], [Res("W2a"), Res("W2b")]
            R_hid, R_r = [Res("hid0"), Res("hid1")], [Res("r0"), Res("r1")]
            w1v = w_f1.rearrange("(c p) n -> p c n", p=128)
            w2v = w_f2.rearrange("(c p) n -> p c n", p=128)
            tgs = []
            for b0 in range(0, NOWN, 4):
                nb_ = min(4, NOWN - b0)
                tgs.append((b0 * 128, nb_ * 128, [(b0 + i, 128) for i in range(nb_)]))
            tgs.append((NOWN * 128, DS, [(NOWN, DS)]))
            NCH = 8
            rr = [0]

            def load_ffn(c):
                k = c % 2
                dma("pool", W1c[k], w1v[:, :, c * 512:(c + 1) * 512], [], [R_W1[k]])
                dma("pool", W2c[k], w2v[:, c * 4:(c + 1) * 4, :], [], [R_W2[k]])

            items = [(c, ti) for c in range(NCH) for ti in range(len(tgs))]

            def ffn1(idx):
                c, ti = items[idx]
                k = c % 2
                t0, ntok, blks = tgs[ti]
                hb_ = idx % 2
                rj = [R_h2T[b] for b, _ in blks]
                for sub in range(4):
                    bk = sub % 2
                    for ch in range(8):
                        mm(bank(bk)[:, 0:ntok], W1c[k][:, ch, sub * 128:(sub + 1) * 128], h2T[:, ch, t0:t0 + ntok], ch == 0, ch == 7,
                           [R_W1[k]] + rj, [RB[bk]])
                    rb_ = sub % 2
                    act(r_t[rb_][:, 0:ntok], bank(bk)[:, 0:ntok], AF.Relu, [RB[bk]], [R_r[rb_]])
                    tt("pool" if sub % 2 else "dve", hid[hb_][:, sub, 0:ntok], r_t[rb_][:, 0:ntok], r_t[rb_][:, 0:ntok], ALU.mult,
                       [R_r[rb_]], [R_hid[hb_]])

            def ffn2(idx):
                c, ti = items[idx]
                k = c % 2
                t0, ntok, blks = tgs[ti]
                hb_ = idx % 2
                for bi_, (b, nt) in enumerate(blks):
                    yb = 2 + 2 * (bi_ % 2)
                    for half in range(2):
                        for sub in range(4):
                            mm(bank(yb + half)[0:nt, :], hid[hb_][:, sub, bi_ * 128:bi_ * 128 + nt],
                               W2c[k][:, sub, half * 512:(half + 1) * 512], sub == 0, sub == 3, [R_hid[hb_], R_W2[k]], [RB[yb + half]])
                    tt("dve", x1[0:nt, b, :], x1[0:nt, b, :], PS[0:nt, yb * 512:(yb + 2) * 512], ALU.add,
                       [RB[yb], RB[yb + 1], R_x1[b]], [R_x1[b]])

            load_ffn(0)
            if NCH > 1:
                load_ffn(1)
            ffn1(0)
            for idx in range(len(items)):
                if idx + 1 < len(items):
                    ffn1(idx + 1)
                ffn2(idx)
                c, ti = items[idx]
                if ti == len(tgs) - 1 and c + 2 < NCH:
                    load_ffn(c + 2)
            for j in range(NOWN):
                dma("sp", y_own[j * 128:(j + 1) * 128, :], x1[:, j, :], [R_x1[j]], [])
            dma("sp", y_s[:, :], x1[0:DS, NOWN, :], [R_x1[NOWN]], [])

        S.finish()
    return nc


ROT_DIM = 16
ROPE_THETA = 500000.0


def _rope_tables(pos):
    inv = ROPE_THETA ** (-np.arange(0, ROT_DIM, 2, dtype=np.float32) / ROT_DIM)
    ang = pos.astype(np.float32)[:, None] * inv[None, :].astype(np.float32)
    return np.cos(ang).astype(np.float32), np.sin(ang).astype(np.float32)


_NC_CACHE = {}
STOP_AFTER = 99


def kernel(x_prompt, x_sample, cache_diff_k, cache_diff_v, cache_sb_k, cache_sb_v, meta_tokens,
           g_mix, w_in, q_norm_g, k_norm_g, lam_q1, lam_k1, lam_q2, lam_k2, sub_g,
           w_diff_out, w_sb_out, w_out, g_ffn, w_ff1, w_ff2):
    f = lambda a: np.ascontiguousarray(np.asarray(a, dtype=np.float32))
    x_prompt, x_sample, meta_tokens = f(x_prompt), f(x_sample), f(meta_tokens)
    B, SEQ, _ = x_prompt.shape
    NB = SEQ // 128
    PAST = cache_diff_k.shape[2]
    PB = PAST // 128
    NOWN = NB // 2
    NSLOT = NB + 2
    key = (NB, PB, STOP_AFTER)
    if key not in _NC_CACHE:
        _NC_CACHE[key] = build(NB, PB, STOP_AFTER)
    nc = _NC_CACHE[key]

    ii = np.arange(128)
    Dd = ((ii[:, None] // 64) <= (ii[None, :] // 64)).astype(np.float32)
    Dsm = (ii[:, None] < ii[None, :]).astype(np.float32)
    trineg = -(ii[:, None] >= ii[None, :]).astype(np.float32)
    ones = np.ones((128, 128), np.float32)
    shared = dict(
        w_in=f(w_in)[0], w_do=f(w_diff_out)[0], w_so=f(w_sb_out)[0], w_o=f(w_out)[0], w_f1=f(w_ff1)[0], w_f2=f(w_ff2)[0],
        g_mix=f(g_mix)[0], g_ffn=f(g_ffn)[0], qng=f(q_norm_g)[0], kng=f(k_norm_g)[0], subg=f(sub_g)[0],
        lamv=np.stack([f(lam_q1)[0], f(lam_k1)[0], f(lam_q2)[0], f(lam_k2)[0]]),
    )
    in_maps = []
    for c in range(8):
        b, p = c // 2, c % 2
        own = [2 * j + p for j in range(NOWN)]
        oth = [2 * j + 1 - p for j in range(NOWN)]
        order = own + oth
        xb = x_prompt[b].reshape(NB, 128, DM)
        x_slots = np.concatenate([xb[order].reshape(NB * 128, DM), meta_tokens], axis=0)
        cs = np.zeros((128, NSLOT, 8), np.float32)
        sn = np.zeros((128, NSLOT, 8), np.float32)
        for s, g in enumerate(order):
            cc, ss = _rope_tables(N_META + 128 * g + np.arange(128))
            cs[:, s], sn[:, s] = cc, ss
        cc, ss = _rope_tables(np.arange(N_META))
        cs[:N_META, NB], sn[:N_META, NB] = cc, ss
        cc, ss = _rope_tables(PAST + np.arange(DS))
        cs[:DS, NB + 1], sn[:DS, NB + 1] = cc, ss
        a_, b_ = (1.0, 0.0) if p == 0 else (0.0, 1.0)
        cmat = np.stack([trineg, Dd, Dsm, ones * float(p), -a_ * ones, -b_ * ones, -ones, 0 * ones]).astype(np.float32)
        m = dict(shared)
        m.update(x_slots=np.ascontiguousarray(x_slots), xs=x_sample[c],
                 c_dk=f(cache_diff_k)[0, c], c_dv=f(cache_diff_v)[0, c], c_sk=f(cache_sb_k)[0, c], c_sv=f(cache_sb_v)[0, c],
                 cs_t=cs, sn_t=sn, cmat=cmat)
        in_maps.append(m)
    res = run_bass_kernel_spmd(nc, in_maps, core_ids=list(range(8)))
    R = res.results
    T = SEQ + N_META
    y_prompt = np.zeros((B, SEQ, DM), np.float32)
    y_sample = np.zeros((8, DS, DM), np.float32)
    pk = np.zeros((1, B, T, NH, 128), np.float32)
    pv = np.zeros((1, B, T, NH, 128), np.float32)
    psk = np.zeros((1, B, T, NH, 64), np.float32)
    psv = np.zeros((1, B, T, NH, 64), np.float32)
    sk_ = np.zeros((1, 8, DS, NH, 128), np.float32)
    sv_ = np.zeros((1, 8, DS, NH, 128), np.float32)
    ssk = np.zeros((1, 8, DS, NH, 64), np.float32)
    ssv = np.zeros((1, 8, DS, NH, 64), np.float32)
    for c in range(8):
        b, p = c // 2, c % 2
        r = R[c]
        for j in range(NOWN):
            g = 2 * j + p
            y_prompt[b, 128 * g:128 * (g + 1)] = r["y_own"][128 * j:128 * (j + 1)]
            sl = slice(N_META + 128 * g, N_META + 128 * (g + 1))
            pk[0, b, sl] = r["kd_o"][128 * j:128 * (j + 1)]
            pv[0, b, sl] = r["vd_o"][128 * j:128 * (j + 1)]
            psk[0, b, sl] = r["sk_o"][128 * j:128 * (j + 1)]
            psv[0, b, sl] = r["sv_o"][128 * j:128 * (j + 1)]
        if p == 0:
            pk[0, b, :N_META] = r["kd_m"]
            pv[0, b, :N_META] = r["vd_m"]
            psk[0, b, :N_META] = r["sk_m"]
            psv[0, b, :N_META] = r["sv_m"]
        y_sample[c] = r["y_s"]
        sk_[0, c], sv_[0, c], ssk[0, c], ssv[0, c] = r["kd_s"], r["vd_s"], r["sk_s"], r["sv_s"]
    return (y_prompt, y_sample, pk, pv, psk, psv, sk_, sv_, ssk, ssv)
```

```python
from contextlib import ExitStack

import numpy as np
import concourse.bass as bass
import concourse.mybir as mybir
from concourse.bass_utils import run_bass_kernel_spmd

F32 = mybir.dt.float32
BF16 = mybir.dt.bfloat16
ALU = mybir.AluOpType
AF = mybir.ActivationFunctionType
AX = mybir.AxisListType


class Res:
    __slots__ = ("name", "w", "r", "excl", "wl")

    def __init__(self, name, excl=False):
        self.name = name
        self.wl = []
        self.excl = excl
        self.w = None
        self.r = []


class Sched:
    ENG = ("pe", "act", "dve", "pool", "sp")
    NDMA = {"sp": 12, "pool": 8, "act": 6}

    def __init__(self, nc, es):
        self.nc = nc
        self.es = es
        self.ops = {e: [] for e in self.ENG}
        self.cnt = {e: 0 for e in self.ENG}
        self.waited = {e: {} for e in self.ENG}
        self.sem = {}
        for e in self.ENG:
            self.sem[e] = es.enter_context(nc.semaphore("c_" + e))
        self.dsem = {}
        self.dcnt = {}
        self.drr = {}
        for q, n in self.NDMA.items():
            self.dsem[q] = []
            for i in range(n):
                nm = f"d_{q}{i}"
                self.sem[nm] = es.enter_context(nc.semaphore(nm))
                self.dsem[q].append(nm)
                self.dcnt[nm] = 0
            self.drr[q] = 0
        self.nwaits = 0
        self.pending = {}

    def sbuf(self, name, shape, dtype):
        return self.es.enter_context(self.nc.sbuf_tensor(name, shape, dtype))

    def psum(self, name, shape, dtype):
        return self.es.enter_context(self.nc.psum_tensor(name, shape, dtype))

    def _collect(self, eng, reads, writes, is_dma=False):
        deps = {}

        def add(tk, raw):
            if tk is None:
                return
            s, v = tk
            if s == eng and eng == "pe":
                return
            if deps.get(s, 0) < v:
                deps[s] = v

        for r in reads:
            add(r.w, True)
            for t in r.wl:
                add(t, True)
            if r.excl:
                for t in r.r:
                    add(t, False)
        for w in writes:
            if not (is_dma and w.w is not None and w.w[0].startswith("d_")):
                add(w.w, False)
                for t in w.wl:
                    add(t, False)
            for t in w.r:
                add(t, False)
        waits = []
        wd = self.waited[eng]
        for s, v in deps.items():
            if wd.get(s, 0) < v:
                wd[s] = v
                waits.append((s, v))
        self.nwaits += len(waits)
        return waits

    @staticmethod
    def _update(tk, reads, writes, is_dma=False):
        for r in reads:
            r.r.append(tk)
        for w in writes:
            if is_dma and w.w is not None and w.w[0].startswith("d_"):
                w.wl = w.wl + [w.w]
            else:
                w.wl = []
            w.w = tk
            w.r = []

    def op(self, eng, fn, reads=(), writes=()):
        waits = self.pending.pop(eng, []) + self._collect(eng, reads, writes)
        self.cnt[eng] += 1
        tk = (eng, self.cnt[eng])
        self.ops[eng].append((waits, fn, (eng, 1)))
        self._update(tk, reads, writes)
        return tk

    def dma(self, q, fn, reads=(), writes=()):
        names = self.dsem[q]
        nm = names[self.drr[q] % len(names)]
        self.drr[q] += 1
        waits = self.pending.pop(q, []) + self._collect(q, reads, writes, is_dma=True)
        prev = 16 * self.dcnt[nm]
        if prev and self.waited[q].get(nm, 0) < prev:
            self.waited[q][nm] = prev
            waits.append((nm, prev))
        self.dcnt[nm] += 1
        tk = (nm, 16 * self.dcnt[nm])
        self.ops[q].append((waits, fn, (nm, 16)))
        self._update(tk, reads, writes, is_dma=True)
        return tk

    def barrier(self):
        allt = [(e, c) for e, c in self.cnt.items() if c] + [(nm, 16 * c) for nm, c in self.dcnt.items() if c]
        for e in self.ENG:
            waits = []
            wd = self.waited[e]
            for sname, v in allt:
                if sname == e and e == "pe":
                    continue
                if wd.get(sname, 0) < v:
                    wd[sname] = v
                    waits.append((sname, v))
            self.pending[e] = self.pending.get(e, []) + waits

    def finish(self):
        final = [(nm, 16 * c) for nm, c in self.dcnt.items() if c]
        sem = self.sem

        def replay(name, e, tail=()):
            for waits, fn, inc in self.ops[name]:
                for s, v in waits:
                    e.wait_ge(sem[s], v)
                ins = fn(e)
                ins.then_inc(sem[inc[0]], inc[1])
            for s, v in tail:
                e.wait_ge(sem[s], v)

        with self.nc.Block() as block:
            @block.tensor
            def _(e):
                replay("pe", e)

            @block.scalar
            def _(e):
                replay("act", e)

            @block.vector
            def _(e):
                replay("dve", e)

            @block.gpsimd
            def _(e):
                replay("pool", e)

            @block.sync
            def _(e):
                replay("sp", e, tail=final)


DM = 1024
NH = 8
N_META = 16
DS = 32
EPS = 1e-6
LAM_INIT = 0.2
IN_COLS = 6656
GQ = 4


class Arena:
    def __init__(self, S, nbytes):
        self.n = nbytes
        self.t = S.sbuf("arena", [128, nbytes // 2], BF16)
        self.off = 0

    def seek(self, off):
        self.off = off

    def alloc(self, shape, dtype):
        esz = 4 if dtype == F32 else 2
        n = 1
        for d in shape[1:]:
            n *= d
        nb = (n * esz + 31) // 32 * 32
        assert self.off + nb <= self.n, f"arena overflow {self.off}+{nb}>{self.n}"
        v = self.t[:, self.off // 2:(self.off + nb) // 2]
        self.off += nb
        if dtype == F32:
            v = v.bitcast(F32)
        v = v[:, 0:n]
        if len(shape) == 3:
            v = v.rearrange("p (a b) -> p a b", b=shape[2])
        elif len(shape) == 4:
            v = v.rearrange("p (a b c) -> p a b c", b=shape[2], c=shape[3])
        return v


def build(NB, PB, stop_after=99):
    assert NB % 2 == 0 and PB == NB
    NOWN = NB // 2
    gq = min(GQ, NOWN)
    assert NOWN % gq == 0
    NG = NOWN // gq
    NT = NB * 128 + N_META
    NSLOT = NB + 2
    HALF = NOWN * 128
    SKW = HALF + 128
    KW = NB * 128 + 32

    nc = bass.Bass("TRN2", target_bir_lowering=False)

    def din(name, shape):
        return nc.dram_tensor(name, shape, F32, kind="ExternalInput").ap()

    def dout(name, shape):
        return nc.dram_tensor(name, shape, F32, kind="ExternalOutput").ap()

    x_slots = din("x_slots", [NT, DM])
    xs = din("xs", [DS, DM])
    c_dk = din("c_dk", [PB * 128, NH, 128])
    c_dv = din("c_dv", [PB * 128, NH, 128])
    c_sk = din("c_sk", [PB * 128, NH, 64])
    c_sv = din("c_sv", [PB * 128, NH, 64])
    w_in = din("w_in", [DM, IN_COLS])
    w_do = din("w_do", [DM, DM])
    w_so = din("w_so", [512, DM])
    w_o = din("w_o", [DM, DM])
    w_f1 = din("w_f1", [DM, 4 * DM])
    w_f2 = din("w_f2", [4 * DM, DM])
    g_mix = din("g_mix", [DM])
    g_ffn = din("g_ffn", [DM])
    qng = din("qng", [64])
    kng = din("kng", [64])
    lamv = din("lamv", [4, 64])
    subg = din("subg", [128])
    cs_t = din("cs_t", [128, NSLOT, 8])
    sn_t = din("sn_t", [128, NSLOT, 8])
    cmat = din("cmat", [8, 128, 128])

    y_own = dout("y_own", [NOWN * 128, DM])
    y_s = dout("y_s", [DS, DM])
    kd_o = dout("kd_o", [NOWN * 128, NH, 128])
    vd_o = dout("vd_o", [NOWN * 128, NH, 128])
    sk_o = dout("sk_o", [NOWN * 128, NH, 64])
    sv_o = dout("sv_o", [NOWN * 128, NH, 64])
    kd_m = dout("kd_m", [N_META, NH, 128])
    vd_m = dout("vd_m", [N_META, NH, 128])
    sk_m = dout("sk_m", [N_META, NH, 64])
    sv_m = dout("sv_m", [N_META, NH, 64])
    kd_s = dout("kd_s", [DS, NH, 128])
    vd_s = dout("vd_s", [DS, NH, 128])
    sk_s = dout("sk_s", [DS, NH, 64])
    sv_s = dout("sv_s", [DS, NH, 64])

    es = ExitStack()
    with es:
        S = Sched(nc, es)
        AR = Arena(S, 204800)
        PS = S.psum("PS", [128, 4096], F32)
        RB = [Res(f"bank{i}", excl=True) for i in range(8)]

        def bank(i):
            return PS[:, i * 512:(i + 1) * 512]

        def bankb(i):
            return PS[:, i * 512:(i + 1) * 512].bitcast(BF16)

        def mm(out, lhsT, rhs, start, stop, reads, writes, skip=False):
            S.op("pe", lambda e, o=out, l=lhsT, r=rhs, a=start, b=stop, k=skip:
                 e.matmul(o, lhsT=l, rhs=r, start=a, stop=b, skip_group_check=k), reads, writes)

        def tr(out, in_, reads, writes):
            n = in_.shape[0]
            S.op("pe", lambda e, o=out, i=in_, n=n: e.transpose(out=o, in_=i, identity=ident[0:n, 0:n]),
                 list(reads) + [R_const], writes)

        def act(out, in_, func, reads, writes, scale=1.0, bias=0.0, accum=None):
            S.op("act", lambda e, o=out, i=in_, f=func, s=scale, b=bias, a=accum:
                 e.activation(out=o, in_=i, func=f, scale=s, bias=b, accum_out=a), reads, writes)

        def tt(eng, out, in0, in1, op, reads, writes):
            S.op(eng, lambda e, o=out, a=in0, b=in1, p=op: e.tensor_tensor(out=o, in0=a, in1=b, op=p), reads, writes)

        def ts(eng, out, in0, s1, s2, op0, op1, reads, writes):
            if s2 is None:
                S.op(eng, lambda e, o=out, a=in0, x=s1, p=op0: e.tensor_scalar(out=o, in0=a, scalar1=x, scalar2=None, op0=p), reads, writes)
            else:
                S.op(eng, lambda e, o=out, a=in0, x=s1, y=s2, p=op0, q=op1:
                     e.tensor_scalar(out=o, in0=a, scalar1=x, scalar2=y, op0=p, op1=q), reads, writes)

        def stt(eng, out, in0, scalar, in1, op0, op1, reads, writes):
            S.op(eng, lambda e, o=out, a=in0, s=scalar, b=in1, p=op0, q=op1:
                 e.scalar_tensor_tensor(out=o, in0=a, scalar=s, in1=b, op0=p, op1=q), reads, writes)

        def cp(eng, out, in_, reads, writes):
            if eng == "act":
                act(out, in_, AF.Copy, reads, writes)
            else:
                S.op(eng, lambda e, o=out, i=in_: e.tensor_copy(out=o, in_=i), reads, writes)

        def dma(q, out, in_, reads, writes):
            S.dma(q, lambda e, o=out, i=in_: e.dma_start(out=o, in_=i), reads, writes)

        def barrier():
            S.barrier()

        R_const = Res("const")
        cm = AR.alloc([128, 8, 128], BF16)
        trineg, Dd, Ds, MO, MA, MB, onesneg, zero = [cm[:, i, :] for i in range(8)]
        ident = AR.alloc([128, 128], BF16)
        CS = AR.alloc([128, NSLOT, 8], F32)
        SN = AR.alloc([128, NSLOT, 8], F32)
        qg = AR.alloc([128, 64], F32)
        kg = AR.alloc([128, 64], F32)
        subg8 = AR.alloc([128, 128], F32)
        lv = AR.alloc([128, 4, 64], F32)
        lamt = AR.alloc([128, 16], F32)
        OFF_O = AR.off

        dma("pool", cm, cmat.rearrange("m p c -> p m c"), [], [R_const])
        dma("sp", CS, cs_t, [], [R_const])
        dma("sp", SN, sn_t, [], [R_const])
        dma("sp", qg, qng.partition_broadcast(128), [], [R_const])
        dma("sp", kg, kng.partition_broadcast(128), [], [R_const])
        dma("sp", subg8, subg.partition_broadcast(128), [], [R_const])
        for i in range(4):
            dma("sp", lv[:, i, :], lamv[i].partition_broadcast(128), [], [R_const])
        S.op("pool", lambda e: e.memset(ident, 1.0), [], [R_const])
        S.op("pool", lambda e: e.affine_select(out=ident, in_=ident, pattern=[[-1, 128]], compare_op=ALU.is_equal,
                                               fill=0.0, base=0, channel_multiplier=1), [R_const], [R_const])
        ts("dve", subg8, subg8, 1.0 - LAM_INIT, None, ALU.mult, None, [R_const], [R_const])
        tt("dve", lv[:, 0, :], lv[:, 0, :], lv[:, 1, :], ALU.mult, [R_const], [R_const])
        tt("dve", lv[:, 2, :], lv[:, 2, :], lv[:, 3, :], ALU.mult, [R_const], [R_const])
        S.op("dve", lambda e: e.tensor_reduce(out=lamt[:, 0:1], in_=lv[:, 0, :], axis=AX.X, op=ALU.add), [R_const], [R_const])
        S.op("dve", lambda e: e.tensor_reduce(out=lamt[:, 1:2], in_=lv[:, 2, :], axis=AX.X, op=ALU.add), [R_const], [R_const])
        act(lamt[:, 2:4], lamt[:, 0:2], AF.Exp, [R_const], [R_const])
        tt("dve", lamt[:, 5:6], lamt[:, 3:4], lamt[:, 2:3], ALU.subtract, [R_const], [R_const])
        ts("dve", lamt[:, 4:5], lamt[:, 5:6], -LAM_INIT, None, ALU.add, None, [R_const], [R_const])
        neg_lam = lamt[:, 4:5]

        Od = AR.alloc([128, NOWN + 1, 1024], BF16)
        Os = AR.alloc([128, NOWN + 1, 512], BF16)
        R_O = [Res(f"O{j}") for j in range(NOWN + 1)]
        OFF_P12 = AR.off
        hT = AR.alloc([128, 8, NT], BF16)
        hTs = AR.alloc([128, 8, DS], BF16)
        R_hT = [Res(f"hT{s}") for s in range(NB + 2)]
        OFF_HEAD = AR.off
        NALT = 2 * gq + 1
        KT = AR.alloc([128, KW + NALT * 128], BF16)
        Vaug = AR.alloc([128, NB + 1 + NALT, 130], BF16)
        SKT2 = AR.alloc([128, SKW + gq * 128 + 128], BF16)
        SVaug = AR.alloc([128, NB + 1 + NALT, 66], BF16)
        QT = AR.alloc([128, (NOWN + gq) * 128], BF16)
        SQT2 = AR.alloc([128, (NOWN + gq) * 128], BF16)
        Wh = AR.alloc([128, 8, 576], BF16)
        R_K = [Res(f"K{s}") for s in range(NB + 1)]
        R_Q = [Res(f"Q{s}") for s in range(NOWN)]
        R_Kalt = [Res(f"Ka{s}") for s in range(NALT)]
        R_Qalt = [Res(f"Qa{s}") for s in range(gq)]

        class Map:
            def __init__(self, par):
                self.par = par

            def alt(self, s):
                if not self.par:
                    return None
                if s < gq:
                    return s
                if NOWN <= s < NOWN + gq:
                    return gq + s - NOWN
                if s == NB:
                    return 2 * gq
                return None

            def kt(self, s):
                a = self.alt(s)
                return s * 128 if a is None else KW + a * 128

            def vs(self, s):
                a = self.alt(s)
                return s if a is None else NB + 1 + a

            def skl(self, s):
                a = self.alt(s)
                if a is None:
                    return (0, s * 128) if s < NOWN else (64, (s - NOWN) * 128)
                if a < gq:
                    return 0, SKW + a * 128
                return 64, SKW + (a - gq) * 128

            def qc(self, j):
                return (NOWN + j) * 128 if (self.par and j < gq) else j * 128

            def rk(self, s):
                a = self.alt(s)
                return R_K[s] if a is None else R_Kalt[a]

            def rq(self, j):
                return R_Qalt[j] if (self.par and j < gq) else R_Q[j]

        MAPS = [Map(0), Map(1)]
        R_Wh = Res("Wh")
        OFF_WORK = AR.off

        def slot_tok0(s):
            return s * 128

        def sk_loc(s):
            if s < NOWN:
                return 0, s * 128
            return 64, (s - NOWN) * 128

        S.op("pool", lambda e, t=Vaug[:, :, 128:130]: e.memset(t, 1.0), [], R_K)
        S.op("pool", lambda e, t=SVaug[:, :, 64:66]: e.memset(t, 1.0), [], R_K)

        AR.seek(OFF_WORK)
        x_st = [AR.alloc([128, DM], F32) for _ in range(2)]
        h_sb = [AR.alloc([128, DM], BF16) for _ in range(2)]
        junk = AR.alloc([128, DM], BF16)
        gb = AR.alloc([128, DM], F32)
        st1 = AR.alloc([128, 8], F32)
        R_x = [Res("x0"), Res("x1")]
        R_h = [Res("h0"), Res("h1")]
        R_junk = Res("junk")
        R_gb = Res("gb")
        R_st = [Res("st0"), Res("st1")]
        dma("sp", gb, g_mix.partition_broadcast(128), [], [R_gb])

        p1_cnt = [0]

        def norm_block(src_ap, nt, dstT, R_dst, gtile, R_g, x_keep=None, tbank=None):
            k = p1_cnt[0] % 2
            p1_cnt[0] += 1
            xb, hb = x_st[k], h_sb[k]
            if src_ap is not None:
                dma("sp", xb[0:nt, :], src_ap, [], [R_x[k]])
            else:
                xb = x_keep
            stc = st1[:, 4 * k:4 * k + 4]
            act(junk[0:nt, :], xb[0:nt, :], AF.Square, [R_x[k]], [R_junk, R_st[k]], accum=stc[0:nt, 0:1])
            act(stc[0:nt, 1:2], stc[0:nt, 0:1], AF.Ln, [R_st[k]], [R_st[k]], scale=1.0 / DM, bias=EPS)
            act(stc[0:nt, 2:3], stc[0:nt, 1:2], AF.Exp, [R_st[k]], [R_st[k]], scale=-0.5)
            stt("dve", hb[0:nt, :], xb[0:nt, :], stc[0:nt, 2:3], gtile[0:nt, :], ALU.mult, ALU.mult,
                [R_x[k], R_st[k], R_g], [R_h[k]])
            bi = 6 + k if tbank is None else tbank
            pT = bankb(bi)
            for ch in range(8):
                tr(pT[:, ch * 128:ch * 128 + nt], hb[0:nt, ch * 128:(ch + 1) * 128], [R_h[k]], [RB[bi]])
            src = pT.rearrange("p (c t) -> p c t", t=128)[:, :, 0:nt]
            return lambda: cp("act" if k == 0 else "dve", dstT, src, [RB[bi]], [R_dst])

        pend = None
        for s in range(NB + 1):
            nt = 128 if s < NB else N_META
            t0 = slot_tok0(s)
            nxt = norm_block(x_slots[t0:t0 + nt, :], nt, hT[:, :, t0:t0 + nt], R_hT[s], gb, R_gb)
            if pend is not None:
                pend()
            pend = nxt
        nxt = norm_block(xs[:, :], DS, hTs[:, :, :], R_hT[NB + 1], gb, R_gb)
        pend()
        nxt()
        barrier()

        AR.seek(OFF_WORK)
        sq_t = AR.alloc([128, 4, 2, 64], F32)
        kn = AR.alloc([128, 4, 2, 64], F32)
        qn = AR.alloc([128, 4, 2, 64], F32)
        knb = AR.alloc([128, 4, 128], BF16)
        qnb = knb
        rt = [AR.alloc([128, 4, 2, 8], F32) for _ in range(4)]
        vout = AR.alloc([128, 4, 128], F32)
        skout = AR.alloc([128, 4, 64], F32)
        svout = AR.alloc([128, 4, 64], F32)
        skb2 = AR.alloc([128, 4, 128], BF16)
        sqb2 = AR.alloc([128, 4, 128], BF16)
        stk = AR.alloc([128, 4, 4, 2], F32)
        stq = AR.alloc([128, 4, 4, 2], F32)
        Et = [[AR.alloc([128, 512], BF16) for _ in range(2)] for _ in range(2)]
        Lt = [[AR.alloc([128, 512], BF16) for _ in range(2)] for _ in range(2)]
        Wt = Et
        Oacc = AR.alloc([128, 4, 64], F32)
        tmpS = AR.alloc([128, 4, 64], F32)
        stick = AR.alloc([128, 8], F32)
        o4 = [AR.alloc([128, 128], F32) for _ in range(4)]
        sqj = AR.alloc([128, 128], BF16)
        fin = AR.alloc([128, 16], F32)
        fin2 = AR.alloc([128, 16], F32)
        R_o4 = [Res(f"o4_{i}") for i in range(4)]
        R_sqj, R_fin2 = Res("sqj"), Res("fin2")
        OFF_P3 = AR.off
        R_sq, R_kn, R_qn, R_knb, R_rt, R_rt2 = Res("sq"), Res("kn"), Res("qn"), Res("knb"), Res("rt"), Res("rt2")
        R_qnb = R_knb
        R_vo, R_sko, R_svo, R_skb, R_sqb, R_stk, R_stq = (Res("vo"), Res("sko"), Res("svo"), Res("skb"),
                                                         Res("sqb"), Res("stk"), Res("stq"))
        R_E = [[Res(f"E{c}{b}") for b in range(2)] for c in range(2)]
        R_L = [[Res(f"L{c}{b}") for b in range(2)] for c in range(2)]
        R_W = R_E
        R_Oacc, R_tmpS, R_stick, R_ot, R_tmpt, R_fin = Res("Oacc"), Res("tmpS"), Res("stick"), Res("ot"), Res("tmpt"), Res("fin")
        S.op("pool", lambda e: e.memset(skb2, 0.0), [], [R_skb])

        w_in_v = w_in.rearrange("(c p) n -> p c n", p=128)

        def load_wh(hd):
            segs = [(1024 + hd * 128, 0, 128), (2048 + hd * 128, 128, 128), (3584 + hd * 64, 256, 64),
                    (4096 + hd * 64, 320, 64), (hd * 128, 384, 128), (3072 + hd * 64, 512, 64)]
            for c0, d0, w in segs:
                dma("pool", Wh[:, :, d0:d0 + w], w_in_v[:, :, c0:c0 + w], [], [R_Wh])

        KV4 = PS[:, 0:2048].rearrange("p (s c) -> p s c", c=512)
        Q4 = PS[:, 2048:3072].rearrange("p (s c) -> p s c", c=256)

        def qk_norm_ops(ns, nt, dst, R_dst, stt_, R_stt, gvec, slot0):
            ops = []
            d = dst[0:nt, 0:ns]
            ops.append(lambda: tt("pool", sq_t[0:nt, 0:ns], d, d, ALU.mult, [R_dst], [R_sq]))
            ops.append(lambda: S.op("dve", lambda e, n=nt, m=ns, s=stt_: e.tensor_reduce(out=s[0:n, 0, 0:m, :], in_=sq_t[0:n, 0:m], axis=AX.X, op=ALU.add),
                                    [R_sq], [R_stt]))
            ops.append(lambda: act(stt_[0:nt, 1, 0:ns, :], stt_[0:nt, 0, 0:ns, :], AF.Ln, [R_stt], [R_stt], scale=1.0 / 64, bias=EPS))
            ops.append(lambda: act(stt_[0:nt, 2, 0:ns, :], stt_[0:nt, 1, 0:ns, :], AF.Exp, [R_stt], [R_stt], scale=-0.5))
            ops.append(lambda: tt("dve", d, d, stt_[0:nt, 2, 0:ns, :].unsqueeze(3).to_broadcast([nt, ns, 2, 64]), ALU.mult,
                                  [R_stt, R_dst], [R_dst]))
            ops.append(lambda: tt("dve", d, d, gvec[0:nt, :].unsqueeze(1).unsqueeze(1).to_broadcast([nt, ns, 2, 64]), ALU.mult,
                                  [R_dst, R_const], [R_dst]))
            cosb = CS[0:nt, slot0:slot0 + ns, :].unsqueeze(2).to_broadcast([nt, ns, 2, 8])
            sinb = SN[0:nt, slot0:slot0 + ns, :].unsqueeze(2).to_broadcast([nt, ns, 2, 8])
            x1 = dst[0:nt, 0:ns, :, 0:8]
            x2 = dst[0:nt, 0:ns, :, 8:16]
            r = [t[0:nt, 0:ns] for t in rt]
            ops.append(lambda: tt("pool", r[0], x1, cosb, ALU.mult, [R_dst, R_const], [R_rt]))
            ops.append(lambda: tt("dve", r[1], x2, sinb, ALU.mult, [R_dst, R_const], [R_rt2]))
            ops.append(lambda: tt("pool", r[2], x2, cosb, ALU.mult, [R_dst, R_const], [R_rt]))
            ops.append(lambda: tt("dve", r[3], x1, sinb, ALU.mult, [R_dst, R_const], [R_rt2]))
            ops.append(lambda: tt("pool", x1, r[0], r[1], ALU.subtract, [R_rt, R_rt2], [R_dst]))
            ops.append(lambda: tt("dve", x2, r[2], r[3], ALU.add, [R_rt, R_rt2], [R_dst]))
            return ops

        def projA(slots, nt, with_q, hsrc, R_hsrc, qslots, outs, rope_slot0, par=0, bgfg=False):
            ns = len(slots)
            s0 = slots[0]
            mp = MAPS[par]
            Rk = [mp.rk(s) for s in slots]
            v0 = mp.vs(s0)
            pb_, c0 = mp.skl(s0)
            two_d = outs is not None and len(outs["kd"].shape) == 2

            def o3(ap):
                return ap[:, 0, :] if two_d else ap
            ops = []
            if not bgfg:
                for si, s in enumerate(slots):
                    hs = hsrc(si)
                    for ch in range(8):
                        mm(bank(si)[0:nt, 0:384], hs[:, ch, :], Wh[:, ch, 0:384], ch == 0, ch == 7, [R_hsrc[si], R_Wh], [RB[si]])
                    if with_q:
                        qb = 4 + si // 2
                        qo = (si % 2) * 256
                        for ch in range(8):
                            mm(bank(qb)[0:nt, qo:qo + 192], hs[:, ch, :], Wh[:, ch, 384:576], ch == 0, ch == 7,
                               [R_hsrc[si], R_Wh], [RB[qb]])
                kvv = KV4[0:nt, 0:ns, :]
                allb = [RB[i] for i in range(ns)]
                act(kn[0:nt, 0:ns], kvv[:, :, 0:128].rearrange("p s (c d) -> p s c d", c=2), AF.Copy, allb, [R_kn])
                vv = kvv[:, :, 128:256]
                cp("dve", Vaug[0:nt, v0:v0 + ns, 0:128], vv, allb, Rk)
                skv = kvv[:, :, 256:320]
                cp("dve", skb2[0:nt, 0:ns, pb_:pb_ + 64], skv, allb, [R_skb])
                svv = kvv[:, :, 320:384]
                cp("dve", SVaug[0:nt, v0:v0 + ns, 0:64], svv, allb, Rk)
                if outs is not None:
                    cp("act", vout[0:nt, 0:ns, :], vv, allb, [R_vo])
                    dma("sp", outs["vd"], o3(vout[0:nt, 0:ns, :]), [R_vo], [])
                    cp("act", skout[0:nt, 0:ns, :], skv, allb, [R_sko])
                    dma("sp", outs["sk"], o3(skout[0:nt, 0:ns, :]), [R_sko], [])
                    cp("act", svout[0:nt, 0:ns, :], svv, allb, [R_svo])
                    dma("sp", outs["sv"], o3(svout[0:nt, 0:ns, :]), [R_svo], [])
                if with_q:
                    qv = Q4[0:nt, 0:ns, :]
                    qbanks = [RB[4], RB[5]]
                    cp("dve", qn[0:nt, 0:ns], qv[:, :, 0:128].rearrange("p s (c d) -> p s c d", c=2), qbanks, [R_qn])
                    act(sqb2[0:nt, 0:ns, 0:64], qv[:, :, 128:192], AF.Copy, qbanks, [R_sqb], scale=0.125)
                    act(sqb2[0:nt, 0:ns, 64:128], qv[:, :, 128:192], AF.Copy, qbanks, [R_sqb], scale=0.125)
            else:
                b7 = bank(7)
                r7 = [RB[7]]
                for si, s in enumerate(slots):
                    hs = hsrc(si)

                    def mm_kv(hs=hs, si=si):
                        for ch in range(8):
                            mm(b7[0:nt, 0:384], hs[:, ch, :], Wh[:, ch, 0:384], ch == 0, ch == 7, [R_hsrc[si], R_Wh], r7)
                    ops.append(mm_kv)
                    ops.append(lambda si=si: act(kn[0:nt, si], b7[0:nt, 0:128].rearrange("p (c d) -> p c d", c=2), AF.Copy, r7, [R_kn]))
                    ops.append(lambda si=si: cp("dve", Vaug[0:nt, v0 + si, 0:128], b7[0:nt, 128:256], r7, [Rk[si]]))
                    ops.append(lambda si=si: cp("dve", skb2[0:nt, si, pb_:pb_ + 64], b7[0:nt, 256:320], r7, [R_skb]))
                    ops.append(lambda si=si: cp("dve", SVaug[0:nt, v0 + si, 0:64], b7[0:nt, 320:384], r7, [Rk[si]]))
                    if outs is not None:
                        ops.append(lambda si=si: cp("act", vout[0:nt, si, :], b7[0:nt, 128:256], r7, [R_vo]))
                        ops.append(lambda si=si: cp("act", skout[0:nt, si, :], b7[0:nt, 256:320], r7, [R_sko]))
                        ops.append(lambda si=si: cp("act", svout[0:nt, si, :], b7[0:nt, 320:384], r7, [R_svo]))
                    if with_q:
                        def mm_q(hs=hs, si=si):
                            for ch in range(8):
                                mm(b7[0:nt, 0:192], hs[:, ch, :], Wh[:, ch, 384:576], ch == 0, ch == 7, [R_hsrc[si], R_Wh], r7)
                        ops.append(mm_q)
                        ops.append(lambda si=si: cp("dve", qn[0:nt, si], b7[0:nt, 0:128].rearrange("p (c d) -> p c d", c=2), r7, [R_qn]))
                        ops.append(lambda si=si: act(sqb2[0:nt, si, 0:64], b7[0:nt, 128:192], AF.Copy, r7, [R_sqb], scale=0.125))
                        ops.append(lambda si=si: act(sqb2[0:nt, si, 64:128], b7[0:nt, 128:192], AF.Copy, r7, [R_sqb], scale=0.125))
                if outs is not None:
                    def out_dmas():
                        dma("sp", outs["vd"], o3(vout[0:nt, 0:ns, :]), [R_vo], [])
                        dma("sp", outs["sk"], o3(skout[0:nt, 0:ns, :]), [R_sko], [])
                        dma("sp", outs["sv"], o3(svout[0:nt, 0:ns, :]), [R_svo], [])
                    ops.append(out_dmas)
            wdt = ns * 128 if nt == 128 else nt
            pT2 = bankb(7)

            def sk_tr():
                for si in range(ns):
                    tr(pT2[:, si * 128:si * 128 + nt], skb2[0:nt, si, :], [R_skb], [RB[7]])
            ops.append(sk_tr)
            ops.append(lambda: cp("dve", SKT2[pb_:pb_ + 64, c0:c0 + wdt], pT2[pb_:pb_ + 64, 0:wdt], [RB[7]], Rk))
            if with_q:
                Rq = [mp.rq(s) for s in qslots]
                q0 = mp.qc(qslots[0])

                def sq_tr():
                    for si in range(ns):
                        tr(pT2[:, si * 128:si * 128 + nt], sqb2[0:nt, si, :], [R_sqb], [RB[7]])
                ops.append(sq_tr)
                ops.append(lambda: cp("dve", SQT2[:, q0:q0 + wdt], pT2[:, 0:wdt], [RB[7]], Rq))
            ops += qk_norm_ops(ns, nt, kn, R_kn, stk, R_stk, kg, rope_slot0)
            knv = kn[0:nt, 0:ns].rearrange("p s c d -> p s (c d)")
            if outs is not None:
                ops.append(lambda: dma("sp", outs["kd"], o3(knv), [R_kn], []))
            ops.append(lambda: cp("pool", knb[0:nt, 0:ns, :], knv, [R_kn], [R_knb]))

            def k_tr():
                for si in range(ns):
                    tr(pT2[:, si * 128:si * 128 + nt], knb[0:nt, si, :], [R_knb], [RB[7]])
            ops.append(k_tr)
            t0 = mp.kt(s0)
            ops.append(lambda: cp("dve", KT[:, t0:t0 + wdt], pT2[:, 0:wdt], [RB[7]], Rk))
            if with_q:
                ops += qk_norm_ops(ns, nt, qn, R_qn, stq, R_stq, qg, rope_slot0)
                ops.append(lambda: ts("dve", qnb[0:nt, 0:ns, :], qn[0:nt, 0:ns].rearrange("p s c d -> p s (c d)"), 0.125, None,
                                      ALU.mult, None, [R_qn], [R_qnb]))

                def q_tr():
                    for si in range(ns):
                        tr(pT2[:, si * 128:si * 128 + nt], qnb[0:nt, si, :], [R_qnb], [RB[7]])
                ops.append(q_tr)
                ops.append(lambda: cp("dve", QT[:, q0:q0 + wdt], pT2[:, 0:wdt], [RB[7]], Rq))
            return ops

        def run_serial(ops):
            for o in ops:
                o()

        def weave(steps, ops, frac=0.75):
            n, m = len(steps), len(ops)
            k = 0
            for i, st in enumerate(steps):
                st()
                tgt = m if n == 0 else min(m, int((i + 1) * m / max(1.0, n * frac)) + 1)
                while k < tgt:
                    ops[k]()
                    k += 1
            while k < m:
                ops[k]()
                k += 1

        def attn_diff(q0, nqb, qw, blocks, out_blk0, hd, par=0):
            mp = MAPS[par]
            def acc(i, c):
                a = i * 2 + c
                return 4 + a // 3, (a % 3) * 129

            def finalize():
                diff_finalize(nqb, qw, acc, out_blk0, hd)
            started = set()
            nblk = len(blocks)
            lastblk = {}
            for bi_, kb in enumerate(blocks):
                for i in range(kb["fq"], nqb):
                    lastblk[i] = bi_

            def qk(t):
                kb = blocks[t]
                buf = t % 2
                nk, fq, s = kb["nk"], kb["fq"], kb["slot"]
                N = (nqb - fq) * qw
                c0 = q0 + fq * qw
                k0 = mp.kt(s)
                for c in range(2):
                    bk = buf * 2 + c
                    mm(bank(bk)[0:nk, 0:N], KT[c * 64:(c + 1) * 64, k0:k0 + nk], QT[c * 64:(c + 1) * 64, c0:c0 + N], True, True,
                       [mp.rk(s)] + kb["rq"], [RB[bk]])
                for c in range(2):
                    bk = buf * 2 + c
                    act(Et[c][buf][0:nk, 0:N], bank(bk)[0:nk, 0:N], AF.Exp, [RB[bk]], [R_E[c][buf]])
                    for qi, m in kb["masks"].items():
                        sl = Et[c][buf][0:nk, (qi - fq) * qw:(qi - fq + 1) * qw]
                        tt("dve", sl, sl, m[0:nk, 0:qw], ALU.mult, [R_E[c][buf], R_const], [R_E[c][buf]])

            def pv(t):
                kb = blocks[t]
                buf = t % 2
                nk, fq, s = kb["nk"], kb["fq"], kb["slot"]
                for i in range(fq, nqb):
                    for c in range(2):
                        bk, col = acc(i, c)
                        st = bk not in started
                        started.add(bk)
                        mm(bank(bk)[0:qw, col:col + 129], Et[c][buf][0:nk, (i - fq) * qw:(i - fq + 1) * qw],
                           Vaug[0:nk, mp.vs(s), 0:129], st, lastblk[i] == t, [R_E[c][buf], mp.rk(s)], [RB[bk]], skip=True)

            steps = []
            for t in range(nblk + 1):
                def st(t=t):
                    if t < nblk:
                        qk(t)
                    if t >= 1:
                        pv(t - 1)
                steps.append(st)
            steps.append(lambda: finalize())
            return steps

        def _unused():
            pass

        def diff_finalize(nqb, qw, acc, out_blk0, hd):
            f = fin[0:qw]
            f2 = fin2[0:qw]
            A = []
            for i in range(nqb):
                b0, c0_ = acc(i, 0)
                b1, c1_ = acc(i, 1)
                A.append((b0, bank(b0)[0:qw, c0_:c0_ + 129], b1, bank(b1)[0:qw, c1_:c1_ + 129]))
            for i, (b0, a0, b1, a1) in enumerate(A):
                S.op("dve", lambda e, o=f[:, i:i + 1], a=a0[:, 128:129]: e.reciprocal(out=o, in_=a), [RB[b0]], [R_fin])
                S.op("dve", lambda e, o=f[:, 4 + i:5 + i], a=a1[:, 128:129]: e.reciprocal(out=o, in_=a), [RB[b1]], [R_fin])
            ts("dve", f[:, 8:8 + nqb], f[:, 4:4 + nqb], neg_lam[0:qw, 0:1], None, ALU.mult, None, [R_fin, R_const], [R_fin])
            for i, (b0, a0, b1, a1) in enumerate(A):
                act(o4[i][0:qw, :], a1[:, 0:128], AF.Copy, [RB[b1], R_fin], [R_o4[i]], scale=f[:, 8 + i:9 + i])
                stt("dve", o4[i][0:qw, :], a0[:, 0:128], f[:, i:i + 1], o4[i][0:qw, :], ALU.mult, ALU.add,
                    [RB[b0], R_fin, R_o4[i]], [R_o4[i]])
                act(sqj[0:qw, :], o4[i][0:qw, :], AF.Square, [R_o4[i]], [R_sqj, R_fin2], accum=f2[:, i:i + 1])
            act(f2[:, 4:4 + nqb], f2[:, 0:nqb], AF.Ln, [R_fin2], [R_fin2], scale=1.0 / 128, bias=EPS)
            act(f2[:, 8:8 + nqb], f2[:, 4:4 + nqb], AF.Exp, [R_fin2], [R_fin2], scale=-0.5)
            for i in range(nqb):
                stt("dve", Od[0:qw, out_blk0 + i, hd * 128:(hd + 1) * 128], o4[i][0:qw, :], f2[:, 8 + i:9 + i], subg8[0:qw, :],
                    ALU.mult, ALU.mult, [R_o4[i], R_fin2, R_const], [R_O[out_blk0 + i]])

        def attn_sb(q0, nqb, qw, units, out_blk0, hd, ma, mb, par=0):
            mp = MAPS[par]
            nu = len(units)
            tt_ = None
            S.op("dve", lambda e: e.memset(stick[0:qw, 0:nqb], 1.0), [], [R_stick])
            S.op("dve", lambda e: e.memset(Oacc[0:qw, 0:nqb, :], 0.0), [], [R_Oacc])

            def geo(u):
                kb = units[u][0]
                fq = kb["fq"]
                return fq, (nqb - fq) * qw, q0 + fq * qw

            def s1(u):
                fq, N, c0 = geo(u)
                buf = u % 2
                for x, kb in enumerate(units[u]):
                    bk = buf * 2 + x
                    nk, s = kb["nk"], kb["slot"]
                    pb_, k0 = mp.skl(s)
                    mm(bank(bk)[0:nk, 0:N], SKT2[pb_:pb_ + 64, k0:k0 + nk], SQT2[pb_:pb_ + 64, c0:c0 + N], True, True,
                       [mp.rk(s)] + kb["rq"], [RB[bk]])
                for x, kb in enumerate(units[u]):
                    bk = buf * 2 + x
                    nk = kb["nk"]
                    eb = 6
                    act(bank(eb)[0:nk, 0:N], bank(bk)[0:nk, 0:N], AF.Exp, [RB[bk]], [RB[eb]])
                    act(Lt[x][buf][0:nk, 0:N], bank(eb)[0:nk, 0:N], AF.Ln, [RB[eb]], [R_L[x][buf]], bias=1.0)
                    for qi, m in kb["masks"].items():
                        sl = Lt[x][buf][0:nk, (qi - fq) * qw:(qi - fq + 1) * qw]
                        tt("dve", sl, sl, m[0:nk, 0:qw], ALU.mult, [R_L[x][buf], R_const], [R_L[x][buf]])

            def s2(u):
                fq, N, c0 = geo(u)
                buf = u % 2
                un = units[u]
                for x, kb in enumerate(un):
                    bk = buf * 2 + x
                    nk = kb["nk"]
                    pair = len(un) == 2
                    mm(bank(bk)[0:nk, 0:N], trineg[0:nk, 0:nk], Lt[x][buf][0:nk, 0:N], False, not pair,
                       [R_L[x][buf], R_const], [RB[bk]], skip=True)
                    if pair:
                        ok = un[1 - x]["nk"]
                        cm_ = ma if x == 0 else mb
                        mm(bank(bk)[0:nk, 0:N], cm_[0:ok, 0:nk], Lt[1 - x][buf][0:ok, 0:N], False, True,
                           [R_L[1 - x][buf], R_const], [RB[bk]], skip=True)
                for x, kb in enumerate(un):
                    bk = buf * 2 + x
                    nk = kb["nk"]
                    act(Wt[x][buf][0:nk, 0:N], bank(bk)[0:nk, 0:N], AF.Exp, [RB[bk]], [R_W[x][buf]])
                    for qi, m in kb["masks"].items():
                        sl = Wt[x][buf][0:nk, (qi - fq) * qw:(qi - fq + 1) * qw]
                        tt("dve", sl, sl, m[0:nk, 0:qw], ALU.mult, [R_W[x][buf], R_const], [R_W[x][buf]])

            def s3(u):
                fq, N, c0 = geo(u)
                buf = u % 2
                un = units[u]
                pbk = 4 + buf
                first = True
                for i in range(fq, nqb):
                    for x, kb in enumerate(un):
                        nk, s = kb["nk"], kb["slot"]
                        mm(bank(pbk)[0:qw, i * 65:i * 65 + 65], Wt[x][buf][0:nk, (i - fq) * qw:(i - fq + 1) * qw],
                           SVaug[0:nk, mp.vs(s), 0:65], first, x == len(un) - 1, [R_W[x][buf], mp.rk(s)], [RB[pbk]], skip=True)
                        first = False
                n = nqb - fq
                Pv = bank(pbk)[0:qw, 0:nqb * 65].rearrange("p (i c) -> p i c", c=65)
                tt("dve", tmpS[0:qw, fq:nqb, :], Pv[:, fq:nqb, 0:64],
                   stick[0:qw, fq:nqb].unsqueeze(2).to_broadcast([qw, n, 64]), ALU.mult, [RB[pbk], R_stick], [R_tmpS])
                tt("dve", Oacc[0:qw, fq:nqb, :], Oacc[0:qw, fq:nqb, :], tmpS[0:qw, fq:nqb, :], ALU.add, [R_tmpS, R_Oacc], [R_Oacc])
                if u < nu - 1:
                    ts("dve", fin[0:qw, 8:8 + n], Pv[:, fq:nqb, 64], -1.0, 1.0, ALU.mult, ALU.add, [RB[pbk]], [R_fin])
                    tt("dve", stick[0:qw, fq:nqb], stick[0:qw, fq:nqb], fin[0:qw, 8:8 + n], ALU.mult, [R_fin, R_stick], [R_stick])

            steps = []
            for t in range(nu + 2):
                def st(t=t):
                    if t < nu:
                        s1(t)
                    if 1 <= t <= nu:
                        s2(t - 1)
                    if t >= 2:
                        s3(t - 2)
                steps.append(st)
            steps.append(lambda: cp("pool", Os[0:qw, out_blk0:out_blk0 + nqb, hd * 64:(hd + 1) * 64], Oacc[0:qw, 0:nqb, :], [R_Oacc],
                                    [R_O[out_blk0 + i] for i in range(nqb)]))
            return steps

        kd_v = kd_o.rearrange("(s t) h d -> t s h d", t=128)
        vd_v = vd_o.rearrange("(s t) h d -> t s h d", t=128)
        sk_v = sk_o.rearrange("(s t) h d -> t s h d", t=128)
        sv_v = sv_o.rearrange("(s t) h d -> t s h d", t=128)

        def own_batch(g, hd):
            own = list(range(g * gq, (g + 1) * gq))
            return projA(own, 128, True, lambda si, o=own: hT[:, :, o[si] * 128:(o[si] + 1) * 128], [R_hT[s] for s in own], own,
                         dict(kd=kd_v[:, own[0]:own[-1] + 1, hd, :], vd=vd_v[:, own[0]:own[-1] + 1, hd, :],
                              sk=sk_v[:, own[0]:own[-1] + 1, hd, :], sv=sv_v[:, own[0]:own[-1] + 1, hd, :]), own[0], par=hd % 2)

        def oth_batch(g, hd):
            oth = [NOWN + j for j in range(g * gq, (g + 1) * gq)]
            return projA(oth, 128, False, lambda si, o=oth: hT[:, :, o[si] * 128:(o[si] + 1) * 128], [R_hT[s] for s in oth], None,
                         None, oth[0], par=hd % 2)

        def meta_batch(hd, bgfg=False):
            return projA([NB], N_META, False, lambda si: hT[:, :, NB * 128:NB * 128 + N_META], [R_hT[NB]], None,
                         dict(kd=kd_m[:, hd, :], vd=vd_m[:, hd, :], sk=sk_m[:, hd, :], sv=sv_m[:, hd, :]), NB, par=hd % 2,
                         bgfg=bgfg)

        def prompt_head(hd):
            par = hd % 2
            mp = MAPS[par]
            if hd == 0:
                run_serial(meta_batch(hd))
                run_serial(own_batch(0, hd))
                run_serial(oth_batch(0, hd))
            if NG == 1 and hd + 1 < NH:
                load_wh(hd + 1)
            for g in range(NG):
                j0 = g * gq
                rq = [mp.rq(j) for j in range(j0, j0 + gq)]
                q0 = mp.qc(j0)
                if g + 1 < NG:
                    bg = own_batch(g + 1, hd)
                elif hd + 1 < NH:
                    bg = own_batch(0, hd + 1)
                else:
                    bg = []
                if stop_after >= 2:
                    blocks = [dict(slot=NB, nk=N_META, fq=0, masks={}, rq=rq)]
                    for i in range(j0 + gq):
                        fq = max(i, j0) - j0
                        mo = {fq: Dd} if i >= j0 else {}
                        mo2 = {fq: MO} if i >= j0 else {}
                        blocks.append(dict(slot=i, nk=128, fq=fq, masks=mo, rq=rq))
                        blocks.append(dict(slot=NOWN + i, nk=128, fq=fq, masks=mo2, rq=rq))
                    weave(attn_diff(q0, gq, 128, blocks, j0, hd, par), bg)
                else:
                    run_serial(bg)
                if g + 1 < NG:
                    bg = oth_batch(g + 1, hd)
                elif hd + 1 < NH:
                    bg = oth_batch(0, hd + 1)
                    bg = bg + meta_batch(hd + 1, bgfg=True)
                else:
                    bg = []
                if g + 1 == NG - 1 and hd + 1 < NH:
                    load_wh(hd + 1)
                if stop_after >= 3:
                    units = []
                    for i in range(j0 + gq - 1, -1, -1):
                        fq = max(i, j0) - j0
                        mo = {fq: Ds} if i >= j0 else {}
                        mo2 = {fq: MO} if i >= j0 else {}
                        units.append([dict(slot=i, nk=128, fq=fq, masks=mo, rq=rq),
                                      dict(slot=NOWN + i, nk=128, fq=fq, masks=mo2, rq=rq)])
                    units.append([dict(slot=NB, nk=N_META, fq=0, masks={}, rq=rq)])
                    weave(attn_sb(q0, gq, 128, units, j0, hd, MA, MB, par), bg)
                else:
                    run_serial(bg)

        load_wh(0)
        for hd in range(NH if stop_after >= 1 else 0):
            prompt_head(hd)
        barrier()

        def sample_diff(hd):
            qw = DS
            batches = [list(range(b0, min(b0 + 16, PB))) for b0 in range(0, PB, 16)] + [[NB]]
            nbt = len(batches)
            started = set()

            def nkof(b):
                return DS if b == NB else 128

            def qk(t):
                bt = batches[t]
                buf = t % 2
                nk = nkof(bt[0])
                for c in range(2):
                    bk = buf * 2 + c
                    for k, b in enumerate(bt):
                        k0 = slot_tok0(b)
                        mm(bank(bk)[0:nk, k * qw:(k + 1) * qw], KT[c * 64:(c + 1) * 64, k0:k0 + nk], QT[c * 64:(c + 1) * 64, 0:qw],
                           k == 0, True, [R_K[b], R_Q[0]], [RB[bk]], skip=True)
                for c in range(2):
                    bk = buf * 2 + c
                    act(Et[c][buf][0:nk, 0:len(bt) * qw], bank(bk)[0:nk, 0:len(bt) * qw], AF.Exp, [RB[bk]], [R_E[c][buf]])

            def pv(t):
                bt = batches[t]
                buf = t % 2
                nk = nkof(bt[0])
                for k, b in enumerate(bt):
                    for c in range(2):
                        bk, col = 4, c * 129
                        st = bk not in started
                        started.add(bk)
                        mm(bank(bk)[0:qw, col:col + 129], Et[c][buf][0:nk, k * qw:(k + 1) * qw], Vaug[0:nk, b, 0:129],
                           st, (t == nbt - 1 and k == len(bt) - 1), [R_E[c][buf], R_K[b]], [RB[bk]], skip=True)

            for t in range(nbt + 1):
                if t < nbt:
                    qk(t)
                if t >= 1:
                    pv(t - 1)
            diff_finalize(1, qw, lambda i, c: (4, c * 129), NOWN, hd)

        def sample_sb(hd):
            qw = DS
            pairs = [(2 * i, 2 * i + 1) for i in range(PB // 2 - 1, -1, -1)]
            batches = [[(NB,)]] + [pairs[i:i + 7] for i in range(0, len(pairs), 7)]
            nbt = len(batches)
            S.op("dve", lambda e: e.memset(stick[0:qw, 0:1], 1.0), [], [R_stick])
            S.op("dve", lambda e: e.memset(Oacc[0:qw, 0:1, :], 0.0), [], [R_Oacc])

            def nkof(b):
                return DS if b == NB else 128

            def s1(t):
                bt = batches[t]
                buf = t % 2
                npair = len(bt)
                N = npair * qw
                nx = len(bt[0])
                nk = nkof(bt[0][0])
                for x in range(nx):
                    bk = buf * 2 + x
                    for k, un in enumerate(bt):
                        b = un[x]
                        pb_, k0 = sk_loc(b)
                        mm(bank(bk)[0:nk, k * qw:(k + 1) * qw], SKT2[pb_:pb_ + 64, k0:k0 + nk], SQT2[pb_:pb_ + 64, 0:qw], k == 0, True,
                           [R_K[b], R_Q[0]], [RB[bk]], skip=True)
                for x in range(nx):
                    bk = buf * 2 + x
                    act(bank(6)[0:nk, 0:N], bank(bk)[0:nk, 0:N], AF.Exp, [RB[bk]], [RB[6]])
                    act(Lt[x][buf][0:nk, 0:N], bank(6)[0:nk, 0:N], AF.Ln, [RB[6]], [R_L[x][buf]], bias=1.0)
                    if nx == 1:
                        sl = Lt[x][buf][0:nk, 0:qw]
                        tt("dve", sl, sl, Ds[0:nk, 0:qw], ALU.mult, [R_L[x][buf], R_const], [R_L[x][buf]])

            def s2(t):
                bt = batches[t]
                buf = t % 2
                N = len(bt) * qw
                nx = len(bt[0])
                nk = nkof(bt[0][0])
                for x in range(nx):
                    bk = buf * 2 + x
                    mm(bank(bk)[0:nk, 0:N], trineg[0:nk, 0:nk], Lt[x][buf][0:nk, 0:N], False, nx == 1, [R_L[x][buf], R_const], [RB[bk]], skip=True)
                    if nx == 2:
                        cm_ = onesneg if x == 0 else zero
                        mm(bank(bk)[0:nk, 0:N], cm_[0:nk, 0:nk], Lt[1 - x][buf][0:nk, 0:N], False, True,
                           [R_L[1 - x][buf], R_const], [RB[bk]], skip=True)
                for x in range(nx):
                    bk = buf * 2 + x
                    act(Wt[x][buf][0:nk, 0:N], bank(bk)[0:nk, 0:N], AF.Exp, [RB[bk]], [R_W[x][buf]])
                    if nx == 1:
                        sl = Wt[x][buf][0:nk, 0:qw]
                        tt("dve", sl, sl, Ds[0:nk, 0:qw], ALU.mult, [R_W[x][buf], R_const], [R_W[x][buf]])

            def s3(t):
                bt = batches[t]
                buf = t % 2
                nx = len(bt[0])
                nk = nkof(bt[0][0])
                pbk = 4 + buf
                first = True
                for k, un in enumerate(bt):
                    for x in range(nx):
                        b = un[x]
                        mm(bank(pbk)[0:qw, k * 65:k * 65 + 65], Wt[x][buf][0:nk, k * qw:(k + 1) * qw], SVaug[0:nk, b, 0:65],
                           first, x == nx - 1, [R_W[x][buf], R_K[b]], [RB[pbk]], skip=True)
                        first = False
                for k, un in enumerate(bt):
                    Pk = bank(pbk)[0:qw, k * 65:k * 65 + 65]
                    stt("dve", Oacc[0:qw, 0, :], Pk[:, 0:64], stick[0:qw, 0:1], Oacc[0:qw, 0, :], ALU.mult, ALU.add,
                        [RB[pbk], R_stick, R_Oacc], [R_Oacc])
                    if not (t == nbt - 1 and k == len(bt) - 1):
                        ts("dve", fin[0:qw, 8:9], Pk[:, 64:65], -1.0, 1.0, ALU.mult, ALU.add, [RB[pbk]], [R_fin])
                        tt("dve", stick[0:qw, 0:1], stick[0:qw, 0:1], fin[0:qw, 8:9], ALU.mult, [R_fin, R_stick], [R_stick])

            for t in range(nbt + 2):
                if t < nbt:
                    s1(t)
                if 1 <= t <= nbt:
                    s2(t - 1)
                if t >= 2:
                    s3(t - 2)
            cp("pool", Os[0:qw, NOWN:NOWN + 1, hd * 64:(hd + 1) * 64], Oacc[0:qw, 0:1, :], [R_Oacc], [R_O[NOWN]])

        if stop_after >= 4:
            AR.seek(OFF_P12)
            Kc2 = [AR.alloc([128, PB, 128], BF16) for _ in range(2)]
            SKc2 = [AR.alloc([128, PB, 128], BF16) for _ in range(2)]
            KT_B = AR.alloc([128, KW], BF16)
            Vaug_B = AR.alloc([128, NB + 1, 130], BF16)
            SKT2_B = AR.alloc([128, SKW], BF16)
            SVaug_B = AR.alloc([128, NB + 1, 66], BF16)
            assert AR.off <= OFF_P12 + 8 * NT * 2
            R_K_B = [Res(f"KB{s}") for s in range(NB + 1)]
            ksets = [(KT, Vaug, SKT2, SVaug, R_K), (KT_B, Vaug_B, SKT2_B, SVaug_B, R_K_B)]
            R_Kc2, R_SKc2 = [Res("Kc0"), Res("Kc1")], [Res("SKc0"), Res("SKc1")]
            for k in range(2):
                S.op("pool", lambda e, t=SKc2[k]: e.memset(t, 0.0), [], [R_SKc2[k]])
            S.op("pool", lambda e, t=Vaug_B[:, :, 128:130]: e.memset(t, 1.0), [], R_K_B)
            S.op("pool", lambda e, t=SVaug_B[:, :, 64:66]: e.memset(t, 1.0), [], R_K_B)
            ck_v = c_dk.rearrange("(b t) h d -> t b h d", t=128)
            cv_v = c_dv.rearrange("(b t) h d -> t b h d", t=128)
            csk_v = c_sk.rearrange("(b t) h d -> t b h d", t=128)
            csv_v = c_sv.rearrange("(b t) h d -> t b h d", t=128)
            HB = PB // 2

            def issue_loads(hd):
                k = hd % 2
                kt_, va_, skt_, sva_, rk_ = ksets[k]
                dma("pool", Kc2[k], ck_v[:, :, hd, :], [], [R_Kc2[k]])
                dma("pool", va_[:, 0:PB, 0:128], cv_v[:, :, hd, :], [], rk_[0:PB])
                dma("pool", SKc2[k][:, 0:HB, 0:64], csk_v[:, 0:HB, hd, :], [], [R_SKc2[k]])
                dma("pool", SKc2[k][:, HB:PB, 64:128], csk_v[:, HB:PB, hd, :], [], [R_SKc2[k]])
                dma("pool", sva_[:, 0:PB, 0:64], csv_v[:, :, hd, :], [], rk_[0:PB])

            load_wh(0)
            issue_loads(0)
            for hd in range(NH):
                if hd + 1 < NH:
                    issue_loads(hd + 1)
                k2 = hd % 2
                KT, Vaug, SKT2, SVaug, R_K = ksets[k2]
                Kc, SKc, R_Kc, R_SKc = Kc2[k2], SKc2[k2], R_Kc2[k2], R_SKc2[k2]
                bgs = projA([NB], DS, True, lambda si: hTs[:, :, :], [R_hT[NB + 1]], [0],
                            dict(kd=kd_s[:, hd, :], vd=vd_s[:, hd, :], sk=sk_s[:, hd, :], sv=sv_s[:, hd, :]), NB + 1)
                if hd + 1 < NH:
                    load_wh(hd + 1)
                tsteps = []
                for b0 in range(0, PB, 4):
                    def st_k(b0=b0, KT=KT, Kc=Kc, R_Kc=R_Kc, R_K=R_K):
                        bi = 2 + (b0 // 4) % 2
                        pT = bankb(bi)
                        for k in range(4):
                            tr(pT[:, k * 128:(k + 1) * 128], Kc[:, b0 + k, :], [R_Kc], [RB[bi]])
                        cp("act", KT[:, b0 * 128:(b0 + 4) * 128], pT[:, 0:512], [RB[bi]], R_K[b0:b0 + 4])
                    tsteps.append(st_k)
                for b0 in range(0, PB, 4):
                    def st_s(b0=b0, SKT2=SKT2, SKc=SKc, R_SKc=R_SKc, R_K=R_K):
                        bi = 2 + (b0 // 4) % 2
                        pT = bankb(bi)
                        for k in range(4):
                            tr(pT[:, k * 128:(k + 1) * 128], SKc[:, b0 + k, :], [R_SKc], [RB[bi]])
                        pb_, c0 = sk_loc(b0)
                        cp("dve", SKT2[pb_:pb_ + 64, c0:c0 + 512], pT[pb_:pb_ + 64, 0:512], [RB[bi]], R_K[b0:b0 + 4])
                    tsteps.append(st_s)
                weave(tsteps, bgs)
                sample_diff(hd)
                sample_sb(hd)
            barrier()

        if stop_after >= 5:
            OFF_MG = OFF_P12
            AR.seek(OFF_MG)
            mg = AR.alloc([128, NOWN + 1, 1024], BF16)
            R_mg = [Res(f"mg{j}") for j in range(NOWN + 1)]
            OFF_X1 = AR.off
            x_st = [AR.alloc([128, DM], F32) for _ in range(2)]
            h_sb = [AR.alloc([128, DM], BF16) for _ in range(2)]
            junk = AR.alloc([128, DM], BF16)
            gb = AR.alloc([128, DM], F32)
            st1 = AR.alloc([128, 8], F32)
            hTb = [AR.alloc([128, 8, 128], BF16) for _ in range(2)]
            odT = [AR.alloc([128, 8, 128], BF16) for _ in range(2)]
            osT = [AR.alloc([128, 4, 128], BF16) for _ in range(2)]
            egd = [AR.alloc([128, 1024], F32) for _ in range(2)]
            egs = [AR.alloc([128, 1024], F32) for _ in range(2)]
            Wd = AR.alloc([128, 8, 1024], BF16)
            Ws = AR.alloc([128, 4, 1024], BF16)
            Wg = AR.alloc([128, 8, 2048], BF16)
            R_x = [Res("x0"), Res("x1")]
            R_h = [Res("h0"), Res("h1")]
            R_junk, R_gb = Res("junk"), Res("gb")
            R_st = [Res("st0"), Res("st1")]
            R_hTb, R_odT, R_osT, R_egd, R_egs = [[Res(f"{n}{k}") for k in range(2)] for n in ("hTb", "odT", "osT", "egd", "egs")]
            R_Wd, R_Ws = Res("Wd"), Res("Ws")
            R_Wg = [Res(f"Wg{q}") for q in range(4)]
            p1_cnt[0] = 0
            dma("sp", gb, g_mix.partition_broadcast(128), [], [R_gb])
            for q_ in range(4):
                dma("pool", Wg[:, :, q_ * 512:(q_ + 1) * 512], w_in_v[:, :, 4608 + q_ * 512:4608 + (q_ + 1) * 512], [], [R_Wg[q_]])
            dma("pool", Wd, w_do.rearrange("(c p) n -> p c n", p=128), [], [R_Wd])
            dma("pool", Ws, w_so.rearrange("(c p) n -> p c n", p=128), [], [R_Ws])

            def blk_src(j):
                if j < NOWN:
                    return x_slots[j * 128:(j + 1) * 128, :], 128
                return xs[:, :], DS

            def c1a_A1(j):
                src, nt = blk_src(j)
                k = j % 2
                xb, hb = x_st[k], h_sb[k]
                dma("sp", xb[0:nt, :], src, [], [R_x[k]])
                stc = st1[:, 4 * k:4 * k + 4]
                act(junk[0:nt, :], xb[0:nt, :], AF.Square, [R_x[k]], [R_junk, R_st[k]], accum=stc[0:nt, 0:1])
                act(stc[0:nt, 1:2], stc[0:nt, 0:1], AF.Ln, [R_st[k]], [R_st[k]], scale=1.0 / DM, bias=EPS)
                act(stc[0:nt, 2:3], stc[0:nt, 1:2], AF.Exp, [R_st[k]], [R_st[k]], scale=-0.5)
                stt("dve", hb[0:nt, :], xb[0:nt, :], stc[0:nt, 2:3], gb[0:nt, :], ALU.mult, ALU.mult,
                    [R_x[k], R_st[k], R_gb], [R_h[k]])

            def c1a_Th(j):
                src, nt = blk_src(j)
                k = j % 2
                pT = bankb(7)
                for ch in range(8):
                    tr(pT[:, ch * 128:ch * 128 + nt], h_sb[k][0:nt, ch * 128:(ch + 1) * 128], [R_h[k]], [RB[7]])
                cp("dve", hTb[k][:, :, 0:nt], pT.rearrange("p (c t) -> p c t", t=128)[:, :, 0:nt], [RB[7]], [R_hTb[k]])

            def c1a_G(j):
                src, nt = blk_src(j)
                k = j % 2
                for gi, (b0, dst, R_dst) in enumerate([(0, egd[k], R_egd[k]), (2, egs[k], R_egs[k])]):
                    for half in range(2):
                        for ch in range(8):
                            mm(bank(b0 + half)[0:nt, :], hTb[k][:, ch, 0:nt], Wg[:, ch, gi * 1024 + half * 512:gi * 1024 + (half + 1) * 512],
                               ch == 0, ch == 7, [R_hTb[k], R_Wg[gi * 2 + half]], [RB[b0 + half]])
                    gv = PS[0:nt, b0 * 512:(b0 + 2) * 512]
                    act(dst[0:nt, :], gv, AF.Exp, [RB[b0], RB[b0 + 1]], [R_dst], scale=-1.0)
                    act(dst[0:nt, :], dst[0:nt, :], AF.Ln, [R_dst], [R_dst], bias=1.0)
                    act(dst[0:nt, :], dst[0:nt, :], AF.Exp, [R_dst], [R_dst], scale=-1.0)

            def c1a_To(j):
                src, nt = blk_src(j)
                k = j % 2
                pT = bankb(7)
                for ch in range(8):
                    tr(pT[:, ch * 128:ch * 128 + nt], Od[0:nt, j, ch * 128:(ch + 1) * 128], [R_O[j]], [RB[7]])
                cp("act", odT[k][:, :, 0:nt], pT.rearrange("p (c t) -> p c t", t=128)[:, :, 0:nt], [RB[7]], [R_odT[k]])
                for ch in range(4):
                    tr(pT[:, ch * 128:ch * 128 + nt], Os[0:nt, j, ch * 128:(ch + 1) * 128], [R_O[j]], [RB[7]])
                cp("dve", osT[k][:, :, 0:nt], pT[:, 0:512].rearrange("p (c t) -> p c t", t=128)[:, :, 0:nt], [RB[7]], [R_osT[k]])

            def c1a_M(j):
                src, nt = blk_src(j)
                k = j % 2
                for half in range(2):
                    for ch in range(8):
                        mm(bank(4 + half)[0:nt, :], odT[k][:, ch, 0:nt], Wd[:, ch, half * 512:(half + 1) * 512], ch == 0, ch == 7,
                           [R_odT[k], R_Wd], [RB[4 + half]])
                tt("dve", egd[k][0:nt, :], PS[0:nt, 2048:3072], egd[k][0:nt, :], ALU.mult, [RB[4], RB[5], R_egd[k]], [R_egd[k]])
                for half in range(2):
                    for ch in range(4):
                        mm(bank(6)[0:nt, :], osT[k][:, ch, 0:nt], Ws[:, ch, half * 512:(half + 1) * 512], ch == 0, ch == 3,
                           [R_osT[k], R_Ws], [RB[6]])
                    hs_ = slice(half * 512, (half + 1) * 512)
                    tt("dve", egs[k][0:nt, hs_], bank(6)[0:nt, :], egs[k][0:nt, hs_], ALU.mult, [RB[6], R_egs[k]], [R_egs[k]])
                tt("pool", mg[0:nt, j, :], egd[k][0:nt, :], egs[k][0:nt, :], ALU.add, [R_egd[k], R_egs[k]], [R_mg[j]])

            c1a_A1(0)
            c1a_A1(1)
            c1a_Th(0)
            c1a_To(0)
            for j in range(NOWN + 1):
                c1a_G(j)
                if j + 1 <= NOWN:
                    c1a_Th(j + 1)
                c1a_M(j)
                if j + 1 <= NOWN:
                    c1a_To(j + 1)
                if j + 2 <= NOWN:
                    c1a_A1(j + 2)
            barrier()

            AR.seek(OFF_X1)
            x1 = AR.alloc([128, NOWN + 1, 1024], F32)
            h2T = AR.alloc([128, 8, NOWN * 128 + DS], BF16)
            OFF_END = AR.off
            AR.seek(OFF_O if OFF_O + 38000 <= OFF_MG else OFF_END)
            x_st = [AR.alloc([128, DM], F32) for _ in range(2)]
            h_sb = [AR.alloc([128, DM], BF16) for _ in range(2)]
            junk = AR.alloc([128, DM], BF16)
            gb = AR.alloc([128, DM], F32)
            st1 = AR.alloc([128, 8], F32)
            mT = [AR.alloc([128, 8, 128], BF16) for _ in range(2)]
            Wo = AR.alloc([128, 8, 1024], BF16)
            assert AR.off <= OFF_MG or AR.off > OFF_END
            R_x1 = [Res(f"x1_{j}") for j in range(NOWN + 1)]
            R_h2T = [Res(f"h2T{j}") for j in range(NOWN + 1)]
            R_x = [Res("x0"), Res("x1")]
            R_h = [Res("h0"), Res("h1")]
            R_junk, R_gb = Res("junk"), Res("gb")
            R_st = [Res("st0"), Res("st1")]
            R_mT, R_Wo = [Res("mT0"), Res("mT1")], Res("Wo")
            p1_cnt[0] = 0
            dma("sp", gb, g_ffn.partition_broadcast(128), [], [R_gb])
            dma("pool", Wo, w_o.rearrange("(c p) n -> p c n", p=128), [], [R_Wo])
            def c1b_Tm(j):
                src, nt = blk_src(j)
                k = j % 2
                pT = bankb(6 + k)
                for ch in range(8):
                    tr(pT[:, ch * 128:ch * 128 + nt], mg[0:nt, j, ch * 128:(ch + 1) * 128], [R_mg[j]], [RB[6 + k]])
                cp("act", mT[k][:, :, 0:nt], pT.rearrange("p (c t) -> p c t", t=128)[:, :, 0:nt], [RB[6 + k]], [R_mT[k]])

            def c1b_X(j):
                src, nt = blk_src(j)
                dma("sp", x_st[j % 2][0:nt, :], src, [], [R_x[j % 2]])

            def c1b_Mo(j):
                src, nt = blk_src(j)
                k = j % 2
                b0 = 2 * k
                for half in range(2):
                    for ch in range(8):
                        mm(bank(b0 + half)[0:nt, :], mT[k][:, ch, 0:nt], Wo[:, ch, half * 512:(half + 1) * 512], ch == 0, ch == 7,
                           [R_mT[k], R_Wo], [RB[b0 + half]])
                tt("dve", x1[0:nt, j, :], PS[0:nt, b0 * 512:(b0 + 2) * 512], x_st[k][0:nt, :], ALU.add,
                   [RB[b0], RB[b0 + 1], R_x[k]], [R_x1[j]])
                if j + 2 <= NOWN:
                    c1b_X(j + 2)
                stc = st1[:, 4 * k:4 * k + 4]
                act(junk[0:nt, :], x1[0:nt, j, :], AF.Square, [R_x1[j]], [R_junk, R_st[k]], accum=stc[0:nt, 0:1])
                act(stc[0:nt, 1:2], stc[0:nt, 0:1], AF.Ln, [R_st[k]], [R_st[k]], scale=1.0 / DM, bias=EPS)
                act(stc[0:nt, 2:3], stc[0:nt, 1:2], AF.Exp, [R_st[k]], [R_st[k]], scale=-0.5)
                stt("dve", h_sb[k][0:nt, :], x1[0:nt, j, :], stc[0:nt, 2:3], gb[0:nt, :], ALU.mult, ALU.mult,
                    [R_x1[j], R_st[k], R_gb], [R_h[k]])

            def c1b_Tn(j):
                src, nt = blk_src(j)
                kk = j % 2
                pT = bankb(4 + kk)
                for ch in range(8):
                    tr(pT[:, ch * 128:ch * 128 + nt], h_sb[kk][0:nt, ch * 128:(ch + 1) * 128], [R_h[kk]], [RB[4 + kk]])
                cp("act", h2T[:, :, j * 128:j * 128 + nt], pT.rearrange("p (c t) -> p c t", t=128)[:, :, 0:nt], [RB[4 + kk]], [R_h2T[j]])

            c1b_X(0)
            c1b_X(1)
            c1b_Tm(0)
            c1b_Mo(0)
            c1b_Tm(1)
            for j in range(NOWN + 1):
                if j + 2 <= NOWN:
                    c1b_Tm(j + 2)
                if j + 1 <= NOWN:
                    c1b_Mo(j + 1)
                c1b_Tn(j)
            barrier()

            AR.seek(OFF_O if OFF_O + 46000 <= OFF_X1 else OFF_END)
            W1c = [AR.alloc([128, 8, 512], BF16) for _ in range(2)]
            W2c = [AR.alloc([128, 4, 1024], BF16) for _ in range(2)]
            hid = [AR.alloc([128, 4, 512], BF16) for _ in range(2)]
            r_t = [AR.alloc([128, 512], F32) for _ in range(2)]
            assert AR.off <= OFF_X1 or AR.off > OFF_END
            R_W1, R_W2 = [Res("W1a"), Res("W1b")], [Res("W2a"), Res("W2b")]
            R_hid, R_r = [Res("hid0"), Res("hid1")], [Res("r0"), Res("r1")]
            w1v = w_f1.rearrange("(c p) n -> p c n", p=128)
            w2v = w_f2.rearrange("(c p) n -> p c n", p=128)
            tgs = []
            for b0 in range(0, NOWN, 4):
                nb_ = min(4, NOWN - b0)
                tgs.append((b0 * 128, nb_ * 128, [(b0 + i, 128) for i in range(nb_)]))
            tgs.append((NOWN * 128, DS, [(NOWN, DS)]))
            NCH = 8
            rr = [0]

            def load_ffn(c):
                k = c % 2
                dma("pool", W1c[k], w1v[:, :, c * 512:(c + 1) * 512], [], [R_W1[k]])
                dma("pool", W2c[k], w2v[:, c * 4:(c + 1) * 4, :], [], [R_W2[k]])

            items = [(c, ti) for c in range(NCH) for ti in range(len(tgs))]

            def ffn1(idx):
                c, ti = items[idx]
                k = c % 2
                t0, ntok, blks = tgs[ti]
                hb_ = idx % 2
                rj = [R_h2T[b] for b, _ in blks]
                for sub in range(4):
                    bk = sub % 2
                    for ch in range(8):
                        mm(bank(bk)[:, 0:ntok], W1c[k][:, ch, sub * 128:(sub + 1) * 128], h2T[:, ch, t0:t0 + ntok], ch == 0, ch == 7,
                           [R_W1[k]] + rj, [RB[bk]])
                    rb_ = sub % 2
                    act(r_t[rb_][:, 0:ntok], bank(bk)[:, 0:ntok], AF.Relu, [RB[bk]], [R_r[rb_]])
                    tt("pool" if sub % 2 else "dve", hid[hb_][:, sub, 0:ntok], r_t[rb_][:, 0:ntok], r_t[rb_][:, 0:ntok], ALU.mult,
                       [R_r[rb_]], [R_hid[hb_]])

            def ffn2(idx):
                c, ti = items[idx]
                k = c % 2
                t0, ntok, blks = tgs[ti]
                hb_ = idx % 2
                for bi_, (b, nt) in enumerate(blks):
                    yb = 2 + 2 * (bi_ % 2)
                    for half in range(2):
                        for sub in range(4):
                            mm(bank(yb + half)[0:nt, :], hid[hb_][:, sub, bi_ * 128:bi_ * 128 + nt],
                               W2c[k][:, sub, half * 512:(half + 1) * 512], sub == 0, sub == 3, [R_hid[hb_], R_W2[k]], [RB[yb + half]])
                    tt("dve", x1[0:nt, b, :], x1[0:nt, b, :], PS[0:nt, yb * 512:(yb + 2) * 512], ALU.add,
                       [RB[yb], RB[yb + 1], R_x1[b]], [R_x1[b]])

            load_ffn(0)
            if NCH > 1:
                load_ffn(1)
            ffn1(0)
            for idx in range(len(items)):
                if idx + 1 < len(items):
                    ffn1(idx + 1)
                ffn2(idx)
                c, ti = items[idx]
                if ti == len(tgs) - 1 and c + 2 < NCH:
                    load_ffn(c + 2)
            for j in range(NOWN):
                dma("sp", y_own[j * 128:(j + 1) * 128, :], x1[:, j, :], [R_x1[j]], [])
            dma("sp", y_s[:, :], x1[0:DS, NOWN, :], [R_x1[NOWN]], [])

        S.finish()
    return nc


ROT_DIM = 16
ROPE_THETA = 500000.0


def _rope_tables(pos):
    inv = ROPE_THETA ** (-np.arange(0, ROT_DIM, 2, dtype=np.float32) / ROT_DIM)
    ang = pos.astype(np.float32)[:, None] * inv[None, :].astype(np.float32)
    return np.cos(ang).astype(np.float32), np.sin(ang).astype(np.float32)


_NC_CACHE = {}
STOP_AFTER = 99


def kernel(x_prompt, x_sample, cache_diff_k, cache_diff_v, cache_sb_k, cache_sb_v, meta_tokens,
           g_mix, w_in, q_norm_g, k_norm_g, lam_q1, lam_k1, lam_q2, lam_k2, sub_g,
           w_diff_out, w_sb_out, w_out, g_ffn, w_ff1, w_ff2):
    f = lambda a: np.ascontiguousarray(np.asarray(a, dtype=np.float32))
    x_prompt, x_sample, meta_tokens = f(x_prompt), f(x_sample), f(meta_tokens)
    B, SEQ, _ = x_prompt.shape
    NB = SEQ // 128
    PAST = cache_diff_k.shape[2]
    PB = PAST // 128
    NOWN = NB // 2
    NSLOT = NB + 2
    key = (NB, PB, STOP_AFTER)
    if key not in _NC_CACHE:
        _NC_CACHE[key] = build(NB, PB, STOP_AFTER)
    nc = _NC_CACHE[key]

    ii = np.arange(128)
    Dd = ((ii[:, None] // 64) <= (ii[None, :] // 64)).astype(np.float32)
    Dsm = (ii[:, None] < ii[None, :]).astype(np.float32)
    trineg = -(ii[:, None] >= ii[None, :]).astype(np.float32)
    ones = np.ones((128, 128), np.float32)
    shared = dict(
        w_in=f(w_in)[0], w_do=f(w_diff_out)[0], w_so=f(w_sb_out)[0], w_o=f(w_out)[0], w_f1=f(w_ff1)[0], w_f2=f(w_ff2)[0],
        g_mix=f(g_mix)[0], g_ffn=f(g_ffn)[0], qng=f(q_norm_g)[0], kng=f(k_norm_g)[0], subg=f(sub_g)[0],
        lamv=np.stack([f(lam_q1)[0], f(lam_k1)[0], f(lam_q2)[0], f(lam_k2)[0]]),
    )
    in_maps = []
    for c in range(8):
        b, p = c // 2, c % 2
        own = [2 * j + p for j in range(NOWN)]
        oth = [2 * j + 1 - p for j in range(NOWN)]
        order = own + oth
        xb = x_prompt[b].reshape(NB, 128, DM)
        x_slots = np.concatenate([xb[order].reshape(NB * 128, DM), meta_tokens], axis=0)
        cs = np.zeros((128, NSLOT, 8), np.float32)
        sn = np.zeros((128, NSLOT, 8), np.float32)
        for s, g in enumerate(order):
            cc, ss = _rope_tables(N_META + 128 * g + np.arange(128))
            cs[:, s], sn[:, s] = cc, ss
        cc, ss = _rope_tables(np.arange(N_META))
        cs[:N_META, NB], sn[:N_META, NB] = cc, ss
        cc, ss = _rope_tables(PAST + np.arange(DS))
        cs[:DS, NB + 1], sn[:DS, NB + 1] = cc, ss
        a_, b_ = (1.0, 0.0) if p == 0 else (0.0, 1.0)
        cmat = np.stack([trineg, Dd, Dsm, ones * float(p), -a_ * ones, -b_ * ones, -ones, 0 * ones]).astype(np.float32)
        m = dict(shared)
        m.update(x_slots=np.ascontiguousarray(x_slots), xs=x_sample[c],
                 c_dk=f(cache_diff_k)[0, c], c_dv=f(cache_diff_v)[0, c], c_sk=f(cache_sb_k)[0, c], c_sv=f(cache_sb_v)[0, c],
                 cs_t=cs, sn_t=sn, cmat=cmat)
        in_maps.append(m)
    res = run_bass_kernel_spmd(nc, in_maps, core_ids=list(range(8)))
    R = res.results
    T = SEQ + N_META
    y_prompt = np.zeros((B, SEQ, DM), np.float32)
    y_sample = np.zeros((8, DS, DM), np.float32)
    pk = np.zeros((1, B, T, NH, 128), np.float32)
    pv = np.zeros((1, B, T, NH, 128), np.float32)
    psk = np.zeros((1, B, T, NH, 64), np.float32)
    psv = np.zeros((1, B, T, NH, 64), np.float32)
    sk_ = np.zeros((1, 8, DS, NH, 128), np.float32)
    sv_ = np.zeros((1, 8, DS, NH, 128), np.float32)
    ssk = np.zeros((1, 8, DS, NH, 64), np.float32)
    ssv = np.zeros((1, 8, DS, NH, 64), np.float32)
    for c in range(8):
        b, p = c // 2, c % 2
        r = R[c]
        for j in range(NOWN):
            g = 2 * j + p
            y_prompt[b, 128 * g:128 * (g + 1)] = r["y_own"][128 * j:128 * (j + 1)]
            sl = slice(N_META + 128 * g, N_META + 128 * (g + 1))
            pk[0, b, sl] = r["kd_o"][128 * j:128 * (j + 1)]
            pv[0, b, sl] = r["vd_o"][128 * j:128 * (j + 1)]
            psk[0, b, sl] = r["sk_o"][128 * j:128 * (j + 1)]
            psv[0, b, sl] = r["sv_o"][128 * j:128 * (j + 1)]
        if p == 0:
            pk[0, b, :N_META] = r["kd_m"]
            pv[0, b, :N_META] = r["vd_m"]
            psk[0, b, :N_META] = r["sk_m"]
            psv[0, b, :N_META] = r["sv_m"]
        y_sample[c] = r["y_s"]
        sk_[0, c], sv_[0, c], ssk[0, c], ssv[0, c] = r["kd_s"], r["vd_s"], r["sk_s"], r["sv_s"]
    return (y_prompt, y_sample, pk, pv, psk, psv, sk_, sv_, ssk, ssv)
```

```python
from contextlib import ExitStack

import numpy as np
import concourse.bass as bass
import concourse.mybir as mybir
from concourse.bass_utils import run_bass_kernel_spmd

F32 = mybir.dt.float32
BF16 = mybir.dt.bfloat16
ALU = mybir.AluOpType
AF = mybir.ActivationFunctionType
AX = mybir.AxisListType


class Res:
    __slots__ = ("name", "w", "r", "excl", "wl")

    def __init__(self, name, excl=False):
        self.name = name
        self.wl = []
        self.excl = excl
        self.w = None
        self.r = []


class Sched:
    ENG = ("pe", "act", "dve", "pool", "sp")
    NDMA = {"sp": 12, "pool": 8, "act": 6}

    def __init__(self, nc, es):
        self.nc = nc
        self.es = es
        self.ops = {e: [] for e in self.ENG}
        self.cnt = {e: 0 for e in self.ENG}
        self.waited = {e: {} for e in self.ENG}
        self.sem = {}
        for e in self.ENG:
            self.sem[e] = es.enter_context(nc.semaphore("c_" + e))
        self.dsem = {}
        self.dcnt = {}
        self.drr = {}
        for q, n in self.NDMA.items():
            self.dsem[q] = []
            for i in range(n):
                nm = f"d_{q}{i}"
                self.sem[nm] = es.enter_context(nc.semaphore(nm))
                self.dsem[q].append(nm)
                self.dcnt[nm] = 0
            self.drr[q] = 0
        self.nwaits = 0
        self.pending = {}

    def sbuf(self, name, shape, dtype):
        return self.es.enter_context(self.nc.sbuf_tensor(name, shape, dtype))

    def psum(self, name, shape, dtype):
        return self.es.enter_context(self.nc.psum_tensor(name, shape, dtype))

    def _collect(self, eng, reads, writes, is_dma=False):
        deps = {}

        def add(tk, raw):
            if tk is None:
                return
            s, v = tk
            if s == eng and eng == "pe":
                return
            if deps.get(s, 0) < v:
                deps[s] = v

        for r in reads:
            add(r.w, True)
            for t in r.wl:
                add(t, True)
            if r.excl:
                for t in r.r:
                    add(t, False)
        for w in writes:
            if not (is_dma and w.w is not None and w.w[0].startswith("d_")):
                add(w.w, False)
                for t in w.wl:
                    add(t, False)
            for t in w.r:
                add(t, False)
        waits = []
        wd = self.waited[eng]
        for s, v in deps.items():
            if wd.get(s, 0) < v:
                wd[s] = v
                waits.append((s, v))
        self.nwaits += len(waits)
        return waits

    @staticmethod
    def _update(tk, reads, writes, is_dma=False):
        for r in reads:
            r.r.append(tk)
        for w in writes:
            if is_dma and w.w is not None and w.w[0].startswith("d_"):
                w.wl = w.wl + [w.w]
            else:
                w.wl = []
            w.w = tk
            w.r = []

    def op(self, eng, fn, reads=(), writes=()):
        waits = self.pending.pop(eng, []) + self._collect(eng, reads, writes)
        self.cnt[eng] += 1
        tk = (eng, self.cnt[eng])
        self.ops[eng].append((waits, fn, (eng, 1)))
        self._update(tk, reads, writes)
        return tk

    def dma(self, q, fn, reads=(), writes=()):
        names = self.dsem[q]
        nm = names[self.drr[q] % len(names)]
        self.drr[q] += 1
        waits = self.pending.pop(q, []) + self._collect(q, reads, writes, is_dma=True)
        prev = 16 * self.dcnt[nm]
        if prev and self.waited[q].get(nm, 0) < prev:
            self.waited[q][nm] = prev
            waits.append((nm, prev))
        self.dcnt[nm] += 1
        tk = (nm, 16 * self.dcnt[nm])
        self.ops[q].append((waits, fn, (nm, 16)))
        self._update(tk, reads, writes, is_dma=True)
        return tk

    def barrier(self):
        allt = [(e, c) for e, c in self.cnt.items() if c] + [(nm, 16 * c) for nm, c in self.dcnt.items() if c]
        for e in self.ENG:
            waits = []
            wd = self.waited[e]
            for sname, v in allt:
                if sname == e and e == "pe":
                    continue
                if wd.get(sname, 0) < v:
                    wd[sname] = v
                    waits.append((sname, v))
            self.pending[e] = self.pending.get(e, []) + waits

    def finish(self):
        final = [(nm, 16 * c) for nm, c in self.dcnt.items() if c]
        sem = self.sem

        def replay(name, e, tail=()):
            for waits, fn, inc in self.ops[name]:
                for s, v in waits:
                    e.wait_ge(sem[s], v)
                ins = fn(e)
                ins.then_inc(sem[inc[0]], inc[1])
            for s, v in tail:
                e.wait_ge(sem[s], v)

        with self.nc.Block() as block:
            @block.tensor
            def _(e):
                replay("pe", e)

            @block.scalar
            def _(e):
                replay("act", e)

            @block.vector
            def _(e):
                replay("dve", e)

            @block.gpsimd
            def _(e):
                replay("pool", e)

            @block.sync
            def _(e):
                replay("sp", e, tail=final)


DM = 1024
NH = 8
N_META = 16
DS = 32
EPS = 1e-6
LAM_INIT = 0.2
IN_COLS = 6656
GQ = 4


class Arena:
    def __init__(self, S, nbytes):
        self.n = nbytes
        self.t = S.sbuf("arena", [128, nbytes // 2], BF16)
        self.off = 0

    def seek(self, off):
        self.off = off

    def alloc(self, shape, dtype):
        esz = 4 if dtype == F32 else 2
        n = 1
        for d in shape[1:]:
            n *= d
        nb = (n * esz + 31) // 32 * 32
        assert self.off + nb <= self.n, f"arena overflow {self.off}+{nb}>{self.n}"
        v = self.t[:, self.off // 2:(self.off + nb) // 2]
        self.off += nb
        if dtype == F32:
            v = v.bitcast(F32)
        v = v[:, 0:n]
        if len(shape) == 3:
            v = v.rearrange("p (a b) -> p a b", b=shape[2])
        elif len(shape) == 4:
            v = v.rearrange("p (a b c) -> p a b c", b=shape[2], c=shape[3])
        return v


def build(NB, PB, stop_after=99):
    assert NB % 2 == 0 and PB == NB
    NOWN = NB // 2
    gq = min(GQ, NOWN)
    assert NOWN % gq == 0
    NG = NOWN // gq
    NT = NB * 128 + N_META
    NSLOT = NB + 2
    HALF = NOWN * 128
    SKW = HALF + 128
    KW = NB * 128 + 32

    nc = bass.Bass("TRN2", target_bir_lowering=False)

    def din(name, shape):
        return nc.dram_tensor(name, shape, F32, kind="ExternalInput").ap()

    def dout(name, shape):
        return nc.dram_tensor(name, shape, F32, kind="ExternalOutput").ap()

    x_slots = din("x_slots", [NT, DM])
    xs = din("xs", [DS, DM])
    c_dk = din("c_dk", [PB * 128, NH, 128])
    c_dv = din("c_dv", [PB * 128, NH, 128])
    c_sk = din("c_sk", [PB * 128, NH, 64])
    c_sv = din("c_sv", [PB * 128, NH, 64])
    w_in = din("w_in", [DM, IN_COLS])
    w_do = din("w_do", [DM, DM])
    w_so = din("w_so", [512, DM])
    w_o = din("w_o", [DM, DM])
    w_f1 = din("w_f1", [DM, 4 * DM])
    w_f2 = din("w_f2", [4 * DM, DM])
    g_mix = din("g_mix", [DM])
    g_ffn = din("g_ffn", [DM])
    qng = din("qng", [64])
    kng = din("kng", [64])
    lamv = din("lamv", [4, 64])
    subg = din("subg", [128])
    cs_t = din("cs_t", [128, NSLOT, 8])
    sn_t = din("sn_t", [128, NSLOT, 8])
    cmat = din("cmat", [8, 128, 128])

    y_own = dout("y_own", [NOWN * 128, DM])
    y_s = dout("y_s", [DS, DM])
    kd_o = dout("kd_o", [NOWN * 128, NH, 128])
    vd_o = dout("vd_o", [NOWN * 128, NH, 128])
    sk_o = dout("sk_o", [NOWN * 128, NH, 64])
    sv_o = dout("sv_o", [NOWN * 128, NH, 64])
    kd_m = dout("kd_m", [N_META, NH, 128])
    vd_m = dout("vd_m", [N_META, NH, 128])
    sk_m = dout("sk_m", [N_META, NH, 64])
    sv_m = dout("sv_m", [N_META, NH, 64])
    kd_s = dout("kd_s", [DS, NH, 128])
    vd_s = dout("vd_s", [DS, NH, 128])
    sk_s = dout("sk_s", [DS, NH, 64])
    sv_s = dout("sv_s", [DS, NH, 64])

    es = ExitStack()
    with es:
        S = Sched(nc, es)
        AR = Arena(S, 204800)
        PS = S.psum("PS", [128, 4096], F32)
        RB = [Res(f"bank{i}", excl=True) for i in range(8)]

        def bank(i):
            return PS[:, i * 512:(i + 1) * 512]

        def bankb(i):
            return PS[:, i * 512:(i + 1) * 512].bitcast(BF16)

        def mm(out, lhsT, rhs, start, stop, reads, writes, skip=False):
            S.op("pe", lambda e, o=out, l=lhsT, r=rhs, a=start, b=stop, k=skip:
                 e.matmul(o, lhsT=l, rhs=r, start=a, stop=b, skip_group_check=k), reads, writes)

        def tr(out, in_, reads, writes):
            n = in_.shape[0]
            S.op("pe", lambda e, o=out, i=in_, n=n: e.transpose(out=o, in_=i, identity=ident[0:n, 0:n]),
                 list(reads) + [R_const], writes)

        def act(out, in_, func, reads, writes, scale=1.0, bias=0.0, accum=None):
            S.op("act", lambda e, o=out, i=in_, f=func, s=scale, b=bias, a=accum:
                 e.activation(out=o, in_=i, func=f, scale=s, bias=b, accum_out=a), reads, writes)

        def tt(eng, out, in0, in1, op, reads, writes):
            S.op(eng, lambda e, o=out, a=in0, b=in1, p=op: e.tensor_tensor(out=o, in0=a, in1=b, op=p), reads, writes)

        def ts(eng, out, in0, s1, s2, op0, op1, reads, writes):
            if s2 is None:
                S.op(eng, lambda e, o=out, a=in0, x=s1, p=op0: e.tensor_scalar(out=o, in0=a, scalar1=x, scalar2=None, op0=p), reads, writes)
            else:
                S.op(eng, lambda e, o=out, a=in0, x=s1, y=s2, p=op0, q=op1:
                     e.tensor_scalar(out=o, in0=a, scalar1=x, scalar2=y, op0=p, op1=q), reads, writes)

        def stt(eng, out, in0, scalar, in1, op0, op1, reads, writes):
            S.op(eng, lambda e, o=out, a=in0, s=scalar, b=in1, p=op0, q=op1:
                 e.scalar_tensor_tensor(out=o, in0=a, scalar=s, in1=b, op0=p, op1=q), reads, writes)

        def cp(eng, out, in_, reads, writes):
            if eng == "act":
                act(out, in_, AF.Copy, reads, writes)
            else:
                S.op(eng, lambda e, o=out, i=in_: e.tensor_copy(out=o, in_=i), reads, writes)

        def dma(q, out, in_, reads, writes):
            S.dma(q, lambda e, o=out, i=in_: e.dma_start(out=o, in_=i), reads, writes)

        def barrier():
            S.barrier()

        R_const = Res("const")
        cm = AR.alloc([128, 8, 128], BF16)
        trineg, Dd, Ds, MO, MA, MB, onesneg, zero = [cm[:, i, :] for i in range(8)]
        ident = AR.alloc([128, 128], BF16)
        CS = AR.alloc([128, NSLOT, 8], F32)
        SN = AR.alloc([128, NSLOT, 8], F32)
        qg = AR.alloc([128, 64], F32)
        kg = AR.alloc([128, 64], F32)
        subg8 = AR.alloc([128, 128], F32)
        lv = AR.alloc([128, 4, 64], F32)
        lamt = AR.alloc([128, 16], F32)
        OFF_O = AR.off

        dma("pool", cm, cmat.rearrange("m p c -> p m c"), [], [R_const])
        dma("sp", CS, cs_t, [], [R_const])
        dma("sp", SN, sn_t, [], [R_const])
        dma("sp", qg, qng.partition_broadcast(128), [], [R_const])
        dma("sp", kg, kng.partition_broadcast(128), [], [R_const])
        dma("sp", subg8, subg.partition_broadcast(128), [], [R_const])
        for i in range(4):
            dma("sp", lv[:, i, :], lamv[i].partition_broadcast(128), [], [R_const])
        S.op("pool", lambda e: e.memset(ident, 1.0), [], [R_const])
        S.op("pool", lambda e: e.affine_select(out=ident, in_=ident, pattern=[[-1, 128]], compare_op=ALU.is_equal,
                                               fill=0.0, base=0, channel_multiplier=1), [R_const], [R_const])
        ts("dve", subg8, subg8, 1.0 - LAM_INIT, None, ALU.mult, None, [R_const], [R_const])
        tt("dve", lv[:, 0, :], lv[:, 0, :], lv[:, 1, :], ALU.mult, [R_const], [R_const])
        tt("dve", lv[:, 2, :], lv[:, 2, :], lv[:, 3, :], ALU.mult, [R_const], [R_const])
        S.op("dve", lambda e: e.tensor_reduce(out=lamt[:, 0:1], in_=lv[:, 0, :], axis=AX.X, op=ALU.add), [R_const], [R_const])
        S.op("dve", lambda e: e.tensor_reduce(out=lamt[:, 1:2], in_=lv[:, 2, :], axis=AX.X, op=ALU.add), [R_const], [R_const])
        act(lamt[:, 2:4], lamt[:, 0:2], AF.Exp, [R_const], [R_const])
        tt("dve", lamt[:, 5:6], lamt[:, 3:4], lamt[:, 2:3], ALU.subtract, [R_const], [R_const])
        ts("dve", lamt[:, 4:5], lamt[:, 5:6], -LAM_INIT, None, ALU.add, None, [R_const], [R_const])
        neg_lam = lamt[:, 4:5]

        Od = AR.alloc([128, NOWN + 1, 1024], BF16)
        Os = AR.alloc([128, NOWN + 1, 512], BF16)
        R_O = [Res(f"O{j}") for j in range(NOWN + 1)]
        OFF_P12 = AR.off
        hT = AR.alloc([128, 8, NT], BF16)
        hTs = AR.alloc([128, 8, DS], BF16)
        R_hT = [Res(f"hT{s}") for s in range(NB + 2)]
        OFF_HEAD = AR.off
        NALT = 2 * gq + 1
        KT = AR.alloc([128, KW + NALT * 128], BF16)
        Vaug = AR.alloc([128, NB + 1 + NALT, 130], BF16)
        SKT2 = AR.alloc([128, SKW + gq * 128 + 128], BF16)
        SVaug = AR.alloc([128, NB + 1 + NALT, 66], BF16)
        QT = AR.alloc([128, (NOWN + gq) * 128], BF16)
        SQT2 = AR.alloc([128, (NOWN + gq) * 128], BF16)
        Wh = AR.alloc([128, 8, 576], BF16)
        R_K = [Res(f"K{s}") for s in range(NB + 1)]
        R_Q = [Res(f"Q{s}") for s in range(NOWN)]
        R_Kalt = [Res(f"Ka{s}") for s in range(NALT)]
        R_Qalt = [Res(f"Qa{s}") for s in range(gq)]

        class Map:
            def __init__(self, par):
                self.par = par

            def alt(self, s):
                if not self.par:
                    return None
                if s < gq:
                    return s
                if NOWN <= s < NOWN + gq:
                    return gq + s - NOWN
                if s == NB:
                    return 2 * gq
                return None

            def kt(self, s):
                a = self.alt(s)
                return s * 128 if a is None else KW + a * 128

            def vs(self, s):
                a = self.alt(s)
                return s if a is None else NB + 1 + a

            def skl(self, s):
                a = self.alt(s)
                if a is None:
                    return (0, s * 128) if s < NOWN else (64, (s - NOWN) * 128)
                if a < gq:
                    return 0, SKW + a * 128
                return 64, SKW + (a - gq) * 128

            def qc(self, j):
                return (NOWN + j) * 128 if (self.par and j < gq) else j * 128

            def rk(self, s):
                a = self.alt(s)
                return R_K[s] if a is None else R_Kalt[a]

            def rq(self, j):
                return R_Qalt[j] if (self.par and j < gq) else R_Q[j]

        MAPS = [Map(0), Map(1)]
        R_Wh = Res("Wh")
        OFF_WORK = AR.off

        def slot_tok0(s):
            return s * 128

        def sk_loc(s):
            if s < NOWN:
                return 0, s * 128
            return 64, (s - NOWN) * 128

        S.op("pool", lambda e, t=Vaug[:, :, 128:130]: e.memset(t, 1.0), [], R_K)
        S.op("pool", lambda e, t=SVaug[:, :, 64:66]: e.memset(t, 1.0), [], R_K)

        AR.seek(OFF_WORK)
        x_st = [AR.alloc([128, DM], F32) for _ in range(2)]
        h_sb = [AR.alloc([128, DM], BF16) for _ in range(2)]
        junk = AR.alloc([128, DM], BF16)
        gb = AR.alloc([128, DM], F32)
        st1 = AR.alloc([128, 8], F32)
        R_x = [Res("x0"), Res("x1")]
        R_h = [Res("h0"), Res("h1")]
        R_junk = Res("junk")
        R_gb = Res("gb")
        R_st = [Res("st0"), Res("st1")]
        dma("sp", gb, g_mix.partition_broadcast(128), [], [R_gb])

        p1_cnt = [0]

        def norm_block(src_ap, nt, dstT, R_dst, gtile, R_g, x_keep=None, tbank=None):
            k = p1_cnt[0] % 2
            p1_cnt[0] += 1
            xb, hb = x_st[k], h_sb[k]
            if src_ap is not None:
                dma("sp", xb[0:nt, :], src_ap, [], [R_x[k]])
            else:
                xb = x_keep
            stc = st1[:, 4 * k:4 * k + 4]
            act(junk[0:nt, :], xb[0:nt, :], AF.Square, [R_x[k]], [R_junk, R_st[k]], accum=stc[0:nt, 0:1])
            act(stc[0:nt, 1:2], stc[0:nt, 0:1], AF.Ln, [R_st[k]], [R_st[k]], scale=1.0 / DM, bias=EPS)
            act(stc[0:nt, 2:3], stc[0:nt, 1:2], AF.Exp, [R_st[k]], [R_st[k]], scale=-0.5)
            stt("dve", hb[0:nt, :], xb[0:nt, :], stc[0:nt, 2:3], gtile[0:nt, :], ALU.mult, ALU.mult,
                [R_x[k], R_st[k], R_g], [R_h[k]])
            bi = 6 + k if tbank is None else tbank
            pT = bankb(bi)
            for ch in range(8):
                tr(pT[:, ch * 128:ch * 128 + nt], hb[0:nt, ch * 128:(ch + 1) * 128], [R_h[k]], [RB[bi]])
            src = pT.rearrange("p (c t) -> p c t", t=128)[:, :, 0:nt]
            return lambda: cp("act" if k == 0 else "dve", dstT, src, [RB[bi]], [R_dst])

        pend = None
        for s in range(NB + 1):
            nt = 128 if s < NB else N_META
            t0 = slot_tok0(s)
            nxt = norm_block(x_slots[t0:t0 + nt, :], nt, hT[:, :, t0:t0 + nt], R_hT[s], gb, R_gb)
            if pend is not None:
                pend()
            pend = nxt
        nxt = norm_block(xs[:, :], DS, hTs[:, :, :], R_hT[NB + 1], gb, R_gb)
        pend()
        nxt()
        barrier()

        AR.seek(OFF_WORK)
        sq_t = AR.alloc([128, 4, 2, 64], F32)
        kn = AR.alloc([128, 4, 2, 64], F32)
        qn = AR.alloc([128, 4, 2, 64], F32)
        knb = AR.alloc([128, 4, 128], BF16)
        qnb = knb
        rt = [AR.alloc([128, 4, 2, 8], F32) for _ in range(4)]
        vout = AR.alloc([128, 4, 128], F32)
        skout = AR.alloc([128, 4, 64], F32)
        svout = AR.alloc([128, 4, 64], F32)
        skb2 = AR.alloc([128, 4, 128], BF16)
        sqb2 = AR.alloc([128, 4, 128], BF16)
        stk = AR.alloc([128, 4, 4, 2], F32)
        stq = AR.alloc([128, 4, 4, 2], F32)
        Et = [[AR.alloc([128, 512], BF16) for _ in range(2)] for _ in range(2)]
        Lt = [[AR.alloc([128, 512], BF16) for _ in range(2)] for _ in range(2)]
        Wt = Et
        Oacc = AR.alloc([128, 4, 64], F32)
        tmpS = AR.alloc([128, 4, 64], F32)
        stick = AR.alloc([128, 8], F32)
        o4 = [AR.alloc([128, 128], F32) for _ in range(4)]
        sqj = AR.alloc([128, 128], BF16)
        fin = AR.alloc([128, 16], F32)
        fin2 = AR.alloc([128, 16], F32)
        R_o4 = [Res(f"o4_{i}") for i in range(4)]
        R_sqj, R_fin2 = Res("sqj"), Res("fin2")
        OFF_P3 = AR.off
        R_sq, R_kn, R_qn, R_knb, R_rt, R_rt2 = Res("sq"), Res("kn"), Res("qn"), Res("knb"), Res("rt"), Res("rt2")
        R_qnb = R_knb
        R_vo, R_sko, R_svo, R_skb, R_sqb, R_stk, R_stq = (Res("vo"), Res("sko"), Res("svo"), Res("skb"),
                                                         Res("sqb"), Res("stk"), Res("stq"))
        R_E = [[Res(f"E{c}{b}") for b in range(2)] for c in range(2)]
        R_L = [[Res(f"L{c}{b}") for b in range(2)] for c in range(2)]
        R_W = R_E
        R_Oacc, R_tmpS, R_stick, R_ot, R_tmpt, R_fin = Res("Oacc"), Res("tmpS"), Res("stick"), Res("ot"), Res("tmpt"), Res("fin")
        S.op("pool", lambda e: e.memset(skb2, 0.0), [], [R_skb])

        w_in_v = w_in.rearrange("(c p) n -> p c n", p=128)

        def load_wh(hd):
            segs = [(1024 + hd * 128, 0, 128), (2048 + hd * 128, 128, 128), (3584 + hd * 64, 256, 64),
                    (4096 + hd * 64, 320, 64), (hd * 128, 384, 128), (3072 + hd * 64, 512, 64)]
            for c0, d0, w in segs:
                dma("pool", Wh[:, :, d0:d0 + w], w_in_v[:, :, c0:c0 + w], [], [R_Wh])

        KV4 = PS[:, 0:2048].rearrange("p (s c) -> p s c", c=512)
        Q4 = PS[:, 2048:3072].rearrange("p (s c) -> p s c", c=256)

        def qk_norm_ops(ns, nt, dst, R_dst, stt_, R_stt, gvec, slot0):
            ops = []
            d = dst[0:nt, 0:ns]
            ops.append(lambda: tt("pool", sq_t[0:nt, 0:ns], d, d, ALU.mult, [R_dst], [R_sq]))
            ops.append(lambda: S.op("dve", lambda e, n=nt, m=ns, s=stt_: e.tensor_reduce(out=s[0:n, 0, 0:m, :], in_=sq_t[0:n, 0:m], axis=AX.X, op=ALU.add),
                                    [R_sq], [R_stt]))
            ops.append(lambda: act(stt_[0:nt, 1, 0:ns, :], stt_[0:nt, 0, 0:ns, :], AF.Ln, [R_stt], [R_stt], scale=1.0 / 64, bias=EPS))
            ops.append(lambda: act(stt_[0:nt, 2, 0:ns, :], stt_[0:nt, 1, 0:ns, :], AF.Exp, [R_stt], [R_stt], scale=-0.5))
            ops.append(lambda: tt("dve", d, d, stt_[0:nt, 2, 0:ns, :].unsqueeze(3).to_broadcast([nt, ns, 2, 64]), ALU.mult,
                                  [R_stt, R_dst], [R_dst]))
            ops.append(lambda: tt("dve", d, d, gvec[0:nt, :].unsqueeze(1).unsqueeze(1).to_broadcast([nt, ns, 2, 64]), ALU.mult,
                                  [R_dst, R_const], [R_dst]))
            cosb = CS[0:nt, slot0:slot0 + ns, :].unsqueeze(2).to_broadcast([nt, ns, 2, 8])
            sinb = SN[0:nt, slot0:slot0 + ns, :].unsqueeze(2).to_broadcast([nt, ns, 2, 8])
            x1 = dst[0:nt, 0:ns, :, 0:8]
            x2 = dst[0:nt, 0:ns, :, 8:16]
            r = [t[0:nt, 0:ns] for t in rt]
            ops.append(lambda: tt("pool", r[0], x1, cosb, ALU.mult, [R_dst, R_const], [R_rt]))
            ops.append(lambda: tt("dve", r[1], x2, sinb, ALU.mult, [R_dst, R_const], [R_rt2]))
            ops.append(lambda: tt("pool", r[2], x2, cosb, ALU.mult, [R_dst, R_const], [R_rt]))
            ops.append(lambda: tt("dve", r[3], x1, sinb, ALU.mult, [R_dst, R_const], [R_rt2]))
            ops.append(lambda: tt("pool", x1, r[0], r[1], ALU.subtract, [R_rt, R_rt2], [R_dst]))
            ops.append(lambda: tt("dve", x2, r[2], r[3], ALU.add, [R_rt, R_rt2], [R_dst]))
            return ops

        def projA(slots, nt, with_q, hsrc, R_hsrc, qslots, outs, rope_slot0, par=0, bgfg=False):
            ns = len(slots)
            s0 = slots[0]
            mp = MAPS[par]
            Rk = [mp.rk(s) for s in slots]
            v0 = mp.vs(s0)
            pb_, c0 = mp.skl(s0)
            two_d = outs is not None and len(outs["kd"].shape) == 2

            def o3(ap):
                return ap[:, 0, :] if two_d else ap
            ops = []
            if not bgfg:
                for si, s in enumerate(slots):
                    hs = hsrc(si)
                    for ch in range(8):
                        mm(bank(si)[0:nt, 0:384], hs[:, ch, :], Wh[:, ch, 0:384], ch == 0, ch == 7, [R_hsrc[si], R_Wh], [RB[si]])
                    if with_q:
                        qb = 4 + si // 2
                        qo = (si % 2) * 256
                        for ch in range(8):
                            mm(bank(qb)[0:nt, qo:qo + 192], hs[:, ch, :], Wh[:, ch, 384:576], ch == 0, ch == 7,
                               [R_hsrc[si], R_Wh], [RB[qb]])
                kvv = KV4[0:nt, 0:ns, :]
                allb = [RB[i] for i in range(ns)]
                act(kn[0:nt, 0:ns], kvv[:, :, 0:128].rearrange("p s (c d) -> p s c d", c=2), AF.Copy, allb, [R_kn])
                vv = kvv[:, :, 128:256]
                cp("dve", Vaug[0:nt, v0:v0 + ns, 0:128], vv, allb, Rk)
                skv = kvv[:, :, 256:320]
                cp("dve", skb2[0:nt, 0:ns, pb_:pb_ + 64], skv, allb, [R_skb])
                svv = kvv[:, :, 320:384]
                cp("dve", SVaug[0:nt, v0:v0 + ns, 0:64], svv, allb, Rk)
                if outs is not None:
                    cp("act", vout[0:nt, 0:ns, :], vv, allb, [R_vo])
                    dma("sp", outs["vd"], o3(vout[0:nt, 0:ns, :]), [R_vo], [])
                    cp("act", skout[0:nt, 0:ns, :], skv, allb, [R_sko])
                    dma("sp", outs["sk"], o3(skout[0:nt, 0:ns, :]), [R_sko], [])
                    cp("act", svout[0:nt, 0:ns, :], svv, allb, [R_svo])
                    dma("sp", outs["sv"], o3(svout[0:nt, 0:ns, :]), [R_svo], [])
                if with_q:
                    qv = Q4[0:nt, 0:ns, :]
                    qbanks = [RB[4], RB[5]]
                    cp("dve", qn[0:nt, 0:ns], qv[:, :, 0:128].rearrange("p s (c d) -> p s c d", c=2), qbanks, [R_qn])
                    act(sqb2[0:nt, 0:ns, 0:64], qv[:, :, 128:192], AF.Copy, qbanks, [R_sqb], scale=0.125)
                    act(sqb2[0:nt, 0:ns, 64:128], qv[:, :, 128:192], AF.Copy, qbanks, [R_sqb], scale=0.125)
            else:
                b7 = bank(7)
                r7 = [RB[7]]
                for si, s in enumerate(slots):
                    hs = hsrc(si)

                    def mm_kv(hs=hs, si=si):
                        for ch in range(8):
                            mm(b7[0:nt, 0:384], hs[:, ch, :], Wh[:, ch, 0:384], ch == 0, ch == 7, [R_hsrc[si], R_Wh], r7)
                    ops.append(mm_kv)
                    ops.append(lambda si=si: act(kn[0:nt, si], b7[0:nt, 0:128].rearrange("p (c d) -> p c d", c=2), AF.Copy, r7, [R_kn]))
                    ops.append(lambda si=si: cp("dve", Vaug[0:nt, v0 + si, 0:128], b7[0:nt, 128:256], r7, [Rk[si]]))
                    ops.append(lambda si=si: cp("dve", skb2[0:nt, si, pb_:pb_ + 64], b7[0:nt, 256:320], r7, [R_skb]))
                    ops.append(lambda si=si: cp("dve", SVaug[0:nt, v0 + si, 0:64], b7[0:nt, 320:384], r7, [Rk[si]]))
                    if outs is not None:
                        ops.append(lambda si=si: cp("act", vout[0:nt, si, :], b7[0:nt, 128:256], r7, [R_vo]))
                        ops.append(lambda si=si: cp("act", skout[0:nt, si, :], b7[0:nt, 256:320], r7, [R_sko]))
                        ops.append(lambda si=si: cp("act", svout[0:nt, si, :], b7[0:nt, 320:384], r7, [R_svo]))
                    if with_q:
                        def mm_q(hs=hs, si=si):
                            for ch in range(8):
                                mm(b7[0:nt, 0:192], hs[:, ch, :], Wh[:, ch, 384:576], ch == 0, ch == 7, [R_hsrc[si], R_Wh], r7)
                        ops.append(mm_q)
                        ops.append(lambda si=si: cp("dve", qn[0:nt, si], b7[0:nt, 0:128].rearrange("p (c d) -> p c d", c=2), r7, [R_qn]))
                        ops.append(lambda si=si: act(sqb2[0:nt, si, 0:64], b7[0:nt, 128:192], AF.Copy, r7, [R_sqb], scale=0.125))
                        ops.append(lambda si=si: act(sqb2[0:nt, si, 64:128], b7[0:nt, 128:192], AF.Copy, r7, [R_sqb], scale=0.125))
                if outs is not None:
                    def out_dmas():
                        dma("sp", outs["vd"], o3(vout[0:nt, 0:ns, :]), [R_vo], [])
                        dma("sp", outs["sk"], o3(skout[0:nt, 0:ns, :]), [R_sko], [])
                        dma("sp", outs["sv"], o3(svout[0:nt, 0:ns, :]), [R_svo], [])
                    ops.append(out_dmas)
            wdt = ns * 128 if nt == 128 else nt
            pT2 = bankb(7)

            def sk_tr():
                for si in range(ns):
                    tr(pT2[:, si * 128:si * 128 + nt], skb2[0:nt, si, :], [R_skb], [RB[7]])
            ops.append(sk_tr)
            ops.append(lambda: cp("dve", SKT2[pb_:pb_ + 64, c0:c0 + wdt], pT2[pb_:pb_ + 64, 0:wdt], [RB[7]], Rk))
            if with_q:
                Rq = [mp.rq(s) for s in qslots]
                q0 = mp.qc(qslots[0])

                def sq_tr():
                    for si in range(ns):
                        tr(pT2[:, si * 128:si * 128 + nt], sqb2[0:nt, si, :], [R_sqb], [RB[7]])
                ops.append(sq_tr)
                ops.append(lambda: cp("dve", SQT2[:, q0:q0 + wdt], pT2[:, 0:wdt], [RB[7]], Rq))
            ops += qk_norm_ops(ns, nt, kn, R_kn, stk, R_stk, kg, rope_slot0)
            knv = kn[0:nt, 0:ns].rearrange("p s c d -> p s (c d)")
            if outs is not None:
                ops.append(lambda: dma("sp", outs["kd"], o3(knv), [R_kn], []))
            ops.append(lambda: cp("pool", knb[0:nt, 0:ns, :], knv, [R_kn], [R_knb]))

            def k_tr():
                for si in range(ns):
                    tr(pT2[:, si * 128:si * 128 + nt], knb[0:nt, si, :], [R_knb], [RB[7]])
            ops.append(k_tr)
            t0 = mp.kt(s0)
            ops.append(lambda: cp("dve", KT[:, t0:t0 + wdt], pT2[:, 0:wdt], [RB[7]], Rk))
            if with_q:
                ops += qk_norm_ops(ns, nt, qn, R_qn, stq, R_stq, qg, rope_slot0)
                ops.append(lambda: ts("dve", qnb[0:nt, 0:ns, :], qn[0:nt, 0:ns].rearrange("p s c d -> p s (c d)"), 0.125, None,
                                      ALU.mult, None, [R_qn], [R_qnb]))

                def q_tr():
                    for si in range(ns):
                        tr(pT2[:, si * 128:si * 128 + nt], qnb[0:nt, si, :], [R_qnb], [RB[7]])
                ops.append(q_tr)
                ops.append(lambda: cp("dve", QT[:, q0:q0 + wdt], pT2[:, 0:wdt], [RB[7]], Rq))
            return ops

        def run_serial(ops):
            for o in ops:
                o()

        def weave(steps, ops, frac=0.75):
            n, m = len(steps), len(ops)
            k = 0
            for i, st in enumerate(steps):
                st()
                tgt = m if n == 0 else min(m, int((i + 1) * m / max(1.0, n * frac)) + 1)
                while k < tgt:
                    ops[k]()
                    k += 1
            while k < m:
                ops[k]()
                k += 1

        def attn_diff(q0, nqb, qw, blocks, out_blk0, hd, par=0):
            mp = MAPS[par]
            def acc(i, c):
                a = i * 2 + c
                return 4 + a // 3, (a % 3) * 129

            def finalize():
                diff_finalize(nqb, qw, acc, out_blk0, hd)
            started = set()
            nblk = len(blocks)
            lastblk = {}
            for bi_, kb in enumerate(blocks):
                for i in range(kb["fq"], nqb):
                    lastblk[i] = bi_

            def qk(t):
                kb = blocks[t]
                buf = t % 2
                nk, fq, s = kb["nk"], kb["fq"], kb["slot"]
                N = (nqb - fq) * qw
                c0 = q0 + fq * qw
                k0 = mp.kt(s)
                for c in range(2):
                    bk = buf * 2 + c
                    mm(bank(bk)[0:nk, 0:N], KT[c * 64:(c + 1) * 64, k0:k0 + nk], QT[c * 64:(c + 1) * 64, c0:c0 + N], True, True,
                       [mp.rk(s)] + kb["rq"], [RB[bk]])
                for c in range(2):
                    bk = buf * 2 + c
                    act(Et[c][buf][0:nk, 0:N], bank(bk)[0:nk, 0:N], AF.Exp, [RB[bk]], [R_E[c][buf]])
                    for qi, m in kb["masks"].items():
                        sl = Et[c][buf][0:nk, (qi - fq) * qw:(qi - fq + 1) * qw]
                        tt("dve", sl, sl, m[0:nk, 0:qw], ALU.mult, [R_E[c][buf], R_const], [R_E[c][buf]])

            def pv(t):
                kb = blocks[t]
                buf = t % 2
                nk, fq, s = kb["nk"], kb["fq"], kb["slot"]
                for i in range(fq, nqb):
                    for c in range(2):
                        bk, col = acc(i, c)
                        st = bk not in started
                        started.add(bk)
                        mm(bank(bk)[0:qw, col:col + 129], Et[c][buf][0:nk, (i - fq) * qw:(i - fq + 1) * qw],
                           Vaug[0:nk, mp.vs(s), 0:129], st, lastblk[i] == t, [R_E[c][buf], mp.rk(s)], [RB[bk]], skip=True)

            steps = []
            for t in range(nblk + 1):
                def st(t=t):
                    if t < nblk:
                        qk(t)
                    if t >= 1:
                        pv(t - 1)
                steps.append(st)
            steps.append(lambda: finalize())
            return steps

        def _unused():
            pass

        def diff_finalize(nqb, qw, acc, out_blk0, hd):
            f = fin[0:qw]
            f2 = fin2[0:qw]
            A = []
            for i in range(nqb):
                b0, c0_ = acc(i, 0)
                b1, c1_ = acc(i, 1)
                A.append((b0, bank(b0)[0:qw, c0_:c0_ + 129], b1, bank(b1)[0:qw, c1_:c1_ + 129]))
            for i, (b0, a0, b1, a1) in enumerate(A):
                S.op("dve", lambda e, o=f[:, i:i + 1], a=a0[:, 128:129]: e.reciprocal(out=o, in_=a), [RB[b0]], [R_fin])
                S.op("dve", lambda e, o=f[:, 4 + i:5 + i], a=a1[:, 128:129]: e.reciprocal(out=o, in_=a), [RB[b1]], [R_fin])
            ts("dve", f[:, 8:8 + nqb], f[:, 4:4 + nqb], neg_lam[0:qw, 0:1], None, ALU.mult, None, [R_fin, R_const], [R_fin])
            for i, (b0, a0, b1, a1) in enumerate(A):
                act(o4[i][0:qw, :], a1[:, 0:128], AF.Copy, [RB[b1], R_fin], [R_o4[i]], scale=f[:, 8 + i:9 + i])
                stt("dve", o4[i][0:qw, :], a0[:, 0:128], f[:, i:i + 1], o4[i][0:qw, :], ALU.mult, ALU.add,
                    [RB[b0], R_fin, R_o4[i]], [R_o4[i]])
                act(sqj[0:qw, :], o4[i][0:qw, :], AF.Square, [R_o4[i]], [R_sqj, R_fin2], accum=f2[:, i:i + 1])
            act(f2[:, 4:4 + nqb], f2[:, 0:nqb], AF.Ln, [R_fin2], [R_fin2], scale=1.0 / 128, bias=EPS)
            act(f2[:, 8:8 + nqb], f2[:, 4:4 + nqb], AF.Exp, [R_fin2], [R_fin2], scale=-0.5)
            for i in range(nqb):
                stt("dve", Od[0:qw, out_blk0 + i, hd * 128:(hd + 1) * 128], o4[i][0:qw, :], f2[:, 8 + i:9 + i], subg8[0:qw, :],
                    ALU.mult, ALU.mult, [R_o4[i], R_fin2, R_const], [R_O[out_blk0 + i]])

        def attn_sb(q0, nqb, qw, units, out_blk0, hd, ma, mb, par=0):
            mp = MAPS[par]
            nu = len(units)
            tt_ = None
            S.op("dve", lambda e: e.memset(stick[0:qw, 0:nqb], 1.0), [], [R_stick])
            S.op("dve", lambda e: e.memset(Oacc[0:qw, 0:nqb, :], 0.0), [], [R_Oacc])

            def geo(u):
                kb = units[u][0]
                fq = kb["fq"]
                return fq, (nqb - fq) * qw, q0 + fq * qw

            def s1(u):
                fq, N, c0 = geo(u)
                buf = u % 2
                for x, kb in enumerate(units[u]):
                    bk = buf * 2 + x
                    nk, s = kb["nk"], kb["slot"]
                    pb_, k0 = mp.skl(s)
                    mm(bank(bk)[0:nk, 0:N], SKT2[pb_:pb_ + 64, k0:k0 + nk], SQT2[pb_:pb_ + 64, c0:c0 + N], True, True,
                       [mp.rk(s)] + kb["rq"], [RB[bk]])
                for x, kb in enumerate(units[u]):
                    bk = buf * 2 + x
                    nk = kb["nk"]
                    eb = 6
                    act(bank(eb)[0:nk, 0:N], bank(bk)[0:nk, 0:N], AF.Exp, [RB[bk]], [RB[eb]])
                    act(Lt[x][buf][0:nk, 0:N], bank(eb)[0:nk, 0:N], AF.Ln, [RB[eb]], [R_L[x][buf]], bias=1.0)
                    for qi, m in kb["masks"].items():
                        sl = Lt[x][buf][0:nk, (qi - fq) * qw:(qi - fq + 1) * qw]
                        tt("dve", sl, sl, m[0:nk, 0:qw], ALU.mult, [R_L[x][buf], R_const], [R_L[x][buf]])

            def s2(u):
                fq, N, c0 = geo(u)
                buf = u % 2
                un = units[u]
                for x, kb in enumerate(un):
                    bk = buf * 2 + x
                    nk = kb["nk"]
                    pair = len(un) == 2
                    mm(bank(bk)[0:nk, 0:N], trineg[0:nk, 0:nk], Lt[x][buf][0:nk, 0:N], False, not pair,
                       [R_L[x][buf], R_const], [RB[bk]], skip=True)
                    if pair:
                        ok = un[1 - x]["nk"]
                        cm_ = ma if x == 0 else mb
                        mm(bank(bk)[0:nk, 0:N], cm_[0:ok, 0:nk], Lt[1 - x][buf][0:ok, 0:N], False, True,
                           [R_L[1 - x][buf], R_const], [RB[bk]], skip=True)
                for x, kb in enumerate(un):
                    bk = buf * 2 + x
                    nk = kb["nk"]
                    act(Wt[x][buf][0:nk, 0:N], bank(bk)[0:nk, 0:N], AF.Exp, [RB[bk]], [R_W[x][buf]])
                    for qi, m in kb["masks"].items():
                        sl = Wt[x][buf][0:nk, (qi - fq) * qw:(qi - fq + 1) * qw]
                        tt("dve", sl, sl, m[0:nk, 0:qw], ALU.mult, [R_W[x][buf], R_const], [R_W[x][buf]])

            def s3(u):
                fq, N, c0 = geo(u)
                buf = u % 2
                un = units[u]
                pbk = 4 + buf
                first = True
                for i in range(fq, nqb):
                    for x, kb in enumerate(un):
                        nk, s = kb["nk"], kb["slot"]
                        mm(bank(pbk)[0:qw, i * 65:i * 65 + 65], Wt[x][buf][0:nk, (i - fq) * qw:(i - fq + 1) * qw],
                           SVaug[0:nk, mp.vs(s), 0:65], first, x == len(un) - 1, [R_W[x][buf], mp.rk(s)], [RB[pbk]], skip=True)
                        first = False
                n = nqb - fq
                Pv = bank(pbk)[0:qw, 0:nqb * 65].rearrange("p (i c) -> p i c", c=65)
                tt("dve", tmpS[0:qw, fq:nqb, :], Pv[:, fq:nqb, 0:64],
                   stick[0:qw, fq:nqb].unsqueeze(2).to_broadcast([qw, n, 64]), ALU.mult, [RB[pbk], R_stick], [R_tmpS])
                tt("dve", Oacc[0:qw, fq:nqb, :], Oacc[0:qw, fq:nqb, :], tmpS[0:qw, fq:nqb, :], ALU.add, [R_tmpS, R_Oacc], [R_Oacc])
                if u < nu - 1:
                    ts("dve", fin[0:qw, 8:8 + n], Pv[:, fq:nqb, 64], -1.0, 1.0, ALU.mult, ALU.add, [RB[pbk]], [R_fin])
                    tt("dve", stick[0:qw, fq:nqb], stick[0:qw, fq:nqb], fin[0:qw, 8:8 + n], ALU.mult, [R_fin, R_stick], [R_stick])

            steps = []
            for t in range(nu + 2):
                def st(t=t):
                    if t < nu:
                        s1(t)
                    if 1 <= t <= nu:
                        s2(t - 1)
                    if t >= 2:
                        s3(t - 2)
                steps.append(st)
            steps.append(lambda: cp("pool", Os[0:qw, out_blk0:out_blk0 + nqb, hd * 64:(hd + 1) * 64], Oacc[0:qw, 0:nqb, :], [R_Oacc],
                                    [R_O[out_blk0 + i] for i in range(nqb)]))
            return steps

        kd_v = kd_o.rearrange("(s t) h d -> t s h d", t=128)
        vd_v = vd_o.rearrange("(s t) h d -> t s h d", t=128)
        sk_v = sk_o.rearrange("(s t) h d -> t s h d", t=128)
        sv_v = sv_o.rearrange("(s t) h d -> t s h d", t=128)

        def own_batch(g, hd):
            own = list(range(g * gq, (g + 1) * gq))
            return projA(own, 128, True, lambda si, o=own: hT[:, :, o[si] * 128:(o[si] + 1) * 128], [R_hT[s] for s in own], own,
                         dict(kd=kd_v[:, own[0]:own[-1] + 1, hd, :], vd=vd_v[:, own[0]:own[-1] + 1, hd, :],
                              sk=sk_v[:, own[0]:own[-1] + 1, hd, :], sv=sv_v[:, own[0]:own[-1] + 1, hd, :]), own[0], par=hd % 2)

        def oth_batch(g, hd):
            oth = [NOWN + j for j in range(g * gq, (g + 1) * gq)]
            return projA(oth, 128, False, lambda si, o=oth: hT[:, :, o[si] * 128:(o[si] + 1) * 128], [R_hT[s] for s in oth], None,
                         None, oth[0], par=hd % 2)

        def meta_batch(hd, bgfg=False):
            return projA([NB], N_META, False, lambda si: hT[:, :, NB * 128:NB * 128 + N_META], [R_hT[NB]], None,
                         dict(kd=kd_m[:, hd, :], vd=vd_m[:, hd, :], sk=sk_m[:, hd, :], sv=sv_m[:, hd, :]), NB, par=hd % 2,
                         bgfg=bgfg)

        def prompt_head(hd):
            par = hd % 2
            mp = MAPS[par]
            if hd == 0:
                run_serial(meta_batch(hd))
                run_serial(own_batch(0, hd))
                run_serial(oth_batch(0, hd))
            if NG == 1 and hd + 1 < NH:
                load_wh(hd + 1)
            for g in range(NG):
                j0 = g * gq
                rq = [mp.rq(j) for j in range(j0, j0 + gq)]
                q0 = mp.qc(j0)
                if g + 1 < NG:
                    bg = own_batch(g + 1, hd)
                elif hd + 1 < NH:
                    bg = own_batch(0, hd + 1)
                else:
                    bg = []
                if stop_after >= 2:
                    blocks = [dict(slot=NB, nk=N_META, fq=0, masks={}, rq=rq)]
                    for i in range(j0 + gq):
                        fq = max(i, j0) - j0
                        mo = {fq: Dd} if i >= j0 else {}
                        mo2 = {fq: MO} if i >= j0 else {}
                        blocks.append(dict(slot=i, nk=128, fq=fq, masks=mo, rq=rq))
                        blocks.append(dict(slot=NOWN + i, nk=128, fq=fq, masks=mo2, rq=rq))
                    weave(attn_diff(q0, gq, 128, blocks, j0, hd, par), bg)
                else:
                    run_serial(bg)
                if g + 1 < NG:
                    bg = oth_batch(g + 1, hd)
                elif hd + 1 < NH:
                    bg = oth_batch(0, hd + 1)
                    bg = bg + meta_batch(hd + 1, bgfg=True)
                else:
                    bg = []
                if g + 1 == NG - 1 and hd + 1 < NH:
                    load_wh(hd + 1)
                if stop_after >= 3:
                    units = []
                    for i in range(j0 + gq - 1, -1, -1):
                        fq = max(i, j0) - j0
                        mo = {fq: Ds} if i >= j0 else {}
                        mo2 = {fq: MO} if i >= j0 else {}
                        units.append([dict(slot=i, nk=128, fq=fq, masks=mo, rq=rq),
                                      dict(slot=NOWN + i, nk=128, fq=fq, masks=mo2, rq=rq)])
                    units.append([dict(slot=NB, nk=N_META, fq=0, masks={}, rq=rq)])
                    weave(attn_sb(q0, gq, 128, units, j0, hd, MA, MB, par), bg)
                else:
                    run_serial(bg)

        load_wh(0)
        for hd in range(NH if stop_after >= 1 else 0):
            prompt_head(hd)
        barrier()

        def sample_diff(hd):
            qw = DS
            batches = [list(range(b0, min(b0 + 16, PB))) for b0 in range(0, PB, 16)] + [[NB]]
            nbt = len(batches)
            started = set()

            def nkof(b):
                return DS if b == NB else 128

            def qk(t):
                bt = batches[t]
                buf = t % 2
                nk = nkof(bt[0])
                for c in range(2):
                    bk = buf * 2 + c
                    for k, b in enumerate(bt):
                        k0 = slot_tok0(b)
                        mm(bank(bk)[0:nk, k * qw:(k + 1) * qw], KT[c * 64:(c + 1) * 64, k0:k0 + nk], QT[c * 64:(c + 1) * 64, 0:qw],
                           k == 0, True, [R_K[b], R_Q[0]], [RB[bk]], skip=True)
                for c in range(2):
                    bk = buf * 2 + c
                    act(Et[c][buf][0:nk, 0:len(bt) * qw], bank(bk)[0:nk, 0:len(bt) * qw], AF.Exp, [RB[bk]], [R_E[c][buf]])

            def pv(t):
                bt = batches[t]
                buf = t % 2
                nk = nkof(bt[0])
                for k, b in enumerate(bt):
                    for c in range(2):
                        bk, col = 4, c * 129
                        st = bk not in started
                        started.add(bk)
                        mm(bank(bk)[0:qw, col:col + 129], Et[c][buf][0:nk, k * qw:(k + 1) * qw], Vaug[0:nk, b, 0:129],
                           st, (t == nbt - 1 and k == len(bt) - 1), [R_E[c][buf], R_K[b]], [RB[bk]], skip=True)

            for t in range(nbt + 1):
                if t < nbt:
                    qk(t)
                if t >= 1:
                    pv(t - 1)
            diff_finalize(1, qw, lambda i, c: (4, c * 129), NOWN, hd)

        def sample_sb(hd):
            qw = DS
            pairs = [(2 * i, 2 * i + 1) for i in range(PB // 2 - 1, -1, -1)]
            batches = [[(NB,)]] + [pairs[i:i + 7] for i in range(0, len(pairs), 7)]
            nbt = len(batches)
            S.op("dve", lambda e: e.memset(stick[0:qw, 0:1], 1.0), [], [R_stick])
            S.op("dve", lambda e: e.memset(Oacc[0:qw, 0:1, :], 0.0), [], [R_Oacc])

            def nkof(b):
                return DS if b == NB else 128

            def s1(t):
                bt = batches[t]
                buf = t % 2
                npair = len(bt)
                N = npair * qw
                nx = len(bt[0])
                nk = nkof(bt[0][0])
                for x in range(nx):
                    bk = buf * 2 + x
                    for k, un in enumerate(bt):
                        b = un[x]
                        pb_, k0 = sk_loc(b)
                        mm(bank(bk)[0:nk, k * qw:(k + 1) * qw], SKT2[pb_:pb_ + 64, k0:k0 + nk], SQT2[pb_:pb_ + 64, 0:qw], k == 0, True,
                           [R_K[b], R_Q[0]], [RB[bk]], skip=True)
                for x in range(nx):
                    bk = buf * 2 + x
                    act(bank(6)[0:nk, 0:N], bank(bk)[0:nk, 0:N], AF.Exp, [RB[bk]], [RB[6]])
                    act(Lt[x][buf][0:nk, 0:N], bank(6)[0:nk, 0:N], AF.Ln, [RB[6]], [R_L[x][buf]], bias=1.0)
                    if nx == 1:
                        sl = Lt[x][buf][0:nk, 0:qw]
                        tt("dve", sl, sl, Ds[0:nk, 0:qw], ALU.mult, [R_L[x][buf], R_const], [R_L[x][buf]])

            def s2(t):
                bt = batches[t]
                buf = t % 2
                N = len(bt) * qw
                nx = len(bt[0])
                nk = nkof(bt[0][0])
                for x in range(nx):
                    bk = buf * 2 + x
                    mm(bank(bk)[0:nk, 0:N], trineg[0:nk, 0:nk], Lt[x][buf][0:nk, 0:N], False, nx == 1, [R_L[x][buf], R_const], [RB[bk]], skip=True)
                    if nx == 2:
                        cm_ = onesneg if x == 0 else zero
                        mm(bank(bk)[0:nk, 0:N], cm_[0:nk, 0:nk], Lt[1 - x][buf][0:nk, 0:N], False, True,
                           [R_L[1 - x][buf], R_const], [RB[bk]], skip=True)
                for x in range(nx):
                    bk = buf * 2 + x
                    act(Wt[x][buf][0:nk, 0:N], bank(bk)[0:nk, 0:N], AF.Exp, [RB[bk]], [R_W[x][buf]])
                    if nx == 1:
                        sl = Wt[x][buf][0:nk, 0:qw]
                        tt("dve", sl, sl, Ds[0:nk, 0:qw], ALU.mult, [R_W[x][buf], R_const], [R_W[x][buf]])

            def s3(t):
                bt = batches[t]
                buf = t % 2
                nx = len(bt[0])
                nk = nkof(bt[0][0])
                pbk = 4 + buf
                first = True
                for k, un in enumerate(bt):
                    for x in range(nx):
                        b = un[x]
                        mm(bank(pbk)[0:qw, k * 65:k * 65 + 65], Wt[x][buf][0:nk, k * qw:(k + 1) * qw], SVaug[0:nk, b, 0:65],
                           first, x == nx - 1, [R_W[x][buf], R_K[b]], [RB[pbk]], skip=True)
                        first = False
                for k, un in enumerate(bt):
                    Pk = bank(pbk)[0:qw, k * 65:k * 65 + 65]
                    stt("dve", Oacc[0:qw, 0, :], Pk[:, 0:64], stick[0:qw, 0:1], Oacc[0:qw, 0, :], ALU.mult, ALU.add,
                        [RB[pbk], R_stick, R_Oacc], [R_Oacc])
                    if not (t == nbt - 1 and k == len(bt) - 1):
                        ts("dve", fin[0:qw, 8:9], Pk[:, 64:65], -1.0, 1.0, ALU.mult, ALU.add, [RB[pbk]], [R_fin])
                        tt("dve", stick[0:qw, 0:1], stick[0:qw, 0:1], fin[0:qw, 8:9], ALU.mult, [R_fin, R_stick], [R_stick])

            for t in range(nbt + 2):
                if t < nbt:
                    s1(t)
                if 1 <= t <= nbt:
                    s2(t - 1)
                if t >= 2:
                    s3(t - 2)
            cp("pool", Os[0:qw, NOWN:NOWN + 1, hd * 64:(hd + 1) * 64], Oacc[0:qw, 0:1, :], [R_Oacc], [R_O[NOWN]])

        if stop_after >= 4:
            AR.seek(OFF_P12)
            Kc2 = [AR.alloc([128, PB, 128], BF16) for _ in range(2)]
            SKc2 = [AR.alloc([128, PB, 128], BF16) for _ in range(2)]
            KT_B = AR.alloc([128, KW], BF16)
            Vaug_B = AR.alloc([128, NB + 1, 130], BF16)
            SKT2_B = AR.alloc([128, SKW], BF16)
            SVaug_B = AR.alloc([128, NB + 1, 66], BF16)
            assert AR.off <= OFF_P12 + 8 * NT * 2
            R_K_B = [Res(f"KB{s}") for s in range(NB + 1)]
            ksets = [(KT, Vaug, SKT2, SVaug, R_K), (KT_B, Vaug_B, SKT2_B, SVaug_B, R_K_B)]
            R_Kc2, R_SKc2 = [Res("Kc0"), Res("Kc1")], [Res("SKc0"), Res("SKc1")]
            for k in range(2):
                S.op("pool", lambda e, t=SKc2[k]: e.memset(t, 0.0), [], [R_SKc2[k]])
            S.op("pool", lambda e, t=Vaug_B[:, :, 128:130]: e.memset(t, 1.0), [], R_K_B)
            S.op("pool", lambda e, t=SVaug_B[:, :, 64:66]: e.memset(t, 1.0), [], R_K_B)
            ck_v = c_dk.rearrange("(b t) h d -> t b h d", t=128)
            cv_v = c_dv.rearrange("(b t) h d -> t b h d", t=128)
            csk_v = c_sk.rearrange("(b t) h d -> t b h d", t=128)
            csv_v = c_sv.rearrange("(b t) h d -> t b h d", t=128)
            HB = PB // 2

            def issue_loads(hd):
                k = hd % 2
                kt_, va_, skt_, sva_, rk_ = ksets[k]
                dma("pool", Kc2[k], ck_v[:, :, hd, :], [], [R_Kc2[k]])
                dma("pool", va_[:, 0:PB, 0:128], cv_v[:, :, hd, :], [], rk_[0:PB])
                dma("pool", SKc2[k][:, 0:HB, 0:64], csk_v[:, 0:HB, hd, :], [], [R_SKc2[k]])
                dma("pool", SKc2[k][:, HB:PB, 64:128], csk_v[:, HB:PB, hd, :], [], [R_SKc2[k]])
                dma("pool", sva_[:, 0:PB, 0:64], csv_v[:, :, hd, :], [], rk_[0:PB])

            load_wh(0)
            issue_loads(0)
            for hd in range(NH):
                if hd + 1 < NH:
                    issue_loads(hd + 1)
                k2 = hd % 2
                KT, Vaug, SKT2, SVaug, R_K = ksets[k2]
                Kc, SKc, R_Kc, R_SKc = Kc2[k2], SKc2[k2], R_Kc2[k2], R_SKc2[k2]
                bgs = projA([NB], DS, True, lambda si: hTs[:, :, :], [R_hT[NB + 1]], [0],
                            dict(kd=kd_s[:, hd, :], vd=vd_s[:, hd, :], sk=sk_s[:, hd, :], sv=sv_s[:, hd, :]), NB + 1)
                if hd + 1 < NH:
                    load_wh(hd + 1)
                tsteps = []
                for b0 in range(0, PB, 4):
                    def st_k(b0=b0, KT=KT, Kc=Kc, R_Kc=R_Kc, R_K=R_K):
                        bi = 2 + (b0 // 4) % 2
                        pT = bankb(bi)
                        for k in range(4):
                            tr(pT[:, k * 128:(k + 1) * 128], Kc[:, b0 + k, :], [R_Kc], [RB[bi]])
                        cp("act", KT[:, b0 * 128:(b0 + 4) * 128], pT[:, 0:512], [RB[bi]], R_K[b0:b0 + 4])
                    tsteps.append(st_k)
                for b0 in range(0, PB, 4):
                    def st_s(b0=b0, SKT2=SKT2, SKc=SKc, R_SKc=R_SKc, R_K=R_K):
                        bi = 2 + (b0 // 4) % 2
                        pT = bankb(bi)
                        for k in range(4):
                            tr(pT[:, k * 128:(k + 1) * 128], SKc[:, b0 + k, :], [R_SKc], [RB[bi]])
                        pb_, c0 = sk_loc(b0)
                        cp("dve", SKT2[pb_:pb_ + 64, c0:c0 + 512], pT[pb_:pb_ + 64, 0:512], [RB[bi]], R_K[b0:b0 + 4])
                    tsteps.append(st_s)
                weave(tsteps[len(tsteps) // 2:] + tsteps[:len(tsteps) // 2], bgs)
                sample_sb(hd)
                sample_diff(hd)
            barrier()

        if stop_after >= 5:
            OFF_MG = OFF_P12
            AR.seek(OFF_MG)
            mg = AR.alloc([128, NOWN + 1, 1024], BF16)
            R_mg = [Res(f"mg{j}") for j in range(NOWN + 1)]
            OFF_X1 = AR.off
            x_st = [AR.alloc([128, DM], F32) for _ in range(2)]
            h_sb = [AR.alloc([128, DM], BF16) for _ in range(2)]
            junk = AR.alloc([128, DM], BF16)
            gb = AR.alloc([128, DM], F32)
            st1 = AR.alloc([128, 8], F32)
            hTb = [AR.alloc([128, 8, 128], BF16) for _ in range(2)]
            odT = [AR.alloc([128, 8, 128], BF16) for _ in range(2)]
            osT = [AR.alloc([128, 4, 128], BF16) for _ in range(2)]
            egd = [AR.alloc([128, 1024], F32) for _ in range(2)]
            egs = [AR.alloc([128, 1024], F32) for _ in range(2)]
            Wd = AR.alloc([128, 8, 1024], BF16)
            Ws = AR.alloc([128, 4, 1024], BF16)
            Wg = AR.alloc([128, 8, 2048], BF16)
            R_x = [Res("x0"), Res("x1")]
            R_h = [Res("h0"), Res("h1")]
            R_junk, R_gb = Res("junk"), Res("gb")
            R_st = [Res("st0"), Res("st1")]
            R_hTb, R_odT, R_osT, R_egd, R_egs = [[Res(f"{n}{k}") for k in range(2)] for n in ("hTb", "odT", "osT", "egd", "egs")]
            R_Wd, R_Ws = Res("Wd"), Res("Ws")
            R_Wg = [Res(f"Wg{q}") for q in range(4)]
            p1_cnt[0] = 0
            dma("sp", gb, g_mix.partition_broadcast(128), [], [R_gb])
            for q_ in range(4):
                dma("pool", Wg[:, :, q_ * 512:(q_ + 1) * 512], w_in_v[:, :, 4608 + q_ * 512:4608 + (q_ + 1) * 512], [], [R_Wg[q_]])
            dma("pool", Wd, w_do.rearrange("(c p) n -> p c n", p=128), [], [R_Wd])
            dma("pool", Ws, w_so.rearrange("(c p) n -> p c n", p=128), [], [R_Ws])

            def blk_src(j):
                if j < NOWN:
                    return x_slots[j * 128:(j + 1) * 128, :], 128
                return xs[:, :], DS

            def c1a_A1(j):
                src, nt = blk_src(j)
                k = j % 2
                xb, hb = x_st[k], h_sb[k]
                dma("sp", xb[0:nt, :], src, [], [R_x[k]])
                stc = st1[:, 4 * k:4 * k + 4]
                act(junk[0:nt, :], xb[0:nt, :], AF.Square, [R_x[k]], [R_junk, R_st[k]], accum=stc[0:nt, 0:1])
                act(stc[0:nt, 1:2], stc[0:nt, 0:1], AF.Ln, [R_st[k]], [R_st[k]], scale=1.0 / DM, bias=EPS)
                act(stc[0:nt, 2:3], stc[0:nt, 1:2], AF.Exp, [R_st[k]], [R_st[k]], scale=-0.5)
                stt("dve", hb[0:nt, :], xb[0:nt, :], stc[0:nt, 2:3], gb[0:nt, :], ALU.mult, ALU.mult,
                    [R_x[k], R_st[k], R_gb], [R_h[k]])

            def c1a_Th(j):
                src, nt = blk_src(j)
                k = j % 2
                pT = bankb(7)
                for ch in range(8):
                    tr(pT[:, ch * 128:ch * 128 + nt], h_sb[k][0:nt, ch * 128:(ch + 1) * 128], [R_h[k]], [RB[7]])
                cp("dve", hTb[k][:, :, 0:nt], pT.rearrange("p (c t) -> p c t", t=128)[:, :, 0:nt], [RB[7]], [R_hTb[k]])

            def c1a_G(j):
                src, nt = blk_src(j)
                k = j % 2
                for gi, (b0, dst, R_dst) in enumerate([(0, egd[k], R_egd[k]), (2, egs[k], R_egs[k])]):
                    for half in range(2):
                        for ch in range(8):
                            mm(bank(b0 + half)[0:nt, :], hTb[k][:, ch, 0:nt], Wg[:, ch, gi * 1024 + half * 512:gi * 1024 + (half + 1) * 512],
                               ch == 0, ch == 7, [R_hTb[k], R_Wg[gi * 2 + half]], [RB[b0 + half]])
                    gv = PS[0:nt, b0 * 512:(b0 + 2) * 512]
                    act(dst[0:nt, :], gv, AF.Exp, [RB[b0], RB[b0 + 1]], [R_dst], scale=-1.0)
                    act(dst[0:nt, :], dst[0:nt, :], AF.Ln, [R_dst], [R_dst], bias=1.0)
                    act(dst[0:nt, :], dst[0:nt, :], AF.Exp, [R_dst], [R_dst], scale=-1.0)

            def c1a_To(j):
                src, nt = blk_src(j)
                k = j % 2
                pT = bankb(7)
                for ch in range(8):
                    tr(pT[:, ch * 128:ch * 128 + nt], Od[0:nt, j, ch * 128:(ch + 1) * 128], [R_O[j]], [RB[7]])
                cp("act", odT[k][:, :, 0:nt], pT.rearrange("p (c t) -> p c t", t=128)[:, :, 0:nt], [RB[7]], [R_odT[k]])
                for ch in range(4):
                    tr(pT[:, ch * 128:ch * 128 + nt], Os[0:nt, j, ch * 128:(ch + 1) * 128], [R_O[j]], [RB[7]])
                cp("dve", osT[k][:, :, 0:nt], pT[:, 0:512].rearrange("p (c t) -> p c t", t=128)[:, :, 0:nt], [RB[7]], [R_osT[k]])

            def c1a_M(j):
                src, nt = blk_src(j)
                k = j % 2
                for half in range(2):
                    for ch in range(8):
                        mm(bank(4 + half)[0:nt, :], odT[k][:, ch, 0:nt], Wd[:, ch, half * 512:(half + 1) * 512], ch == 0, ch == 7,
                           [R_odT[k], R_Wd], [RB[4 + half]])
                tt("dve", egd[k][0:nt, :], PS[0:nt, 2048:3072], egd[k][0:nt, :], ALU.mult, [RB[4], RB[5], R_egd[k]], [R_egd[k]])
                for half in range(2):
                    for ch in range(4):
                        mm(bank(6)[0:nt, :], osT[k][:, ch, 0:nt], Ws[:, ch, half * 512:(half + 1) * 512], ch == 0, ch == 3,
                           [R_osT[k], R_Ws], [RB[6]])
                    hs_ = slice(half * 512, (half + 1) * 512)
                    tt("dve", egs[k][0:nt, hs_], bank(6)[0:nt, :], egs[k][0:nt, hs_], ALU.mult, [RB[6], R_egs[k]], [R_egs[k]])
                tt("pool", mg[0:nt, j, :], egd[k][0:nt, :], egs[k][0:nt, :], ALU.add, [R_egd[k], R_egs[k]], [R_mg[j]])

            c1a_A1(0)
            c1a_A1(1)
            c1a_Th(0)
            c1a_To(0)
            for j in range(NOWN + 1):
                c1a_G(j)
                if j + 1 <= NOWN:
                    c1a_Th(j + 1)
                c1a_M(j)
                if j + 1 <= NOWN:
                    c1a_To(j + 1)
                if j + 2 <= NOWN:
                    c1a_A1(j + 2)
            barrier()

            AR.seek(OFF_X1)
            x1 = AR.alloc([128, NOWN + 1, 1024], F32)
            h2T = AR.alloc([128, 8, NOWN * 128 + DS], BF16)
            OFF_END = AR.off
            AR.seek(OFF_O if OFF_O + 38000 <= OFF_MG else OFF_END)
            x_st = [AR.alloc([128, DM], F32) for _ in range(2)]
            h_sb = [AR.alloc([128, DM], BF16) for _ in range(2)]
            junk = AR.alloc([128, DM], BF16)
            gb = AR.alloc([128, DM], F32)
            st1 = AR.alloc([128, 8], F32)
            mT = [AR.alloc([128, 8, 128], BF16) for _ in range(2)]
            Wo = AR.alloc([128, 8, 1024], BF16)
            assert AR.off <= OFF_MG or AR.off > OFF_END
            R_x1 = [Res(f"x1_{j}") for j in range(NOWN + 1)]
            R_h2T = [Res(f"h2T{j}") for j in range(NOWN + 1)]
            R_x = [Res("x0"), Res("x1")]
            R_h = [Res("h0"), Res("h1")]
            R_junk, R_gb = Res("junk"), Res("gb")
            R_st = [Res("st0"), Res("st1")]
            R_mT, R_Wo = [Res("mT0"), Res("mT1")], Res("Wo")
            p1_cnt[0] = 0
            dma("sp", gb, g_ffn.partition_broadcast(128), [], [R_gb])
            dma("pool", Wo, w_o.rearrange("(c p) n -> p c n", p=128), [], [R_Wo])
            def c1b_Tm(j):
                src, nt = blk_src(j)
                k = j % 2
                pT = bankb(6 + k)
                for ch in range(8):
                    tr(pT[:, ch * 128:ch * 128 + nt], mg[0:nt, j, ch * 128:(ch + 1) * 128], [R_mg[j]], [RB[6 + k]])
                cp("act", mT[k][:, :, 0:nt], pT.rearrange("p (c t) -> p c t", t=128)[:, :, 0:nt], [RB[6 + k]], [R_mT[k]])

            def c1b_X(j):
                src, nt = blk_src(j)
                dma("sp", x_st[j % 2][0:nt, :], src, [], [R_x[j % 2]])

            def c1b_Mo(j):
                src, nt = blk_src(j)
                k = j % 2
                b0 = 2 * k
                for half in range(2):
                    for ch in range(8):
                        mm(bank(b0 + half)[0:nt, :], mT[k][:, ch, 0:nt], Wo[:, ch, half * 512:(half + 1) * 512], ch == 0, ch == 7,
                           [R_mT[k], R_Wo], [RB[b0 + half]])
                tt("dve", x1[0:nt, j, :], PS[0:nt, b0 * 512:(b0 + 2) * 512], x_st[k][0:nt, :], ALU.add,
                   [RB[b0], RB[b0 + 1], R_x[k]], [R_x1[j]])
                if j + 2 <= NOWN:
                    c1b_X(j + 2)
                stc = st1[:, 4 * k:4 * k + 4]
                act(junk[0:nt, :], x1[0:nt, j, :], AF.Square, [R_x1[j]], [R_junk, R_st[k]], accum=stc[0:nt, 0:1])
                act(stc[0:nt, 1:2], stc[0:nt, 0:1], AF.Ln, [R_st[k]], [R_st[k]], scale=1.0 / DM, bias=EPS)
                act(stc[0:nt, 2:3], stc[0:nt, 1:2], AF.Exp, [R_st[k]], [R_st[k]], scale=-0.5)
                stt("dve", h_sb[k][0:nt, :], x1[0:nt, j, :], stc[0:nt, 2:3], gb[0:nt, :], ALU.mult, ALU.mult,
                    [R_x1[j], R_st[k], R_gb], [R_h[k]])

            def c1b_Tn(j):
                src, nt = blk_src(j)
                kk = j % 2
                pT = bankb(4 + kk)
                for ch in range(8):
                    tr(pT[:, ch * 128:ch * 128 + nt], h_sb[kk][0:nt, ch * 128:(ch + 1) * 128], [R_h[kk]], [RB[4 + kk]])
                cp("act", h2T[:, :, j * 128:j * 128 + nt], pT.rearrange("p (c t) -> p c t", t=128)[:, :, 0:nt], [RB[4 + kk]], [R_h2T[j]])

            c1b_X(0)
            c1b_X(1)
            c1b_Tm(0)
            c1b_Mo(0)
            c1b_Tm(1)
            for j in range(NOWN + 1):
                if j + 2 <= NOWN:
                    c1b_Tm(j + 2)
                if j + 1 <= NOWN:
                    c1b_Mo(j + 1)
                c1b_Tn(j)
            barrier()

            AR.seek(OFF_O if OFF_O + 46000 <= OFF_X1 else OFF_END)
            W1c = [AR.alloc([128, 8, 512], BF16) for _ in range(2)]
            W2c = [AR.alloc([128, 4, 1024], BF16) for _ in range(2)]
            hid = [AR.alloc([128, 4, 512], BF16) for _ in range(2)]
            r_t = [AR.alloc([128, 512], F32) for _ in range(2)]
            assert AR.off <= OFF_X1 or AR.off > OFF_END
            R_W1, R_W2 = [Res("W1a"), Res("W1b")], [Res("W2a"), Res("W2b")]
            R_hid, R_r = [Res("hid0"), Res("hid1")], [Res("r0"), Res("r1")]
            w1v = w_f1.rearrange("(c p) n -> p c n", p=128)
            w2v = w_f2.rearrange("(c p) n -> p c n", p=128)
            tgs = []
            for b0 in range(0, NOWN, 4):
                nb_ = min(4, NOWN - b0)
                tgs.append((b0 * 128, nb_ * 128, [(b0 + i, 128) for i in range(nb_)]))
            tgs.append((NOWN * 128, DS, [(NOWN, DS)]))
            NCH = 8
            rr = [0]

            def load_ffn(c):
                k = c % 2
                dma("pool", W1c[k], w1v[:, :, c * 512:(c + 1) * 512], [], [R_W1[k]])
                dma("pool", W2c[k], w2v[:, c * 4:(c + 1) * 4, :], [], [R_W2[k]])

            items = [(c, ti) for c in range(NCH) for ti in range(len(tgs))]

            def ffn1(idx):
                c, ti = items[idx]
                k = c % 2
                t0, ntok, blks = tgs[ti]
                hb_ = idx % 2
                rj = [R_h2T[b] for b, _ in blks]
                for sub in range(4):
                    bk = sub % 2
                    for ch in range(8):
                        mm(bank(bk)[:, 0:ntok], W1c[k][:, ch, sub * 128:(sub + 1) * 128], h2T[:, ch, t0:t0 + ntok], ch == 0, ch == 7,
                           [R_W1[k]] + rj, [RB[bk]])
                    rb_ = sub % 2
                    act(r_t[rb_][:, 0:ntok], bank(bk)[:, 0:ntok], AF.Relu, [RB[bk]], [R_r[rb_]])
                    tt("pool" if sub % 2 else "dve", hid[hb_][:, sub, 0:ntok], r_t[rb_][:, 0:ntok], r_t[rb_][:, 0:ntok], ALU.mult,
                       [R_r[rb_]], [R_hid[hb_]])

            def ffn2(idx):
                c, ti = items[idx]
                k = c % 2
                t0, ntok, blks = tgs[ti]
                hb_ = idx % 2
                for bi_, (b, nt) in enumerate(blks):
                    yb = 2 + 2 * (bi_ % 2)
                    for half in range(2):
                        for sub in range(4):
                            mm(bank(yb + half)[0:nt, :], hid[hb_][:, sub, bi_ * 128:bi_ * 128 + nt],
                               W2c[k][:, sub, half * 512:(half + 1) * 512], sub == 0, sub == 3, [R_hid[hb_], R_W2[k]], [RB[yb + half]])
                    tt("dve", x1[0:nt, b, :], x1[0:nt, b, :], PS[0:nt, yb * 512:(yb + 2) * 512], ALU.add,
                       [RB[yb], RB[yb + 1], R_x1[b]], [R_x1[b]])

            load_ffn(0)
            if NCH > 1:
                load_ffn(1)
            ffn1(0)
            for idx in range(len(items)):
                if idx + 1 < len(items):
                    ffn1(idx + 1)
                ffn2(idx)
                c, ti = items[idx]
                if ti == len(tgs) - 1 and c + 2 < NCH:
                    load_ffn(c + 2)
            for j in range(NOWN):
                dma("sp", y_own[j * 128:(j + 1) * 128, :], x1[:, j, :], [R_x1[j]], [])
            dma("sp", y_s[:, :], x1[0:DS, NOWN, :], [R_x1[NOWN]], [])

        S.finish()
    return nc


ROT_DIM = 16
ROPE_THETA = 500000.0


def _rope_tables(pos):
    inv = ROPE_THETA ** (-np.arange(0, ROT_DIM, 2, dtype=np.float32) / ROT_DIM)
    ang = pos.astype(np.float32)[:, None] * inv[None, :].astype(np.float32)
    return np.cos(ang).astype(np.float32), np.sin(ang).astype(np.float32)


_NC_CACHE = {}
STOP_AFTER = 99


def kernel(x_prompt, x_sample, cache_diff_k, cache_diff_v, cache_sb_k, cache_sb_v, meta_tokens,
           g_mix, w_in, q_norm_g, k_norm_g, lam_q1, lam_k1, lam_q2, lam_k2, sub_g,
           w_diff_out, w_sb_out, w_out, g_ffn, w_ff1, w_ff2):
    f = lambda a: np.ascontiguousarray(np.asarray(a, dtype=np.float32))
    x_prompt, x_sample, meta_tokens = f(x_prompt), f(x_sample), f(meta_tokens)
    B, SEQ, _ = x_prompt.shape
    NB = SEQ // 128
    PAST = cache_diff_k.shape[2]
    PB = PAST // 128
    NOWN = NB // 2
    NSLOT = NB + 2
    key = (NB, PB, STOP_AFTER)
    if key not in _NC_CACHE:
        _NC_CACHE[key] = build(NB, PB, STOP_AFTER)
    nc = _NC_CACHE[key]

    ii = np.arange(128)
    Dd = ((ii[:, None] // 64) <= (ii[None, :] // 64)).astype(np.float32)
    Dsm = (ii[:, None] < ii[None, :]).astype(np.float32)
    trineg = -(ii[:, None] >= ii[None, :]).astype(np.float32)
    ones = np.ones((128, 128), np.float32)
    shared = dict(
        w_in=f(w_in)[0], w_do=f(w_diff_out)[0], w_so=f(w_sb_out)[0], w_o=f(w_out)[0], w_f1=f(w_ff1)[0], w_f2=f(w_ff2)[0],
        g_mix=f(g_mix)[0], g_ffn=f(g_ffn)[0], qng=f(q_norm_g)[0], kng=f(k_norm_g)[0], subg=f(sub_g)[0],
        lamv=np.stack([f(lam_q1)[0], f(lam_k1)[0], f(lam_q2)[0], f(lam_k2)[0]]),
    )
    in_maps = []
    for c in range(8):
        b, p = c // 2, c % 2
        own = [2 * j + p for j in range(NOWN)]
        oth = [2 * j + 1 - p for j in range(NOWN)]
        order = own + oth
        xb = x_prompt[b].reshape(NB, 128, DM)
        x_slots = np.concatenate([xb[order].reshape(NB * 128, DM), meta_tokens], axis=0)
        cs = np.zeros((128, NSLOT, 8), np.float32)
        sn = np.zeros((128, NSLOT, 8), np.float32)
        for s, g in enumerate(order):
            cc, ss = _rope_tables(N_META + 128 * g + np.arange(128))
            cs[:, s], sn[:, s] = cc, ss
        cc, ss = _rope_tables(np.arange(N_META))
        cs[:N_META, NB], sn[:N_META, NB] = cc, ss
        cc, ss = _rope_tables(PAST + np.arange(DS))
        cs[:DS, NB + 1], sn[:DS, NB + 1] = cc, ss
        a_, b_ = (1.0, 0.0) if p == 0 else (0.0, 1.0)
        cmat = np.stack([trineg, Dd, Dsm, ones * float(p), -a_ * ones, -b_ * ones, -ones, 0 * ones]).astype(np.float32)
        m = dict(shared)
        m.update(x_slots=np.ascontiguousarray(x_slots), xs=x_sample[c],
                 c_dk=f(cache_diff_k)[0, c], c_dv=f(cache_diff_v)[0, c], c_sk=f(cache_sb_k)[0, c], c_sv=f(cache_sb_v)[0, c],
                 cs_t=cs, sn_t=sn, cmat=cmat)
        in_maps.append(m)
    res = run_bass_kernel_spmd(nc, in_maps, core_ids=list(range(8)))
    R = res.results
    T = SEQ + N_META
    y_prompt = np.zeros((B, SEQ, DM), np.float32)
    y_sample = np.zeros((8, DS, DM), np.float32)
    pk = np.zeros((1, B, T, NH, 128), np.float32)
    pv = np.zeros((1, B, T, NH, 128), np.float32)
    psk = np.zeros((1, B, T, NH, 64), np.float32)
    psv = np.zeros((1, B, T, NH, 64), np.float32)
    sk_ = np.zeros((1, 8, DS, NH, 128), np.float32)
    sv_ = np.zeros((1, 8, DS, NH, 128), np.float32)
    ssk = np.zeros((1, 8, DS, NH, 64), np.float32)
    ssv = np.zeros((1, 8, DS, NH, 64), np.float32)
    for c in range(8):
        b, p = c // 2, c % 2
        r = R[c]
        for j in range(NOWN):
            g = 2 * j + p
            y_prompt[b, 128 * g:128 * (g + 1)] = r["y_own"][128 * j:128 * (j + 1)]
            sl = slice(N_META + 128 * g, N_META + 128 * (g + 1))
            pk[0, b, sl] = r["kd_o"][128 * j:128 * (j + 1)]
            pv[0, b, sl] = r["vd_o"][128 * j:128 * (j + 1)]
            psk[0, b, sl] = r["sk_o"][128 * j:128 * (j + 1)]
            psv[0, b, sl] = r["sv_o"][128 * j:128 * (j + 1)]
        if p == 0:
            pk[0, b, :N_META] = r["kd_m"]
            pv[0, b, :N_META] = r["vd_m"]
            psk[0, b, :N_META] = r["sk_m"]
            psv[0, b, :N_META] = r["sv_m"]
        y_sample[c] = r["y_s"]
        sk_[0, c], sv_[0, c], ssk[0, c], ssv[0, c] = r["kd_s"], r["vd_s"], r["sk_s"], r["sv_s"]
    return (y_prompt, y_sample, pk, pv, psk, psv, sk_, sv_, ssk, ssv)
```
